# Optimizing a Trainium2 kernel written in Bass

```python
import jax, jax.numpy as jnp
from jax import lax
import numpy as np

D_MODEL = 1024
BATCH = 4
SEQ = 4096
DEPTH = 2
DEC_BATCH = 2
DEC_SEQ = 8192
PAST_LEN = 128

GRID_W = 64
A_HEADS = 4
A_DK = 32
A_DV = 64
A_W = A_HEADS * A_DV
A_KW = A_HEADS * A_DK
GLA_RANK = 16
GLA_GATE_NORM = 16.0
GLA_CHUNK = 64
B_GROUPS = 4
B_GC = 64
B_W = B_GROUPS * B_GC
C_HEADS = 8
C_KV = 2
C_HD = 64
C_W = C_HEADS * C_HD
C_KVW = C_KV * C_HD
Q_BLOCK = 128
ROPE_AXIS_DIM = C_HD // 2
ROPE_THETA = 10000.0
D_GROUPS = 4
D_GC = 64
D_W = D_GROUPS * D_GC
SGU_CHUNK = 128
D_MIX = A_W + B_W + C_W + D_W
IN_SPLITS = (A_KW, A_KW, A_W, 2 * GLA_RANK, B_W, C_W, C_KVW, C_KVW, D_W, D_W, D_MIX)
D_IN = 2 * A_KW + A_W + 2 * GLA_RANK + B_W + C_W + 2 * C_KVW + 2 * D_W + D_MIX
EPS = 1e-6

kernel_name = "hymba_style_bidir_hybrid_encoder"


def rms_norm(x, g):
    xf = x.astype(jnp.float32)
    y = xf * lax.rsqrt(jnp.mean(xf * xf, axis=-1, keepdims=True) + EPS)
    return (y * g.astype(jnp.float32)).astype(x.dtype)


def gla_chunked(q, k, v, g, strict):
    bsz, n, h, dk = q.shape
    dv = v.shape[-1]
    nc = n // GLA_CHUNK

    def chunks(t):
        return t.astype(jnp.float32).reshape(bsz, nc, GLA_CHUNK, h, t.shape[-1]).transpose(1, 0, 3, 2, 4)

    qc, kc, vc, gc = chunks(q), chunks(k), chunks(v), chunks(g)
    mask = jnp.tril(jnp.ones((GLA_CHUNK, GLA_CHUNK), dtype=bool), k=-1 if strict else 0)

    def step(state, inp):
        qi, ki, vi, gi = inp
        b = jnp.cumsum(gi, axis=-2)
        inter = jnp.einsum('bhtd,bhdv->bhtv', qi * jnp.exp(b), state)
        diff = b[:, :, :, None, :] - b[:, :, None, :, :]
        decay = jnp.exp(jnp.where(mask[:, :, None], diff, -jnp.inf))
        scores = jnp.einsum('bhtd,bhsd,bhtsd->bhts', qi, ki, decay)
        intra = jnp.einsum('bhts,bhsv->bhtv', scores, vi)
        b_last = b[:, :, -1:, :]
        state = jnp.exp(b_last[:, :, 0, :])[..., None] * state + jnp.einsum(
            'bhsd,bhsv->bhdv', ki * jnp.exp(b_last - b), vi)
        return state, inter + intra

    state0 = jnp.zeros((bsz, h, dk, dv), jnp.float32)
    _, out = lax.scan(step, state0, (qc, kc, vc, gc))
    return out.transpose(1, 0, 3, 2, 4).reshape(bsz, n, h, dv)


def gla_branch(q, k, v, lr, wg2_f, bg_f, wg2_b, bg_b, onorm_g):
    bsz, n, _ = q.shape
    q = q.reshape(bsz, n, A_HEADS, A_DK) * (A_DK ** -0.5)
    k = k.reshape(bsz, n, A_HEADS, A_DK)
    v = v.reshape(bsz, n, A_HEADS, A_DV)
    lr_f, lr_b = jnp.split(lr, 2, axis=-1)

    def log_gate(lr_d, w2, b2):
        logits = (lr_d @ w2 + b2).astype(jnp.float32)
        return (jax.nn.log_sigmoid(logits) / GLA_GATE_NORM).reshape(bsz, n, A_HEADS, A_DK)

    o_f = gla_chunked(q, k, v, log_gate(lr_f, wg2_f, bg_f), strict=False)
    flip = lambda t: jnp.flip(t, axis=1)
    o_b = flip(gla_chunked(flip(q), flip(k), flip(v), flip(log_gate(lr_b, wg2_b, bg_b)), strict=True))
    o = rms_norm(o_f + o_b, onorm_g)
    return o.reshape(bsz, n, A_W).astype(v.dtype)


def fnet_branch(u, fnet_w):
    bsz, n, _ = u.shape
    uf = u.astype(jnp.float32).reshape(bsz, n, B_GROUPS, B_GC)
    mixed = jnp.real(jnp.fft.fft2(uf, axes=(1, 3), norm="ortho"))
    return mixed.reshape(bsz, n, B_W).astype(u.dtype) @ fnet_w


def axial_rope_angles(n_tokens):
    rows = n_tokens // GRID_W
    row = jnp.repeat(jnp.arange(rows, dtype=jnp.float32), GRID_W)
    col = jnp.tile(jnp.arange(GRID_W, dtype=jnp.float32), rows)
    freqs = ROPE_THETA ** (-jnp.arange(0, ROPE_AXIS_DIM, 2, dtype=jnp.float32) / ROPE_AXIS_DIM)
    ang = jnp.concatenate([row[:, None] * freqs, col[:, None] * freqs], axis=-1)
    return jnp.cos(ang), jnp.sin(ang)


def apply_rope(x, cos, sin):
    xf = x.astype(jnp.float32).reshape(*x.shape[:-1], C_HD // 2, 2)
    x0, x1 = xf[..., 0], xf[..., 1]
    c = cos[None, :, None, :]
    s = sin[None, :, None, :]
    out = jnp.stack([x0 * c - x1 * s, x0 * s + x1 * c], axis=-1).reshape(x.shape)
    return out.astype(x.dtype)


def attention_branch(q, k, v, qn_g, kn_g):
    bsz, n, _ = q.shape
    nb = n // Q_BLOCK
    grp = C_HEADS // C_KV
    q = rms_norm(q.reshape(bsz, n, C_HEADS, C_HD), qn_g)
    k = rms_norm(k.reshape(bsz, n, C_KV, C_HD), kn_g)
    v = v.reshape(bsz, n, C_KV, C_HD)
    cos, sin = axial_rope_angles(n)
    q = apply_rope(q, cos, sin)
    k = apply_rope(k, cos, sin)
    qb = q.reshape(bsz, nb, Q_BLOCK, C_KV, grp, C_HD).transpose(1, 0, 3, 4, 2, 5)
    kt = k.transpose(0, 2, 1, 3)
    vt = v.transpose(0, 2, 1, 3)
    scale = C_HD ** -0.5

    def attend_block(qblk):
        s = jnp.einsum('bkgqd,bksd->bkgqs', qblk, kt).astype(jnp.float32) * scale
        p = jax.nn.softmax(s, axis=-1)
        return jnp.einsum('bkgqs,bksd->bkgqd', p.astype(vt.dtype), vt)

    o = lax.map(attend_block, qb)
    return o.transpose(1, 0, 4, 2, 3, 5).reshape(bsz, n, C_W)


def sgu_branch(u, v, norm_g, w_s, b_s):
    bsz, n, _ = u.shape
    nch = n // SGU_CHUNK
    vn = rms_norm(v, norm_g).reshape(bsz, nch, SGU_CHUNK, D_GROUPS, D_GC)
    mixed = jnp.einsum('gts,bnsgc->bntgc', w_s, vn) + b_s.T[None, None, :, :, None]
    return u * mixed.reshape(bsz, n, D_W)


def hybrid_layer(x, c, ada_w, ada_b, pre_g, post_g, w_in, gla_wg2_f, gla_bg_f, gla_wg2_b, gla_bg_b,
                 gla_onorm_g, fnet_w, q_norm_g, k_norm_g, sgu_norm_g, sgu_w, sgu_b, w_out):
    shift, scale, gate = jnp.split(jax.nn.silu(c) @ ada_w + ada_b, 3, axis=-1)
    h = rms_norm(x, pre_g) * (1 + scale[:, None, :]) + shift[:, None, :]
    proj = h @ w_in
    splits = [int(i) for i in np.cumsum(IN_SPLITS)[:-1]]
    a_q, a_k, a_v, a_lr, b_u, c_q, c_k, c_v, d_u, d_v, z = jnp.split(proj, splits, axis=-1)
    out_a = gla_branch(a_q, a_k, a_v, a_lr, gla_wg2_f, gla_bg_f, gla_wg2_b, gla_bg_b, gla_onorm_g)
    out_b = fnet_branch(b_u, fnet_w)
    out_c = attention_branch(c_q, c_k, c_v, q_norm_g, k_norm_g)
    out_d = sgu_branch(d_u, d_v, sgu_norm_g, sgu_w, sgu_b)
    mixed = jnp.concatenate([out_a, out_b, out_c, out_d], axis=-1) * jax.nn.silu(z)
    y = rms_norm(mixed @ w_out, post_g)
    return x + gate[:, None, :] * y


def setup_inputs(seed: int = 0) -> dict:
    key = jax.random.key(seed)
    ks = jax.random.split(key, 21)
    f32 = jnp.float32
    nrm = lambda k, shape, s: jax.random.normal(k, shape, f32) * s
    return {
        "x_prompt": nrm(ks[0], (BATCH, SEQ, D_MODEL), 1.0),
        "x_sample": nrm(ks[1], (DEC_BATCH, DEC_SEQ, D_MODEL), 1.0),
        "c_prompt": nrm(ks[2], (BATCH, D_MODEL), 1.0),
        "c_sample": nrm(ks[3], (DEC_BATCH, D_MODEL), 1.0),
        "ada_w": nrm(ks[4], (DEPTH, D_MODEL, 3 * D_MODEL), 0.5 * D_MODEL ** -0.5),
        "ada_b": nrm(ks[5], (DEPTH, 3 * D_MODEL), 0.02),
        "norm_pre_g": 1.0 + nrm(ks[6], (DEPTH, D_MODEL), 0.02),
        "norm_post_g": 1.0 + nrm(ks[7], (DEPTH, D_MODEL), 0.02),
        "w_in": nrm(ks[8], (DEPTH, D_MODEL, D_IN), D_MODEL ** -0.5),
        "gla_wg2_f": nrm(ks[9], (DEPTH, GLA_RANK, A_KW), GLA_RANK ** -0.5),
        "gla_bg_f": nrm(ks[10], (DEPTH, A_KW), 0.1),
        "gla_wg2_b": nrm(ks[11], (DEPTH, GLA_RANK, A_KW), GLA_RANK ** -0.5),
        "gla_bg_b": nrm(ks[12], (DEPTH, A_KW), 0.1),
        "gla_onorm_g": 1.0 + nrm(ks[13], (DEPTH, A_DV), 0.02),
        "fnet_w": nrm(ks[14], (DEPTH, B_W, B_W), B_W ** -0.5),
        "q_norm_g": 1.0 + nrm(ks[15], (DEPTH, C_HD), 0.02),
        "k_norm_g": 1.0 + nrm(ks[16], (DEPTH, C_HD), 0.02),
        "sgu_norm_g": 1.0 + nrm(ks[17], (DEPTH, D_W), 0.02),
        "sgu_w": nrm(ks[18], (DEPTH, D_GROUPS, SGU_CHUNK, SGU_CHUNK), SGU_CHUNK ** -0.5),
        "sgu_b": 1.0 + nrm(ks[19], (DEPTH, D_GROUPS, SGU_CHUNK), 0.02),
        "w_out": nrm(ks[20], (DEPTH, D_MIX, D_MODEL), D_MIX ** -0.5),
    }


def reference(x_prompt, x_sample, c_prompt, c_sample, ada_w, ada_b, norm_pre_g, norm_post_g, w_in,
              gla_wg2_f, gla_bg_f, gla_wg2_b, gla_bg_b, gla_onorm_g, fnet_w, q_norm_g, k_norm_g,
              sgu_norm_g, sgu_w, sgu_b, w_out):
    y_prompt = x_prompt
    y_sample = x_sample
    for l in range(DEPTH):
        layer_params = (ada_w[l], ada_b[l], norm_pre_g[l], norm_post_g[l], w_in[l],
                        gla_wg2_f[l], gla_bg_f[l], gla_wg2_b[l], gla_bg_b[l], gla_onorm_g[l],
                        fnet_w[l], q_norm_g[l], k_norm_g[l], sgu_norm_g[l], sgu_w[l], sgu_b[l], w_out[l])
        y_prompt = hybrid_layer(y_prompt, c_prompt, *layer_params)
        y_sample = hybrid_layer(y_sample, c_sample, *layer_params)
    return (y_prompt, y_sample)
```

```python
from contextlib import ExitStack
import numpy as np
import ml_dtypes
import concourse.bass as bass
import concourse.mybir as mybir
from concourse.bass_utils import run_bass_kernel_spmd

F32 = mybir.dt.float32
BF16 = mybir.dt.bfloat16
AF = mybir.ActivationFunctionType
ALU = mybir.AluOpType
AX = mybir.AxisListType

T = 8192
TO = 4096
D = 1024
DIN = 3360
DMIX = 1280
NTILE = T // 128
NSUP = TO // 512
NTO = TO // 128
EPS = 1e-6
NPRM = 40
SIDE_RATIO = 1.7
NBC = 3200
BC_ADABG, BC_POSTG, BC_QKG, BC_SGUG, BC_ONG = 0, 1024, 2048, 2688, 2944
P_ADAB, P_PREG, P_BGF, P_BGB, P_SGUB = 0, 24, 32, 33, 34
GC_MF, GC_MB, GC_HM, GC_BD, GC_SM, GC_N = 0, 64, 128, 132, 388, 388 + 2048


class Buf:
    __slots__ = ("name", "w", "r")

    def __init__(self, name):
        self.name = name
        self.w = {}
        self.r = {}


class Tl:
    def __init__(self, t, name):
        self.t = t
        self.b = Buf(name)

    def __getitem__(self, k):
        return self.t[k]


class Sched:
    EPOCH = 30000

    def __init__(self, nc):
        self.nc = nc
        self.E = {"pe": nc.tensor, "act": nc.scalar, "dve": nc.vector, "pool": nc.gpsimd, "sp": nc.sync}
        self.esem = {}
        for e in ("pe", "act", "dve", "pool"):
            self.esem[e] = [nc.alloc_semaphore("es_%s_0" % e), 0, 0]
        self.seen = {e: {} for e in self.E}
        self.dsem = {}
        self.free_ds = []
        self.psum_names = set()
        self.watch = None
        self.pending = False
        self.csem = {}
        self.nds = 0
        self.nins = 0

    def sb(self, name, shape, dt):
        return Tl(self.nc.alloc_sbuf_tensor(name, list(shape), dt), name)

    def ps(self, name, shape, dt):
        return Tl(self.nc.alloc_psum_tensor(name, list(shape), dt), name)

    def _wait(self, e, need):
        for k, (h, v) in need.items():
            if self.seen[e].get(k, 0) < v:
                self.E[e].wait_ge(h, v)
                self.seen[e][k] = v

    @staticmethod
    def _add(need, k, hv):
        if k not in need or need[k][1] < hv[1]:
            need[k] = hv

    def op(self, e, fn, r=(), w=(), pw=()):
        need = {}
        for b in r:
            for k, hv in b.w.items():
                self._add(need, k, hv)
            if b.name in self.psum_names:
                for k, hv in b.r.items():
                    if k[0] != e:
                        self._add(need, k, hv)
        for b in list(w) + list(pw):
            for k, hv in b.r.items():
                if k[0] != e:
                    self._add(need, k, hv)
            for k, hv in b.w.items():
                if k[0] != e:
                    self._add(need, k, hv)
        self._wait(e, need)
        ins = fn()
        if self.watch is not None:
            if e == "pe" and any(b is self.watch for b in list(w) + list(pw)):
                self.pending = True
            elif e != "pe" and any(b is self.watch for b in r):
                self.pending = False
        es = self.esem[e]
        if es[1] >= self.EPOCH:
            es[2] += 1
            es[0] = self.nc.alloc_semaphore("es_%s_%d" % (e, es[2]))
            es[1] = 0
        es[1] += 1
        ins.then_inc(es[0], 1)
        key = (e, es[2])
        hv = (es[0], es[1])
        for b in r:
            b.r[key] = hv
        for b in w:
            b.w = {key: hv}
            b.r = {}
        for b in pw:
            b.w[key] = hv
        self.nins += 1
        return ins

    def dma(self, out, in_, r=(), w=(), pw=(), sem="d", q="sp"):
        if sem not in self.dsem:
            if self.free_ds:
                self.dsem[sem] = self.free_ds.pop()
            else:
                self.nds += 1
                self.dsem[sem] = [self.nc.alloc_semaphore("ds_%d" % self.nds), 0, self.nds]
        ds = self.dsem[sem]
        key = ("dma", ds[2])
        need = {}
        if ds[1] > 0:
            need[key] = (ds[0], ds[1])
        for b in r:
            for k, hv in b.w.items():
                self._add(need, k, hv)
        for b in w:
            for k, hv in b.r.items():
                self._add(need, k, hv)
            for k, hv in b.w.items():
                self._add(need, k, hv)
        for b in pw:
            for k, hv in b.r.items():
                self._add(need, k, hv)
            for k, hv in b.w.items():
                if k[0] != "dma":
                    self._add(need, k, hv)
        self._wait(q, need)
        ins = self.E[q].dma_start(out=out, in_=in_)
        ds[1] += 16
        ins.then_inc(ds[0], 16)
        hv = (ds[0], ds[1])
        for b in r:
            b.r[key] = hv
        for b in w:
            b.w = {key: hv}
            b.r = {}
        for b in pw:
            b.w[key] = hv
        self.nins += 1
        return ins

    def allgather(self, src, dst, rg, name):
        if name not in self.csem:
            self.csem[name] = [self.nc.alloc_semaphore("cs_" + name), 0]
        ds = self.csem[name]
        key = ("cc", name)
        need = {}
        if ds[1] > 0:
            need[key] = (ds[0], ds[1])
        for k, hv in src.b.w.items():
            self._add(need, k, hv)
        for k, hv in list(dst.b.r.items()) + list(dst.b.w.items()):
            self._add(need, k, hv)
        self._wait("pool", need)
        ins = self.nc.gpsimd.collective_compute("AllGather", ALU.bypass, replica_groups=rg,
                                                ins=[src.h.ap().opt()], outs=[dst.h.ap().opt()])
        ds[1] += 1
        ins.then_inc(ds[0])
        hv = (ds[0], ds[1])
        src.b.r[key] = hv
        dst.b.w = {key: hv}
        dst.b.r = {}
        self.nins += 1

    def barrier(self):
        need = {}
        for e, es in self.esem.items():
            if es[1] > 0:
                need[(e, es[2])] = (es[0], es[1])
        for name, ds in self.dsem.items():
            if ds[1] > 0:
                need[("dma", ds[2])] = (ds[0], ds[1])
        for e in self.E:
            self._wait(e, need)
        self.free_ds.extend(self.dsem.values())
        self.dsem = {}

    def finish(self, bufs):
        need = {}
        for b in bufs:
            for k, hv in b.b.w.items():
                self._add(need, k, hv)
        self._wait("sp", need)


def build(dbg=()):
    nc = bass.Bass("TRN2", target_bir_lowering=False)
    S = Sched(nc)
    V, A, P, G = nc.vector, nc.scalar, nc.tensor, nc.gpsimd

    def din(name, shape, dt=F32):
        return Tl(nc.dram_tensor(name, list(shape), dt, kind="ExternalInput").ap(), name)

    def dscr(name, shape, dt):
        kind = "ExternalOutput" if name in dbg else "Internal"
        return Tl(nc.dram_tensor(name, list(shape), dt, kind=kind).ap(), name)

    x_in = din("x", [TO, D])
    cT_in = din("cT", [128, 16])
    adaw_in = din("ada_w", [2, D, 3 * D])
    win_in = din("w_in", [2, D, DIN])
    wout_in = din("w_out", [2, DMIX, D])
    fnet_in = din("fnet_w", [2, 256, 256])
    sguw_in = din("sgu_wT", [2, 128, 512])
    wg2_in = din("wg2", [2, 16, 256])
    prm_in = din("prm", [2, 128, NPRM])
    bc_in = din("bc", [2, 128, NBC])
    rope_in = din("rope", [TO, 64])
    flags_in = din("flags", [128, 16])
    gconst_in = din("gconst", [128, GC_N])
    tab1_in = din("tab1", [64, 128, 256], BF16)
    tab2_in = din("tab2", [128, 128], BF16)
    cs64_in = din("cs64", [128, 2, 512])
    y_out = Tl(nc.dram_tensor("y", [TO, D], F32, kind="ExternalOutput").ap(), "y")

    SROWS = [2048, 2048, 2048, 2048, 256]
    SD, RV = [], []
    for i, rws in enumerate(SROWS):
        h = nc.dram_tensor("SD%d" % i, [rws, 512], BF16)
        t_ = Tl(h.ap(), "SD%d" % i)
        t_.h = h
        SD.append(t_)
        h = nc.dram_tensor("RV%d" % i, [2 * rws, 512], BF16)
        t_ = Tl(h.ap(), "RV%d" % i)
        t_.h = h
        RV.append(t_)
    RG = [[0, 4], [1, 5], [2, 6], [3, 7]]

    def xviews(B, j):
        o = [j * r for r in SROWS]
        v = {}
        v["vv"] = B[2][o[2]:o[2] + 1024, :].rearrange("r (a c) -> (r a) c", a=4)
        v["kt"] = B[2][o[2] + 1024:o[2] + 2048, :].rearrange("(d g tt) c -> d g (tt c)", d=64, g=2, tt=8)
        v["gqt"] = B[3][o[3]:o[3] + 1024, :].rearrange("(f tt) c -> f (tt c)", tt=8)
        v["gkt"] = B[3][o[3] + 1024:o[3] + 2048, :].rearrange("(f tt) c -> f (tt c)", tt=8)
        v["lrf"] = B[4][o[4]:o[4] + 128, :].rearrange("(f tt) c -> f (tt c)", tt=8)
        v["lrb"] = B[4][o[4] + 128:o[4] + 256, :].rearrange("(f tt) c -> f (tt c)", tt=8)
        return v

    def avu_rows(B, j, t0, n):
        i = 0 if t0 < 2048 else 1
        r0 = j * 2048 + (t0 % 2048)
        return B[i], B[i][r0:r0 + n, :]

    SV = xviews(SD, 0)
    RVV = [xviews(RV, 0), xviews(RV, 1)]
    QT = dscr("QT", [64, 8, TO], BF16)
    ZS = dscr("ZS", [TO, DMIX], BF16)
    MX = dscr("MX", [TO, DMIX], BF16)
    GO = dscr("GO", [T, 256], BF16)
    MXF = dscr("MXF", [T, 256], BF16)
    FP = dscr("FP", [T, 256], BF16)
    OF = dscr("OF", [T, 256], F32)
    G1 = dscr("G1", [128, 2, 64, 256], BF16)
    Y1 = dscr("Y1", [TO, D], F32)

    ident = S.sb("ident", [128, 128], BF16)
    identf = S.sb("identf", [128, 128], F32)
    flags = S.sb("flags_sb", [128, 16], F32)
    prm = S.sb("prm_sb", [128, NPRM], F32)
    bc = S.sb("bc_sb", [128, NBC], F32)
    gmod = S.sb("gmod", [128, 32], F32)
    pgt = [S.sb("pgt%d" % h, [128, D], F32) for h in range(2)]
    stat = S.sb("stat", [128, 64], F32)

    PD = [nc.alloc_psum_tensor("PD%d" % i, [128, 1024], F32) for i in range(3)]
    PS_ = [nc.alloc_psum_tensor("PS%d" % i, [128, 512], F32) for i in range(2)]
    pb = [Tl(PD[0][:, 0:512], "pb0"), Tl(PD[0][:, 512:1024], "pb1"), Tl(PD[1][:, 0:512], "pb2"),
          Tl(PD[1][:, 512:1024], "pb3"), Tl(PD[2][:, 0:512], "pb4")]
    b16 = lambda t_: t_.bitcast(BF16).rearrange("p (c t) -> p c t", t=128)
    ptr = Tl(b16(PD[2][:, 512:1024]), "ptr")
    ptq = Tl(b16(PS_[0][:, :]), "ptq")
    ptk = Tl(b16(PS_[1][:, :]), "ptk")
    sideT = Tl(PS_[1][:, :], "sideT")
    sideT.b = ptk.b
    S.psum_names = {t_.b.name for t_ in pb + [ptr, ptq, ptk]}
    S.watch = ptk.b
    pbi = [0]

    def bank():
        pbi[0] += 1
        return pb[pbi[0] % 5]

    S.op("pool", lambda: G.memset(ident[:], 1.0), w=[ident.b])
    S.op("pool", lambda: G.affine_select(out=ident[:], in_=ident[:], pattern=[[-1, 128]], compare_op=ALU.is_equal,
                                        fill=0.0, base=0, channel_multiplier=1), r=[ident.b], w=[ident.b])
    S.op("pool", lambda: G.memset(identf[:], 1.0), w=[identf.b])
    S.op("pool", lambda: G.affine_select(out=identf[:], in_=identf[:], pattern=[[-1, 128]], compare_op=ALU.is_equal,
                                        fill=0.0, base=0, channel_multiplier=1), r=[identf.b], w=[identf.b])
    S.dma(flags[:], flags_in[:, :], r=[flags_in.b], w=[flags.b], sem="misc")

    mhalf = S.sb("mhalf", [128, 64], F32)
    S.op("pool", lambda: G.memset(mhalf[:], -0.5), w=[mhalf.b])

    def rstd_chain(src, dst, n, inv_n):
        S.op("dve", lambda: V.tensor_scalar(out=stat[:, dst:dst + n], in0=stat[:, src:src + n], scalar1=inv_n,
                                            scalar2=EPS, op0=ALU.mult, op1=ALU.add), r=[stat.b], w=[stat.b])
        S.op("pool", lambda: G.tensor_tensor(out=stat[:, dst:dst + n], in0=stat[:, dst:dst + n], in1=mhalf[:, 0:n], op=ALU.pow),
             r=[stat.b, mhalf.b], w=[stat.b])

    def phase0(l, kc_hook=None):
        with ExitStack() as es:
            al = lambda n, s, d: Tl(es.enter_context(nc.sbuf_tensor(n + "_L%d" % l, list(s), d)), n)
            cs = al("cs_sb", [128, 16], F32)
            csrep = al("csrep", [128, 8, 2, 128], F32)
            ad = [al("adst0", [128, 3 * D], F32), al("adst1", [128, 3 * D], F32)]
            S.dma(prm[:], prm_in[l], r=[prm_in.b], w=[prm.b], sem="misc")
            S.dma(bc[:], bc_in[l], r=[bc_in.b], w=[bc.b], sem="misc")
            S.dma(cs[:], cT_in[:, :], r=[cT_in.b], w=[cs.b], sem="misc")
            S.op("act", lambda: A.activation(out=cs[:], in_=cs[:], func=AF.Silu), r=[cs.b], w=[cs.b])
            csv = cs[:].rearrange("p (k h) -> p k h", h=2)
            S.op("dve", lambda: V.tensor_copy(out=csrep[:], in_=csv.unsqueeze(3).to_broadcast([128, 8, 2, 128])),
                 r=[cs.b], w=[csrep.b])
            pm = pb[0]
            pg = [pb[1], pb[2], pb[3], pb[4]]
            S.op("dve", lambda: V.memset(pm[:], 0.0), w=[pm.b])
            for kc in range(8):
                if kc_hook is not None:
                    kc_hook(kc)
                a = ad[kc % 2]
                S.dma(a[:], adaw_in[l, kc * 128:(kc + 1) * 128, :], r=[adaw_in.b], w=[a.b], sem="ad%d" % (kc % 2))
                for j in range(16):
                    S.op("pe", lambda: P.matmul(out=pm[:, j * 2:(j + 1) * 2], lhsT=a[:, j * 128:(j + 1) * 128],
                                               rhs=cs[:, kc * 2:(kc + 1) * 2], start=False, stop=(kc == 7),
                                               skip_group_check=True), r=[a.b, cs.b], pw=[pm.b])
                for h in range(2):
                    for nb in range(2):
                        S.op("pe", lambda: P.matmul(out=pg[h * 2 + nb][:, :], lhsT=csrep[:, kc, h, :],
                                                   rhs=a[:, 2048 + nb * 512:2048 + (nb + 1) * 512],
                                                   start=(kc == 0), stop=(kc == 7)),
                             r=[a.b, csrep.b], **({"w": [pg[h * 2 + nb].b]} if kc == 0 else {"pw": [pg[h * 2 + nb].b]}))
            pmv = pm[:, 0:32].rearrange("p (j h) -> p j h", h=2)
            for h in range(2):
                S.op("dve", lambda: V.tensor_tensor(out=gmod[:, h * 16 + 8:h * 16 + 16], in0=pmv[:, 0:8, h],
                                                    in1=prm[:, P_ADAB:P_ADAB + 8], op=ALU.add),
                     r=[pm.b, prm.b], pw=[gmod.b])
                S.op("dve", lambda: V.tensor_tensor(out=gmod[:, h * 16:h * 16 + 8], in0=pmv[:, 8:16, h],
                                                    in1=prm[:, P_ADAB + 8:P_ADAB + 16], op=ALU.add),
                     r=[pm.b, prm.b], pw=[gmod.b])
                S.op("dve", lambda: V.scalar_tensor_tensor(out=gmod[:, h * 16:h * 16 + 8], in0=gmod[:, h * 16:h * 16 + 8],
                                                           scalar=1.0, in1=prm[:, P_PREG:P_PREG + 8],
                                                           op0=ALU.add, op1=ALU.mult),
                     r=[gmod.b, prm.b], pw=[gmod.b])
                for nb in range(2):
                    sl = slice(nb * 512, (nb + 1) * 512)
                    S.op("dve", lambda: V.tensor_tensor(out=pgt[h][:, sl], in0=pg[h * 2 + nb][:, :],
                                                        in1=bc[:, BC_ADABG + nb * 512:BC_ADABG + (nb + 1) * 512],
                                                        op=ALU.add), r=[pg[h * 2 + nb].b, bc.b], pw=[pgt[h].b])
                    S.op("pool", lambda: G.tensor_tensor(out=pgt[h][:, sl], in0=pgt[h][:, sl],
                                                         in1=bc[:, BC_POSTG + nb * 512:BC_POSTG + (nb + 1) * 512],
                                                         op=ALU.mult), r=[pgt[h].b, bc.b], pw=[pgt[h].b])

    def phaseA(l, xsrc, win):
        with ExitStack() as es:
            al = lambda n, s, d: Tl(es.enter_context(nc.sbuf_tensor(n + "_L%d" % l, list(s), d)), n)
            xa = [al("xa0", [128, 4, D], F32), al("xa1", [128, 4, D], F32)]
            junk = al("junk", [128, D], F32)
            xb = [al("xb%d" % i, [128, D], BF16) for i in range(4)]
            hT = [al("hT0", [128, 8, 512], BF16), al("hT1", [128, 8, 512], BF16)]
            fm = [al("fm%d" % i, [128, 512], BF16) for i in range(4)]
            avu = [al("avu0", [128, 512], BF16), al("avu1", [128, 512], BF16)]
            qk = [al("qk0", [128, 640], F32), al("qk1", [128, 640], F32)]
            rt = [al("rt%d" % i, [128, 320], F32) for i in range(4)]
            qkr = [al("qkr0", [128, 640], BF16), al("qkr1", [128, 640], BF16)]
            qkT = [al("qkT0", [64, 10, 128], BF16), al("qkT1", [64, 10, 128], BF16)]
            vt = [al("vt0", [128, 128], BF16), al("vt1", [128, 128], BF16)]
            vn = [al("vn0", [128, 256], BF16), al("vn1", [128, 256], BF16)]
            sgt = al("sgt", [128, 256], F32)
            od = [al("od0", [128, 256], BF16), al("od1", [128, 256], BF16)]
            zs = [al("zs0", [128, DMIX], BF16), al("zs1", [128, DMIX], BF16)]
            wsT = al("wsT", [128, 4, 128], BF16)
            wsTf = al("wsTf", [128, 512], F32)
            S.dma(wsTf[:], sguw_in[l], r=[sguw_in.b], w=[wsTf.b], sem="misc")
            S.op("dve", lambda: V.tensor_copy(out=wsT[:].rearrange("p g t -> p (g t)"), in_=wsTf[:]), r=[wsTf.b], w=[wsT.b])

            stP = al("stP", [128, 8], F32)
            stQ = [al("stQ0", [128, 32], F32), al("stQ1", [128, 32], F32)]
            stS = [al("stS0", [128, 4], F32), al("stS1", [128, 4], F32)]
            junkQ = [al("junkQ0", [128, 640], F32), al("junkQ1", [128, 640], F32)]
            junkS = al("junkS", [128, 256], F32)
            cr = [al("cr0", [128, 512], F32), al("cr1", [128, 512], F32)]
            dvr = [al("dvr0", [128, 256], F32), al("dvr1", [128, 256], F32)]
            rp4 = [al("rp4%d" % i, [128, 4, 64], F32) for i in range(4)]
            gG = gmod[:, 0:8].unsqueeze(2).to_broadcast([128, 8, 128])
            gS = gmod[:, 8:16].unsqueeze(2).to_broadcast([128, 8, 128])

            def rstd2(t_, src_c, dst_c, n, inv_n):
                S.op("dve", lambda: V.tensor_scalar(out=t_[:, dst_c:dst_c + n], in0=t_[:, src_c:src_c + n], scalar1=inv_n,
                                                    scalar2=EPS, op0=ALU.mult, op1=ALU.add), r=[t_.b], w=[t_.b])
                S.op("pool", lambda: G.tensor_tensor(out=t_[:, dst_c:dst_c + n], in0=t_[:, dst_c:dst_c + n], in1=mhalf[:, 0:n],
                                                     op=ALU.pow), r=[t_.b, mhalf.b], w=[t_.b])

            def load_x(st):
                S.dma(xa[st % 2][:], xsrc[st * 512:(st + 1) * 512, :].rearrange("(a p) d -> p a d", p=128),
                      r=[xsrc.b], w=[xa[st % 2].b], sem="xa%d" % (st % 2))
                if "A_norope" not in dbg:
                    S.dma(rp4[st % 4][:], rope_in[st * 512:(st + 1) * 512, :].rearrange("(a p) c -> p a c", p=128),
                          r=[rope_in.b], w=[rp4[st % 4].b], sem="rp%d" % (st % 4))

            def prepA(st):
                xt = xa[st % 2]
                for a in range(4):
                    S.op("act", lambda: A.activation(out=junk[:], in_=xt[:, a, :], func=AF.Square,
                                                     accum_out=stP[:, a:a + 1]), r=[xt.b], w=[junk.b], pw=[stP.b])
                rstd2(stP, 0, 4, 4, 1.0 / D)
                for a in range(4):
                    xb_ = xb[a]
                    S.op("dve", lambda: V.tensor_scalar(out=xb_[:], in0=xt[:, a, :], scalar1=stP[:, 4 + a:5 + a],
                                                        scalar2=None, op0=ALU.mult), r=[xt.b, stP.b], w=[xb_.b])

            def prepB(st):
                h_ = hT[st % 2]
                for a in range(4):
                    xb_ = xb[a]
                    for c in range(8):
                        S.op("pe", lambda: P.transpose(out=ptr[:, c, :], in_=xb_[:, c * 128:(c + 1) * 128],
                                                      identity=ident[:]), r=[xb_.b, ident.b],
                             **({"w": [ptr.b]} if c == 0 else {"pw": [ptr.b]}))
                    S.op("dve", lambda: V.tensor_tensor(out=h_[:, :, a * 128:(a + 1) * 128], in0=ptr[:], in1=gG,
                                                        op=ALU.mult), r=[ptr.b, gmod.b], pw=[h_.b])
                    S.op("pool", lambda: G.tensor_tensor(out=h_[:, :, a * 128:(a + 1) * 128],
                                                         in0=h_[:, :, a * 128:(a + 1) * 128], in1=gS, op=ALU.add),
                         r=[h_.b, gmod.b], pw=[h_.b])

            def fm_groups(ti):
                st, a = divmod(ti, 4)
                h_ = hT[st % 2]
                tsl = slice(st * 512, (st + 1) * 512)
                for gi, (c0, m, dst, dstT) in enumerate(((0, 128, SV["gqt"], SD[3]), (128, 128, SV["gkt"], SD[3]),
                                                         (512, 16, SV["lrf"], SD[4]), (528, 16, SV["lrb"], SD[4]))):
                    bk = pb[1 + gi]
                    for kc in range(8):
                        S.op("pe", lambda: P.matmul(out=bk[0:m, :], lhsT=win[:, kc, c0:c0 + m], rhs=h_[:, kc, :],
                                                   start=(kc == 0), stop=(kc == 7)), r=[win.b, h_.b],
                             **({"w": [bk.b]} if kc == 0 else {"pw": [bk.b]}))
                    f_ = fm[gi]
                    S.op("act", lambda: A.copy(out=f_[0:m, :], in_=bk[0:m, :]), r=[bk.b], w=[f_.b])
                    S.dma(dst[0:m, tsl], f_[0:m, :], r=[f_.b], pw=[dstT.b], sem="fm%d" % gi)

            def mmgroup(ti, bk, col_lo, col_hi, out_lo, first):
                st, a = divmod(ti, 4)
                h_ = hT[st % 2]
                for kc in range(8):
                    S.op("pe", lambda: P.matmul(out=bk[:, out_lo:out_lo + (col_hi - col_lo)], lhsT=h_[:, kc, a * 128:(a + 1) * 128],
                                               rhs=win[:, kc, col_lo:col_hi], start=(kc == 0), stop=(kc == 7)),
                         r=[win.b, h_.b], **({"w": [bk.b]} if (kc == 0 and first) else {"pw": [bk.b]}))

            def stageXM(ti):
                mmgroup(ti, pb[0], 256, 512, 0, True)
                mmgroup(ti, pb[0], 544, 800, 256, False)
                mmgroup(ti, pb[1], 800, 1312, 0, True)
                mmgroup(ti, pb[2], 1312, 1824, 0, True)
                mmgroup(ti, pb[3], 1824, 2336, 0, True)
                mmgroup(ti, pb[4], 2336, 2848, 0, True)

            def stageXE(ti):
                st, a = divmod(ti, 4)
                rows = slice(ti * 128, (ti + 1) * 128)
                bA, bB, bC, bD, bE = pb
                av_ = avu[ti % 2]
                S.op("act", lambda: A.copy(out=av_[:], in_=bA[:, :]), r=[bA.b], w=[av_.b])
                avT, avv = avu_rows(SD, 0, ti * 128, 128)
                S.dma(avv, av_[:], r=[av_.b], pw=[avT.b], sem="avu%d" % (ti % 2))
                qk_ = qk[ti % 2]
                S.op("dve", lambda: V.tensor_copy(out=qk_[:, 0:512], in_=bB[:, :]), r=[bB.b], w=[qk_.b])
                S.op("dve", lambda: V.tensor_copy(out=qk_[:, 512:640], in_=bC[:, 0:128]), r=[bC.b], pw=[qk_.b])
                cr_ = cr[ti % 2]
                S.op("dve", lambda: V.tensor_copy(out=cr_[:, 128:512], in_=bC[:, 128:512]), r=[bC.b], w=[cr_.b])
                vt_ = vt[ti % 2]
                S.op("pool", lambda: G.tensor_copy(out=vt_[:], in_=cr_[:, 128:256]), r=[cr_.b], w=[vt_.b])
                S.dma(SV["vv"][rows, :], vt_[:], r=[vt_.b], pw=[SD[2].b], sem="vt%d" % (ti % 2))
                dv_ = dvr[ti % 2]
                S.op("act", lambda: A.copy(out=dv_[:], in_=bD[:, 0:256]), r=[bD.b], w=[dv_.b])
                zs_ = zs[ti % 2]
                S.op("act", lambda: A.activation(out=zs_[:, 0:256], in_=bD[:, 256:512], func=AF.Silu), r=[bD.b], w=[zs_.b])
                S.op("act", lambda: A.activation(out=zs_[:, 256:768], in_=bE[:, :], func=AF.Silu), r=[bE.b], pw=[zs_.b])
                mmgroup(ti, pb[0], 2848, 3360, 0, True)
                S.op("act", lambda: A.activation(out=zs_[:, 768:1280], in_=pb[0][:, :], func=AF.Silu), r=[pb[0].b], pw=[zs_.b])
                S.dma(ZS[rows, :], zs_[:], r=[zs_.b], pw=[ZS.b], sem="zs%d" % (ti % 2))
                if a == 0:
                    fm_groups(ti)

            def stageY(ti):
                st, a = divmod(ti, 4)
                tsl = slice(st * 512, (st + 1) * 512)
                rows = slice(ti * 128, (ti + 1) * 128)
                cr_ = cr[ti % 2]
                dv_ = dvr[ti % 2]
                ss_ = stS[ti % 2]
                vn_ = vn[ti % 2]
                S.op("act", lambda: A.activation(out=junkS[:], in_=dv_[:], func=AF.Square, accum_out=ss_[:, 0:1]),
                     r=[dv_.b], w=[junkS.b], pw=[ss_.b])
                rstd2(ss_, 0, 1, 1, 1.0 / 256)
                S.op("dve", lambda: V.scalar_tensor_tensor(out=vn_[:], in0=dv_[:], scalar=ss_[:, 1:2],
                                                           in1=bc[:, BC_SGUG:BC_SGUG + 256], op0=ALU.mult, op1=ALU.mult),
                     r=[dv_.b, ss_.b, bc.b], w=[vn_.b])
                qT_ = qkT[ti % 2]
                qk_ = qk[ti % 2]
                sq_ = stQ[ti % 2]
                jq_ = junkQ[ti % 2]
                S.op("act", lambda: A.activation(out=jq_[:], in_=qk_[:], func=AF.Square), r=[qk_.b], w=[jq_.b])
                S.op("dve", lambda: V.tensor_reduce(out=sq_[:, 0:10], in_=jq_[:].rearrange("p (h d) -> p h d", d=64),
                                                    op=ALU.add, axis=AX.X), r=[jq_.b], w=[sq_.b])
                rstd2(sq_, 0, 10, 10, 1.0 / 64)
                S.op("dve", lambda: V.tensor_tensor(out=qk_[:].rearrange("p (h d) -> p h d", d=64),
                                                    in0=qk_[:].rearrange("p (h d) -> p h d", d=64),
                                                    in1=sq_[:, 10:20].unsqueeze(2).to_broadcast([128, 10, 64]),
                                                    op=ALU.mult), r=[qk_.b, sq_.b], w=[qk_.b])
                S.op("dve", lambda: V.tensor_tensor(out=qk_[:], in0=qk_[:], in1=bc[:, BC_QKG:BC_QKG + 640],
                                                    op=ALU.mult), r=[qk_.b, bc.b], w=[qk_.b])
                rp_ = rp4[st % 4]
                qv = qk_[:].rearrange("p (h i two) -> p h i two", h=10, two=2)
                x0, x1 = qv[:, :, :, 0], qv[:, :, :, 1]
                cosb = rp_[:, a, 0:32].unsqueeze(1).to_broadcast([128, 10, 32])
                sinb = rp_[:, a, 32:64].unsqueeze(1).to_broadcast([128, 10, 32])
                qr_ = qkr[ti % 2]
                ov = qr_[:].rearrange("p (h i two) -> p h i two", h=10, two=2)
                r3 = lambda t_: t_[:].rearrange("p (h i) -> p h i", h=10)
                S.op("dve", lambda: V.tensor_tensor(out=r3(rt[0]), in0=x0, in1=cosb, op=ALU.mult), r=[qk_.b, rp_.b], w=[rt[0].b])
                S.op("pool", lambda: G.tensor_tensor(out=r3(rt[1]), in0=x1, in1=sinb, op=ALU.mult), r=[qk_.b, rp_.b], w=[rt[1].b])
                S.op("pool", lambda: G.tensor_tensor(out=r3(rt[2]), in0=x0, in1=sinb, op=ALU.mult), r=[qk_.b, rp_.b], w=[rt[2].b])
                S.op("dve", lambda: V.tensor_tensor(out=r3(rt[3]), in0=x1, in1=cosb, op=ALU.mult), r=[qk_.b, rp_.b], w=[rt[3].b])
                S.op("dve", lambda: V.tensor_tensor(out=ov[:, :, :, 0], in0=r3(rt[0]), in1=r3(rt[1]), op=ALU.subtract),
                     r=[rt[0].b, rt[1].b], w=[qr_.b])
                S.op("pool", lambda: G.tensor_tensor(out=ov[:, :, :, 1], in0=r3(rt[2]), in1=r3(rt[3]), op=ALU.add),
                     r=[rt[2].b, rt[3].b], pw=[qr_.b])
                for j in range(10):
                    pt_ = ptq if j < 8 else ptk
                    jj = j if j < 8 else j - 8
                    S.op("pe", lambda: P.transpose(out=pt_[0:64, jj, :], in_=qr_[:, j * 64:(j + 1) * 64], identity=ident[:]),
                         r=[qr_.b, ident.b], **({"w": [pt_.b]} if jj == 0 else {"pw": [pt_.b]}))
                S.op("act", lambda: A.copy(out=qT_[:, 0:8, :], in_=ptq[0:64, :, :]), r=[ptq.b], w=[qT_.b])
                S.op("act", lambda: A.copy(out=qT_[:, 8:10, :], in_=ptk[0:64, 0:2, :]), r=[ptk.b], pw=[qT_.b])
                S.dma(QT[:, :, rows], qT_[:, 0:8, :], r=[qT_.b], pw=[QT.b], sem="qT%d" % (ti % 2))
                S.dma(SV["kt"][:, :, rows], qT_[:, 8:10, :], r=[qT_.b], pw=[SD[2].b], sem="kT%d" % (ti % 2))
                bS = sideT
                for g in range(4):
                    S.op("pe", lambda: P.matmul(out=bS[:, g * 64:(g + 1) * 64], lhsT=wsT[:, g, :], rhs=vn_[:, g * 64:(g + 1) * 64],
                                               start=True, stop=True), r=[wsT.b, vn_.b],
                         **({"w": [bS.b]} if g == 0 else {"pw": [bS.b]}))
                S.op("dve", lambda: V.tensor_tensor(out=sgt[:].rearrange("p (g c) -> p g c", c=64),
                                                    in0=bS[:, 0:256].rearrange("p (g c) -> p g c", c=64),
                                                    in1=prm[:, P_SGUB:P_SGUB + 4].unsqueeze(2).to_broadcast([128, 4, 64]),
                                                    op=ALU.add), r=[bS.b, prm.b], w=[sgt.b])
                od_ = od[ti % 2]
                S.op("dve", lambda: V.tensor_tensor(out=od_[:], in0=sgt[:], in1=cr_[:, 256:512], op=ALU.mult),
                     r=[sgt.b, cr_.b], w=[od_.b])
                S.dma(MX[rows, 1024:1280], od_[:], r=[od_.b], pw=[MX.b], sem="od%d" % (ti % 2))

            load_x(0)
            load_x(1)
            prepA(0)
            prepB(0)
            if "A_preponly" in dbg:
                return
            for ti in range(NTO):
                st, a = divmod(ti, 4)
                stageXM(ti)
                if ti >= 1:
                    stageY(ti - 1)
                stageXE(ti)
                if ti == 15:
                    S.allgather(SD[0], RV[0], RG, "ag0")
                if a == 0 and st + 1 < NSUP:
                    prepA(st + 1)
                if a == 2 and st + 1 < NSUP:
                    prepB(st + 1)
                    if st + 2 < NSUP:
                        load_x(st + 2)
            stageY(NTO - 1)

    def phaseC(l, xsrc, ydst):
        NS = 4
        with ExitStack() as es:
            al = lambda n, s, d: Tl(es.enter_context(nc.sbuf_tensor(n + "_L%d" % l, list(s), d)), n)
            cm = [al("cm%d" % i, [128, DMIX], BF16) for i in range(NS)]
            cz = [al("cz%d" % i, [128, DMIX], BF16) for i in range(NS)]
            cf = [al("cf%d" % i, [128, 6, 256], BF16) for i in range(NS)]
            cx = [al("cx%d" % i, [128, D], F32) for i in range(NS)]
            cmz = [al("cmz%d" % i, [128, DMIX], BF16) for i in range(NS)]
            cmT = [al("cmT%d" % i, [128, 10, 128], BF16) for i in range(NS)]
            cjunk = al("cjunk", [128, 512], F32)
            cy = [al("cy%d" % i, [128, D], F32) for i in range(NS)]
            wout = al("wout", [128, 10, D], BF16)
            ws = [al("wsc0", [128, D], F32), al("wsc1", [128, D], F32)]
            for kc in range(10):
                w_ = ws[kc % 2]
                S.dma(w_[:, 0:D], wout_in[l, kc * 128:(kc + 1) * 128, :], r=[wout_in.b], w=[w_.b], sem="ws%d" % (kc % 2))
                eng = ("act", "dve", "dve")[kc % 3]
                if eng == "act":
                    S.op("act", lambda: A.copy(out=wout[:, kc, :], in_=w_[:, 0:D]), r=[w_.b], pw=[wout.b])
                elif eng == "pool":
                    S.op("pool", lambda: G.tensor_copy(out=wout[:, kc, :], in_=w_[:, 0:D]), r=[w_.b], pw=[wout.b])
                else:
                    S.op("dve", lambda: V.tensor_copy(out=wout[:, kc, :], in_=w_[:, 0:D]), r=[w_.b], pw=[wout.b])
            def loads(ti):
                s = ti % NS
                rows = slice(ti * 128, (ti + 1) * 128)
                S.dma(cm[s][:], MX[rows, :], r=[MX.b], w=[cm[s].b], sem="cm%d" % s)
                cands = [(MXF, 0), (MXF, 1), (FP, 0), (FP, 1), (GO, 0), (GO, 1)]
                for ci, (src_t, rg_) in enumerate(cands):
                    S.dma(cf[s][:, ci, :], src_t[rg_ * TO + ti * 128:rg_ * TO + (ti + 1) * 128, :], r=[src_t.b],
                          **({"w": [cf[s].b]} if ci == 0 else {"pw": [cf[s].b]}), sem="cf%d_%d" % (s, ci))
                S.dma(cz[s][:], ZS[rows, :], r=[ZS.b], w=[cz[s].b], sem="cz%d" % s)
                S.dma(cx[s][:], xsrc[rows, :], r=[xsrc.b], w=[cx[s].b], sem="cx%d" % s)

            cfts = [al("cft%d" % i, [128, 256], F32) for i in range(3)]
            stC = [al("stC%d" % i, [128, 8], F32) for i in range(2)]
            byT = {}

            def front(ti):
                s = ti % NS
                cft_ = cfts[ti % 3]
                S.op("dve", lambda: V.tensor_scalar(out=cft_[:], in0=cf[s][:, 0, :], scalar1=flags[:, 0:1], scalar2=None,
                                                    op0=ALU.mult), r=[cf[s].b, flags.b], w=[cft_.b])
                for ci in (1, 2):
                    S.op("dve", lambda: V.scalar_tensor_tensor(out=cft_[:], in0=cf[s][:, ci, :], scalar=flags[:, ci:ci + 1],
                                                               in1=cft_[:], op0=ALU.mult, op1=ALU.add),
                         r=[cf[s].b, cft_.b, flags.b], w=[cft_.b])
                S.op("dve", lambda: V.scalar_tensor_tensor(out=cm[s][:, 256:512], in0=cf[s][:, 3, :], scalar=flags[:, 3:4],
                                                           in1=cft_[:], op0=ALU.mult, op1=ALU.add),
                     r=[cf[s].b, cft_.b, flags.b], pw=[cm[s].b])
                S.op("dve", lambda: V.tensor_scalar(out=cft_[:], in0=cf[s][:, 4, :], scalar1=flags[:, 4:5], scalar2=None,
                                                    op0=ALU.mult), r=[cf[s].b, flags.b], w=[cft_.b])
                S.op("dve", lambda: V.scalar_tensor_tensor(out=cm[s][:, 0:256], in0=cf[s][:, 5, :], scalar=flags[:, 5:6],
                                                           in1=cft_[:], op0=ALU.mult, op1=ALU.add),
                     r=[cf[s].b, cft_.b, flags.b], pw=[cm[s].b])
                S.op("dve", lambda: V.tensor_tensor(out=cmz[s][:], in0=cm[s][:], in1=cz[s][:], op=ALU.mult),
                     r=[cm[s].b, cz[s].b], w=[cmz[s].b])

            def mid1(ti):
                s = ti % NS
                for c in range(10):
                    pt_ = ptr if c < 8 else ptk
                    cc = c if c < 8 else c - 8
                    S.op("pe", lambda: P.transpose(out=pt_[:, cc, :], in_=cmz[s][:, c * 128:(c + 1) * 128], identity=ident[:]),
                         r=[cmz[s].b, ident.b], **({"w": [pt_.b]} if cc == 0 else {"pw": [pt_.b]}))
                S.op("act", lambda: A.copy(out=cmT[s][:, 0:8, :], in_=ptr[:]), r=[ptr.b], w=[cmT[s].b])
                S.op("act", lambda: A.copy(out=cmT[s][:, 8:10, :], in_=ptk[:, 0:2, :]), r=[ptk.b], pw=[cmT[s].b])

            def mid2(ti):
                s = ti % NS
                by = [pb[(2 * ti) % 4], pb[(2 * ti + 1) % 4]]
                byT[ti] = by
                for nb in range(2):
                    for kc in range(10):
                        S.op("pe", lambda: P.matmul(out=by[nb][:, :], lhsT=cmT[s][:, kc, :], rhs=wout[:, kc, nb * 512:(nb + 1) * 512],
                                                   start=(kc == 0), stop=(kc == 9)), r=[cmT[s].b, wout.b],
                             **({"w": [by[nb].b]} if kc == 0 else {"pw": [by[nb].b]}))

            def tail(ti):
                s = ti % NS
                rows = slice(ti * 128, (ti + 1) * 128)
                by = byT.pop(ti)
                st_ = stC[ti % 2]
                for nb in range(2):
                    S.op("act", lambda: A.activation(out=cjunk[:], in_=by[nb][:, :], func=AF.Square,
                                                     accum_out=st_[:, nb:nb + 1]), r=[by[nb].b], w=[cjunk.b], pw=[st_.b])
                S.op("dve", lambda: V.tensor_tensor(out=st_[:, 2:3], in0=st_[:, 0:1], in1=st_[:, 1:2], op=ALU.add),
                     r=[st_.b], w=[st_.b])
                S.op("dve", lambda: V.tensor_scalar(out=st_[:, 3:4], in0=st_[:, 2:3], scalar1=1.0 / D, scalar2=EPS,
                                                    op0=ALU.mult, op1=ALU.add), r=[st_.b], w=[st_.b])
                S.op("pool", lambda: G.tensor_tensor(out=st_[:, 3:4], in0=st_[:, 3:4], in1=mhalf[:, 0:1], op=ALU.pow),
                     r=[st_.b, mhalf.b], w=[st_.b])
                for nb in range(2):
                    sl = slice(nb * 512, (nb + 1) * 512)
                    S.op("dve", lambda: V.scalar_tensor_tensor(out=cy[s][:, sl], in0=by[nb][:, :], scalar=st_[:, 3:4],
                                                               in1=pgt[0][:, sl], op0=ALU.mult, op1=ALU.mult),
                         r=[by[nb].b, st_.b, pgt[0].b], **({"w": [cy[s].b]} if nb == 0 else {"pw": [cy[s].b]}))
                S.op("dve", lambda: V.tensor_tensor(out=cy[s][:], in0=cy[s][:], in1=cx[s][:], op=ALU.add),
                     r=[cy[s].b, cx[s].b], w=[cy[s].b])
                S.dma(ydst[rows, :], cy[s][:], r=[cy[s].b], pw=[ydst.b], sem="cy%d" % s)

            loads(0)
            loads(1)
            loads(2)
            front(0)
            front(1)
            mid1(0)
            for ti in range(NTO):
                if ti + 3 < NTO:
                    loads(ti + 3)
                if ti + 2 < NTO:
                    front(ti + 2)
                mid2(ti)
                if ti + 1 < NTO:
                    mid1(ti + 1)
                tail(ti)

    def phaseAttn(l, es, qtiles=None):
        if True:
            al = lambda n, s, d: Tl(es.enter_context(nc.sbuf_tensor(n + "_L%d" % l, list(s), d)), n)
            KTs = al("KTs", [128, T], BF16)
            Va = al("Va", [128, 64, 2, 66], BF16)
            aq = [al("aq0", [128, 8, 128], BF16), al("aq1", [128, 8, 128], BF16)]
            pT = [al("pT%d" % i, [128, 1024], BF16) for i in range(3)]
            oT = [al("oT0", [65, 512], F32), al("oT1", [65, 512], F32)]
            on = [al("on0", [128, 512], BF16), al("on1", [128, 512], BF16)]
            rden = al("rden", [128, 8], F32)
            for j in range(2):
                for g in range(2):
                    S.dma(KTs[g * 64:(g + 1) * 64, j * TO:(j + 1) * TO], RVV[j]["kt"][:, g, :], r=[RV[2].b], pw=[KTs.b],
                          sem="kts%d" % g)
            for q_ in aq:
                S.op("pool", lambda: G.memset(q_[:], 0.0), w=[q_.b])
            Vst = al("Vst", [128, 64, 128], BF16)
            for j in range(2):
                S.dma(Vst[:, j * 32:(j + 1) * 32, :], RVV[j]["vv"].rearrange("(k p) c -> p k c", p=128), r=[RV[2].b],
                      pw=[Vst.b], sem="va%d" % j)
            S.op("dve", lambda: V.memset(Va[:], 1.0), w=[Va.b])
            for g in range(2):
                S.op("dve", lambda: V.tensor_copy(out=Va[:, :, g, 0:64], in_=Vst[:, :, g * 64:(g + 1) * 64]), r=[Vst.b], pw=[Va.b])
            for j in range(2):
                S.op("dve", lambda: V.tensor_scalar(out=Va[:, j * 32:(j + 1) * 32, :, :].rearrange("p a b c -> p (a b c)"),
                                                    in0=Va[:, j * 32:(j + 1) * 32, :, :].rearrange("p a b c -> p (a b c)"),
                                                    scalar1=flags[:, 7 + j:8 + j], scalar2=None, op0=ALU.mult),
                     r=[Va.b, flags.b], pw=[Va.b])
            bsb = [[pb[0].b, pb[1].b], [pb[2].b, pb[3].b], [pb[4].b, ptr.b]]
            bo1 = Tl(PS_[0][:, :], "x")
            bo1.b = ptq.b
            bo = [bo1, bo1]
            bt = sideT
            qlist = list(range(NTO) if qtiles is None else qtiles)
            NP2 = NTILE // 2
            its = [(qn, qi, g, kp) for qn, qi in enumerate(qlist) for g in range(2) for kp in range(NP2)]

            def load_q(qn):
                qi = qlist[qn]
                q_ = aq[qn % 2]
                for g in range(2):
                    S.dma(q_[g * 64:(g + 1) * 64, g * 4:(g + 1) * 4, :], QT[:, g * 4:(g + 1) * 4, qi * 128:(qi + 1) * 128],
                          r=[QT.b], pw=[q_.b], sem="aq%d_%d" % (qn % 2, g))

            def mm1(i):
                qn, qi, g, kp = its[i]
                q_ = aq[qn % 2]
                for j in range(2):
                    kt = kp * 2 + j
                    S.op("pe", lambda: P.matmul(out=PD[i % 3][:, j * 512:(j + 1) * 512], lhsT=KTs[:, kt * 128:(kt + 1) * 128],
                                               rhs=q_[:, g * 4:(g + 1) * 4, :], start=True, stop=True),
                         r=[KTs.b, q_.b], w=[bsb[i % 3][j]])

            deferred = []
            load_q(0)
            if len(qlist) > 1:
                load_q(1)
            mm1(0)
            mm1(1)
            for i, (qn, qi, g, kp) in enumerate(its):
                rows = slice(qi * 128, (qi + 1) * 128)
                qhalf = qi // 32
                on_ = on[qn % 2]
                p_ = pT[i % 3]
                S.op("act", lambda: A.activation(out=p_[:], in_=PD[i % 3][:, :], func=AF.Exp, scale=0.125), r=bsb[i % 3], w=[p_.b])
                if i + 2 < len(its):
                    mm1(i + 2)
                for j in range(2):
                    kt = kp * 2 + j
                    vs = Va
                    S.op("pe", lambda: P.matmul(out=bo[g][0:65, :], lhsT=vs[:, kt, g, 0:65], rhs=p_[:, j * 512:(j + 1) * 512],
                                               start=(kt == 0), stop=(kt == NTILE - 1)),
                         r=[vs.b, p_.b], **({"w": [bo[g].b]} if kt == 0 else {"pw": [bo[g].b]}))
                if kp == NP2 - 1:
                    o_ = oT[g]
                    S.op("dve", lambda: V.tensor_copy(out=o_[:], in_=bo[g][0:65, :]), r=[bo[g].b], w=[o_.b])

                    def epi(g=g, o_=o_, on_=on_, rows=rows, qn=qn):
                        for h in range(4):
                            S.op("pe", lambda: P.transpose(out=bt[:, h * 65:(h + 1) * 65], in_=o_[0:65, h * 128:(h + 1) * 128],
                                                          identity=identf[0:65, 0:65]), r=[o_.b, identf.b],
                                 **({"w": [bt.b]} if h == 0 else {"pw": [bt.b]}))
                        btv = bt[:, 0:260].rearrange("p (h e) -> p h e", e=65)
                        S.op("dve", lambda: V.reciprocal(out=rden[:, g * 4:(g + 1) * 4], in_=btv[:, :, 64]), r=[bt.b], pw=[rden.b])
                        S.op("dve", lambda: V.tensor_tensor(out=on_[:, g * 256:(g + 1) * 256].rearrange("p (h d) -> p h d", d=64),
                                                            in0=btv[:, :, 0:64],
                                                            in1=rden[:, g * 4:(g + 1) * 4].unsqueeze(2).to_broadcast([128, 4, 64]),
                                                            op=ALU.mult), r=[bt.b, rden.b], **({"w": [on_.b]} if g == 0 else {"pw": [on_.b]}))
                        if g == 1:
                            S.dma(MX[rows, 512:1024], on_[:], r=[on_.b], pw=[MX.b], sem="on%d" % (qn % 2))
                            if qn + 2 < len(qlist):
                                load_q(qn + 2)

                    deferred.append((i + 3, epi))
                while deferred and deferred[0][0] <= i and not S.pending:
                    deferred.pop(0)[1]()
                yield
            while deferred:
                if S.pending:
                    yield
                    continue
                deferred.pop(0)[1]()
            yield

    def phaseGLA(l, side=False):
        BW = 512
        NCB = BW // 64
        gbank = (lambda: sideT) if side else bank
        ptrT = ptk if side else ptr
        with ExitStack() as es:
            al = lambda n, s, d: Tl(es.enter_context(nc.sbuf_tensor(n + "_L%d" % l, list(s), d)), n)
            gc = al("gc", [128, GC_N], F32)
            wgf = al("wgf", [16, 256], F32)
            wgb = al("wgb", [16, 256], BF16)
            nbg = al("nbg", [128, 4], F32)
            Sst = al("Sst", [128, 256], F32)
            Sbf = al("Sbf", [128, 256], BF16)
            tmpS = al("tmpS", [128, 256], F32)
            lrs = [al("glr%d" % i, [16, BW], BF16) for i in range(2)]
            qTbs = [al("gqTb%d" % i, [128, BW], BF16) for i in range(2)]
            kTbs = [al("gkTb%d" % i, [128, BW], BF16) for i in range(2)]
            sp = al("gsp", [128, BW], F32)
            cs = al("gcs", [128, BW], F32)
            Eb = al("gE", [128, BW], F32)
            Ei = al("gEi", [128, BW], F32)
            totc = al("gtot", [128, NCB], F32)
            Qbd = al("gQbd", [128, NCB, 4, 64], BF16)
            qtl = al("gqtl", [128, BW], BF16)
            ktl = al("gktl", [128, BW], BF16)
            vchs = [al("gvch%d" % i, [64, NCB, 256], BF16) for i in range(2)]
            ktok = [al("gktok0", [64, 128], BF16), al("gktok1", [64, 128], BF16)]
            sT = [al("gsT0", [64, 256], BF16), al("gsT1", [64, 256], BF16)]
            osb = al("gosb", [64, NCB, 256], F32)
            ofls = [al("gofl%d" % i, [64, NCB, 256], F32) for i in range(2)]
            obf = al("gobf", [64, NCB, 256], BF16)
            gst = al("ggst", [64, 2 * NCB * 4], F32)
            S.dma(gc[:], gconst_in[:, :], r=[gconst_in.b], w=[gc.b], sem="misc")
            S.dma(wgf[:], wg2_in[l], r=[wg2_in.b], w=[wgf.b], sem="misc")
            S.op("dve", lambda: V.tensor_copy(out=wgb[:], in_=wgf[:]), r=[wgf.b], w=[wgb.b])
            S.op("dve", lambda: V.tensor_scalar(out=nbg[:, 0:2], in0=prm[:, P_BGF:P_BGF + 2], scalar1=-1.0, scalar2=None,
                                                op0=ALU.mult), r=[prm.b], w=[nbg.b])
            S.op("dve", lambda: V.memset(nbg[:, 2:3], 1.0), pw=[nbg.b])
            nblk = T // BW
            NBR = TO // BW
            gblocks = [(d_, b_) for d_ in range(2) for b_ in (range(nblk) if d_ == 0 else range(nblk - 1, -1, -1))]
            kblk = [0]

            def gl_load(kb, late=False):
                d_, b_ = gblocks[kb]
                s_ = kb % 2
                rj, lb = b_ // NBR, b_ % NBR
                lsl = slice(lb * BW, (lb + 1) * BW)
                S.dma(lrs[s_][:], RVV[rj]["lrf" if d_ == 0 else "lrb"][:, lsl], r=[RV[4].b], w=[lrs[s_].b], sem="g_lr%d" % s_)
                S.dma(qTbs[s_][:], RVV[rj]["gqt"][:, lsl], r=[RV[3].b], w=[qTbs[s_].b], sem="g_q%d" % s_)
                S.dma(kTbs[s_][:], RVV[rj]["gkt"][:, lsl], r=[RV[3].b], w=[kTbs[s_].b], sem="g_k%d" % s_)
                avT, avv = avu_rows(RV, rj, lb * BW, BW)
                S.dma(vchs[s_][:], avv[:, 0:256].rearrange("(c s) f -> s c f", s=64), r=[avT.b], w=[vchs[s_].b], sem="g_v%d" % s_)
                if d_ == 1 and (kb != nblk or late):
                    S.dma(ofls[s_][:], OF[b_ * BW:(b_ + 1) * BW, :].rearrange("(c s) f -> s c f", s=64), r=[OF.b], w=[ofls[s_].b],
                          sem="g_of%d" % s_)

            gl_load(0)
            for d in range(2):
                S.op("dve", lambda: V.memset(Sst[:], 0.0), w=[Sst.b])
                S.op("dve", lambda: V.memset(Sbf[:], 0.0), w=[Sbf.b])
                lrk = "lrf" if d == 0 else "lrb"
                mcol = GC_MF if d == 0 else GC_MB
                for bi in (range(nblk) if d == 0 else range(nblk - 1, -1, -1)):
                    t0 = bi * BW
                    tsl = slice(t0, t0 + BW)
                    kb = kblk[0]
                    kblk[0] += 1
                    if kb + 1 < len(gblocks):
                        gl_load(kb + 1)
                    if kb == nblk:
                        d_, b_ = gblocks[kb]
                        S.dma(ofls[kb % 2][:], OF[b_ * BW:(b_ + 1) * BW, :].rearrange("(c s) f -> s c f", s=64), r=[OF.b],
                              w=[ofls[kb % 2].b], sem="g_of%d" % (kb % 2))
                    lr, qTb, kTb, vch, ofl = lrs[kb % 2], qTbs[kb % 2], kTbs[kb % 2], vchs[kb % 2], ofls[kb % 2]
                    for j in range(BW // 512):
                        bk = gbank()
                        S.op("pe", lambda: P.matmul(out=bk[:, :], lhsT=wgb[:, d * 128:(d + 1) * 128], rhs=lr[:, j * 512:(j + 1) * 512],
                                                   start=True, stop=True), r=[wgb.b, lr.b], w=[bk.b])
                        S.op("act", lambda: A.activation(out=sp[:, j * 512:(j + 1) * 512], in_=bk[:, :], func=AF.Exp, scale=-1.0,
                                                         bias=nbg[:, d:d + 1]), r=[bk.b, nbg.b], **({"w": [sp.b]} if j == 0 else {"pw": [sp.b]}))
                    yield
                    S.op("act", lambda: A.activation(out=sp[:], in_=sp[:], func=AF.Ln, bias=nbg[:, 2:3]), r=[sp.b, nbg.b], w=[sp.b])
                    yield
                    S.op("dve", lambda: V.tensor_tensor_scan(out=cs[:], data0=gc[:, GC_SM:GC_SM + BW], data1=sp[:], initial=0.0,
                                                             op0=ALU.mult, op1=ALU.add), r=[gc.b, sp.b], w=[cs.b])
                    c3 = lambda t_: t_[:].rearrange("p (c s) -> p c s", s=64)
                    if d == 1:
                        S.op("dve", lambda: V.tensor_copy(out=totc[:], in_=c3(cs)[:, :, 63]), r=[cs.b], w=[totc.b])
                        S.op("dve", lambda: V.tensor_tensor(out=Ei[:], in0=sp[:], in1=cs[:], op=ALU.subtract), r=[sp.b, cs.b], w=[Ei.b])
                        S.op("dve", lambda: V.tensor_tensor(out=c3(cs), in0=c3(Ei), in1=totc[:].unsqueeze(2).to_broadcast([128, NCB, 64]),
                                                            op=ALU.add), r=[Ei.b, totc.b], w=[cs.b])
                    yield
                    S.op("act", lambda: A.activation(out=Eb[:], in_=cs[:], func=AF.Exp, scale=-1.0 / 16), r=[cs.b], w=[Eb.b])
                    S.op("act", lambda: A.activation(out=Ei[:], in_=cs[:], func=AF.Exp, scale=1.0 / 16), r=[cs.b], w=[Ei.b])
                    yield
                    for h in range(4):
                        S.op("dve", lambda: V.scalar_tensor_tensor(out=Qbd[:, :, h, :], in0=c3(qTb), scalar=gc[:, GC_HM + h:GC_HM + h + 1],
                                                                   in1=c3(Eb), op0=ALU.mult, op1=ALU.mult),
                             r=[qTb.b, gc.b, Eb.b], **({"w": [Qbd.b]} if h == 0 else {"pw": [Qbd.b]}))
                    S.op("dve", lambda: V.scalar_tensor_tensor(out=qtl[:], in0=qTb[:], scalar=32.0 ** -0.5, in1=Eb[:],
                                                               op0=ALU.mult, op1=ALU.mult), r=[qTb.b, Eb.b], w=[qtl.b])
                    S.op("pool", lambda: G.tensor_tensor(out=ktl[:], in0=kTb[:], in1=Ei[:], op=ALU.mult), r=[kTb.b, Ei.b], w=[ktl.b])
                    yield
                    for c in (range(NCB) if d == 0 else range(NCB - 1, -1, -1)):
                        gcx = bi * NCB + c
                        if (d == 0 and gcx == 64) or (d == 1 and gcx == 63):
                            S.op("dve", lambda: V.tensor_scalar(out=Sst[:], in0=Sst[:], scalar1=flags[:, 6:7], scalar2=None, op0=ALU.mult),
                                 r=[Sst.b, flags.b], w=[Sst.b])
                            S.op("dve", lambda: V.tensor_copy(out=Sbf[:], in_=Sst[:]), r=[Sst.b], w=[Sbf.b])
                        csl = slice(c * 64, (c + 1) * 64)
                        kt_ = ktok[c % 2]
                        sT_ = sT[c % 2]
                        S.op("pe", lambda: P.transpose(out=ptrT[0:64, 0, :], in_=ktl[:, csl], identity=ident[:]), r=[ktl.b, ident.b], w=[ptrT.b])
                        yield
                        S.op("dve", lambda: V.tensor_copy(out=kt_[:], in_=ptrT[0:64, 0, :]), r=[ptrT.b], w=[kt_.b])
                        bsc = gbank()
                        S.op("pe", lambda: P.matmul(out=bsc[0:64, 0:256], lhsT=ktl[:, csl], rhs=Qbd[:, c, :, :], start=True, stop=True),
                             r=[ktl.b, Qbd.b], w=[bsc.b])
                        yield
                        S.op("dve", lambda: V.tensor_tensor(out=sT_[:].rearrange("p (h t) -> p h t", t=64),
                                                            in0=bsc[0:64, 0:256].rearrange("p (h t) -> p h t", t=64),
                                                            in1=gc[0:64, mcol:mcol + 64].unsqueeze(1).to_broadcast([64, 4, 64]),
                                                            op=ALU.mult), r=[bsc.b, gc.b], w=[sT_.b])
                        yield
                        bo = gbank()
                        S.op("pe", lambda: P.matmul(out=bo[0:64, 0:256], lhsT=qtl[:, csl], rhs=Sbf[:, :], start=True, stop=False),
                             r=[qtl.b, Sbf.b], w=[bo.b])
                        for h in range(4):
                            S.op("pe", lambda: P.matmul(out=bo[0:64, h * 64:(h + 1) * 64], lhsT=sT_[:, h * 64:(h + 1) * 64],
                                                       rhs=vch[:, c, h * 64:(h + 1) * 64], start=False, stop=(h == 3)),
                                 r=[sT_.b, vch.b], pw=[bo.b])
                        yield
                        if d == 0:
                            S.op("dve", lambda: V.tensor_copy(out=osb[:, c, :], in_=bo[0:64, 0:256]), r=[bo.b], pw=[osb.b])
                        else:
                            S.op("dve", lambda: V.tensor_tensor(out=osb[:, c, :], in0=bo[0:64, 0:256], in1=ofl[:, c, :], op=ALU.add),
                                 r=[bo.b, ofl.b], pw=[osb.b])
                        bst = gbank()
                        S.op("pe", lambda: P.matmul(out=bst[:, 0:256], lhsT=kt_[:], rhs=vch[:, c, :], start=True, stop=True),
                             r=[kt_.b, vch.b], w=[bst.b])
                        yield
                        didx = c * 64 + (63 if d == 0 else 0)
                        S.op("dve", lambda: V.scalar_tensor_tensor(out=tmpS[:], in0=bst[:, 0:256], scalar=Eb[:, didx:didx + 1],
                                                                   in1=gc[:, GC_BD:GC_BD + 256], op0=ALU.mult, op1=ALU.mult),
                             r=[bst.b, Eb.b, gc.b], w=[tmpS.b])
                        yield
                        S.op("dve", lambda: V.scalar_tensor_tensor(out=Sst[:], in0=Sst[:], scalar=Eb[:, didx:didx + 1], in1=tmpS[:],
                                                                   op0=ALU.mult, op1=ALU.add), r=[Sst.b, Eb.b, tmpS.b], w=[Sst.b])
                        yield
                        S.op("dve", lambda: V.tensor_copy(out=Sbf[:], in_=Sst[:]), r=[Sst.b], w=[Sbf.b])
                        yield
                    if "gla_no_end" in dbg or ("gla_no_end1" in dbg and d == 1):
                        continue
                    if d == 0:
                        S.dma(OF[tsl, :].rearrange("(c s) f -> s c f", s=64), osb[:], r=[osb.b], pw=[OF.b], sem="g_os")
                    else:
                        o4 = lambda t_: t_[:].rearrange("p c (h d) -> p (c h) d", d=64)
                        S.op("dve", lambda: V.tensor_tensor(out=ofl[:], in0=osb[:], in1=osb[:], op=ALU.mult), r=[osb.b], w=[ofl.b])
                        S.op("dve", lambda: V.tensor_reduce(out=gst[:, 0:NCB * 4], in_=o4(ofl), op=ALU.add, axis=AX.X), r=[ofl.b], w=[gst.b])
                        S.op("dve", lambda: V.tensor_scalar(out=gst[:, 0:NCB * 4], in0=gst[:, 0:NCB * 4], scalar1=1.0 / 64, scalar2=EPS,
                                                            op0=ALU.mult, op1=ALU.add), r=[gst.b], w=[gst.b])
                        S.op("act", lambda: A.activation(out=gst[:, 0:NCB * 4], in_=gst[:, 0:NCB * 4], func=AF.Ln), r=[gst.b], w=[gst.b])
                        S.op("act", lambda: A.activation(out=gst[:, 0:NCB * 4], in_=gst[:, 0:NCB * 4], func=AF.Exp, scale=-0.5),
                             r=[gst.b], w=[gst.b])
                        S.op("dve", lambda: V.tensor_tensor(out=o4(osb), in0=o4(osb),
                                                            in1=gst[:, 0:NCB * 4].unsqueeze(2).to_broadcast([64, NCB * 4, 64]), op=ALU.mult),
                             r=[osb.b, gst.b], w=[osb.b])
                        S.op("dve", lambda: V.tensor_tensor(out=obf[:], in0=osb[:],
                                                            in1=bc[0:64, BC_ONG:BC_ONG + 256].unsqueeze(1).to_broadcast([64, NCB, 256]),
                                                            op=ALU.mult), r=[osb.b, bc.b], w=[obf.b])
                        S.dma(GO[tsl, :].rearrange("(c s) f -> s c f", s=64), obf[:], r=[obf.b], pw=[GO.b], sem="g_ob")

    def phaseFFT(l, side=False):
        gbank = (lambda: sideT) if side else bank
        with ExitStack() as es:
            al = lambda n, s, d: Tl(es.enter_context(nc.sbuf_tensor(n + "_L%d" % l, list(s), d)), n)
            t1 = al("ft1", [128, 64, 256], BF16)
            t2 = al("ft2", [128, 128], BF16)
            up = al("fup", [128, 64, 256], BF16)
            csf = al("fcsf", [128, 2, 512], F32)
            csb = al("fcsb", [128, 2, 512], BF16)
            fwf = al("ffwf", [128, 2, 256], F32)
            fwb = al("ffwb", [128, 2, 256], BF16)
            w12 = al("fw12", [128, 2, 2, 256], BF16)
            g1s = [al("fg1s0", [128, 512], BF16), al("fg1s1", [128, 512], BF16)]
            ab = [al("fab0", [128, 8, 256], BF16), al("fab1", [128, 8, 256], BF16)]
            yT = [al("fyT0", [128, 2, 128], BF16), al("fyT1", [128, 2, 128], BF16)]
            ob = [al("fob0", [64, 8, 256], BF16), al("fob1", [64, 8, 256], BF16)]
            for q4 in range(4):
                S.dma(t1[:, q4 * 16:(q4 + 1) * 16, :], tab1_in[q4 * 16:(q4 + 1) * 16, :, :].rearrange("n p c -> p n c"),
                      r=[tab1_in.b], pw=[t1.b], sem="f_t1")
                avT, avv = avu_rows(RV, q4 // 2, (q4 % 2) * 2048, 2048)
                S.dma(up[q4 * 32:(q4 + 1) * 32, :, :], avv[:, 256:512].rearrange("(p n) c -> p n c", n=64),
                      r=[avT.b], pw=[up.b], sem="f_up")
            S.dma(t2[:], tab2_in[:, :], r=[tab2_in.b], w=[t2.b], sem="misc")
            S.dma(csf[:], cs64_in[:, :, :], r=[cs64_in.b], w=[csf.b], sem="misc")
            S.dma(fwf[:], fnet_in[l].rearrange("(k p) c -> p k c", p=128), r=[fnet_in.b], w=[fwf.b], sem="misc")
            if side:
                for _ in range(int(90 * SIDE_RATIO)):
                    yield
            S.op("dve", lambda: V.tensor_copy(out=csb[:], in_=csf[:]), r=[csf.b], w=[csb.b])
            S.op("dve", lambda: V.tensor_copy(out=fwb[:], in_=fwf[:]), r=[fwf.b], w=[fwb.b])
            for m in range(2):
                for wi in range(2):
                    bk = gbank()
                    for kc in range(2):
                        S.op("pe", lambda: P.matmul(out=bk[:, 0:256], lhsT=csb[:, kc, wi * 256 + m * 128:wi * 256 + (m + 1) * 128],
                                                   rhs=fwb[:, kc, :], start=(kc == 0), stop=(kc == 1)), r=[csb.b, fwb.b],
                             **({"w": [bk.b]} if kc == 0 else {"pw": [bk.b]}))
                    S.op("act", lambda: A.copy(out=w12[:, m, wi, :], in_=bk[:, 0:256]), r=[bk.b], pw=[w12.b])
            for n2 in range(64):
                bk = gbank()
                S.op("pe", lambda: P.matmul(out=bk[:, 0:256], lhsT=t1[:, n2, 0:128], rhs=up[:, n2, :], start=True, stop=True),
                     r=[t1.b, up.b], w=[bk.b])
                S.op("pe", lambda: P.matmul(out=bk[:, 256:512], lhsT=t1[:, n2, 128:256], rhs=up[:, n2, :], start=True, stop=True),
                     r=[t1.b, up.b], pw=[bk.b])
                yield
                g_ = g1s[n2 % 2]
                S.op("dve", lambda: V.tensor_copy(out=g_[:], in_=bk[:, :]), r=[bk.b], w=[g_.b])
                S.dma(G1[:, :, n2, :], g_[:].rearrange("p (r c) -> p r c", r=2), r=[g_.b], pw=[G1.b], sem="f_g%d" % (n2 % 2))
                yield
            mxv = MXF[:, :].rearrange("(k p) c -> k p c", p=128)
            for pgp in range(16):
                p0 = pgp * 8
                ab_ = ab[pgp % 2]
                ob_ = ob[pgp % 2]
                if pgp == 0:
                    S.dma(ab_[:], G1[p0:p0 + 8, :, :, :].rearrange("p r n c -> (r n) p c"), r=[G1.b], w=[ab_.b], sem="f_ab%d" % (pgp % 2))
                if pgp + 1 < 16:
                    S.dma(ab[(pgp + 1) % 2][:], G1[p0 + 8:p0 + 16, :, :, :].rearrange("p r n c -> (r n) p c"), r=[G1.b],
                          w=[ab[(pgp + 1) % 2].b], sem="f_ab%d" % ((pgp + 1) % 2))
                for j in range(8):
                    y_ = yT[j % 2]
                    bk = gbank()
                    for m in range(2):
                        S.op("pe", lambda: P.matmul(out=bk[:, m * 128:(m + 1) * 128], lhsT=ab_[:, j, m * 128:(m + 1) * 128], rhs=t2[:, :],
                                                   start=True, stop=True), r=[ab_.b, t2.b], **({"w": [bk.b]} if m == 0 else {"pw": [bk.b]}))
                    yield
                    S.op("dve", lambda: V.tensor_copy(out=y_[:].rearrange("p m k -> p (m k)"), in_=bk[:, 0:256]), r=[bk.b], w=[y_.b])
                    yield
                    b2 = gbank()
                    i4 = 0
                    for m in range(2):
                        for part in range(2):
                            S.op("pe", lambda: P.matmul(out=b2[0:64, 0:256], lhsT=y_[:, m, part * 64:(part + 1) * 64], rhs=w12[:, m, part, :],
                                                       start=(i4 == 0), stop=(i4 == 3)), r=[y_.b, w12.b],
                                 **({"w": [b2.b]} if i4 == 0 else {"pw": [b2.b]}))
                            i4 += 1
                    yield
                    S.op("dve", lambda: V.tensor_copy(out=ob_[:, j, :], in_=b2[0:64, 0:256]), r=[b2.b], pw=[ob_.b])
                    yield
                S.dma(mxv[:, p0:p0 + 8, :], ob_[:], r=[ob_.b], pw=[MXF.b], sem="f_o%d" % (pgp % 2))
                seq = p0 // 64
                k10 = p0 % 64
                fpv = FP[seq * 4096:(seq + 1) * 4096, :].rearrange("(k q) c -> k q c", q=64)
                S.dma(fpv[:, k10:k10 + 8, :], ob_[:], r=[ob_.b], pw=[FP.b], sem="f_p%d" % (pgp % 2))

    def side_chain(l):
        yield from phaseFFT(l, True)
        S.barrier()
        yield from phaseGLA(l, True)

    for l in range(2):
        xsrc = x_in if l == 0 else Y1
        ydst = Y1 if l == 0 else y_out
        S.barrier()
        with ExitStack() as esW:
            win_t = Tl(esW.enter_context(nc.sbuf_tensor("win_L%d" % l, [128, 8, DIN], BF16)), "win")
            with ExitStack() as esS:
                ws = [Tl(esS.enter_context(nc.sbuf_tensor("wst%d_L%d" % (i, l), [128, DIN], F32)), "wst%d" % i) for i in range(2)]

                def w_chunk(kc):
                    w_ = ws[kc % 2]
                    S.dma(w_[:, :], win_in[l, kc * 128:(kc + 1) * 128, :], r=[win_in.b], w=[w_.b], sem="ws%d" % (kc % 2), q="act")
                    S.op("dve", lambda: V.tensor_copy(out=win_t[:, kc, :], in_=w_[:, :]), r=[w_.b], pw=[win_t.b])

                phase0(l, w_chunk)
                S.barrier()
            phaseA(l, xsrc, win_t)
        for i in (2, 1, 3, 4):
            S.allgather(SD[i], RV[i], RG, "ag%d" % i)
        S.barrier()
        if "only_A" in dbg:
            break
        with ExitStack() as esA:
            main = phaseAttn(l, esA)
            side = side_chain(l)
            acc, side_done = 0.0, False
            n_main = n_side = n_tail = 0
            for _ in main:
                n_main += 1
                acc += SIDE_RATIO
                while acc >= 1.0 and not side_done:
                    acc -= 1.0
                    try:
                        next(side)
                        n_side += 1
                    except StopIteration:
                        side_done = True
                        S.side_done_at = n_main
            if not side_done:
                for _ in side:
                    n_tail += 1
            S.counts = (n_main, n_side, n_tail)
            S.barrier()
        phaseC(l, xsrc, ydst)
    outs = [y_out, Y1, MX, ZS, QT, FP, OF, GO, MXF] + RV
    S.finish(outs)
    return nc, S


def _host_layout(inputs):
    f = np.float32
    xp = np.asarray(inputs["x_prompt"], f)
    xs = np.asarray(inputs["x_sample"], f)
    cp = np.asarray(inputs["c_prompt"], f)
    csm = np.asarray(inputs["c_sample"], f)
    blocks_x = [xp[0:2].reshape(T, D), xp[2:4].reshape(T, D), xs[0], xs[1]]
    blocks_c = [cp[0:2], cp[2:4], np.stack([csm[0], csm[0]]), np.stack([csm[1], csm[1]])]
    is_sample = [False, False, True, True]

    def fm(v, n):
        return np.ascontiguousarray(np.asarray(v, f).reshape(n, 128).T)

    def rep(v, p=128):
        return np.ascontiguousarray(np.broadcast_to(np.asarray(v, f)[None, :], (p, v.shape[0])))

    prm = np.zeros((2, 128, NPRM), f)
    bcm = np.zeros((2, 128, NBC), f)
    wg2 = np.zeros((2, 16, 256), f)
    sgu_wT = np.zeros((2, 128, 512), f)
    for l in range(2):
        prm[l, :, P_ADAB:P_ADAB + 24] = fm(inputs["ada_b"][l], 24)
        prm[l, :, P_PREG:P_PREG + 8] = fm(inputs["norm_pre_g"][l], 8)
        prm[l, :, P_BGF] = np.asarray(inputs["gla_bg_f"][l], f)
        prm[l, :, P_BGB] = np.asarray(inputs["gla_bg_b"][l], f)
        prm[l, :, P_SGUB:P_SGUB + 4] = np.asarray(inputs["sgu_b"][l], f).T
        bcm[l, :, BC_ADABG:BC_ADABG + 1024] = rep(np.asarray(inputs["ada_b"][l], f)[2048:3072])
        bcm[l, :, BC_POSTG:BC_POSTG + 1024] = rep(np.asarray(inputs["norm_post_g"][l], f))
        qkg = np.concatenate([np.tile(np.asarray(inputs["q_norm_g"][l], f), 8), np.tile(np.asarray(inputs["k_norm_g"][l], f), 2)])
        bcm[l, :, BC_QKG:BC_QKG + 640] = rep(qkg)
        bcm[l, :, BC_SGUG:BC_SGUG + 256] = rep(np.asarray(inputs["sgu_norm_g"][l], f))
        bcm[l, :, BC_ONG:BC_ONG + 256] = rep(np.tile(np.asarray(inputs["gla_onorm_g"][l], f), 4))
        wg2[l, :, 0:128] = np.asarray(inputs["gla_wg2_f"][l], f)
        wg2[l, :, 128:256] = np.asarray(inputs["gla_wg2_b"][l], f)
        sgu_wT[l] = np.asarray(inputs["sgu_w"][l], f).transpose(2, 0, 1).reshape(128, 512)

    def rope_tab(n):
        t = np.arange(n)
        row = (t // 64).astype(np.float64)
        col = (t % 64).astype(np.float64)
        freqs = 10000.0 ** (-np.arange(0, 32, 2, dtype=np.float64) / 32)
        ang = np.concatenate([row[:, None] * freqs, col[:, None] * freqs], axis=-1)
        return np.concatenate([np.cos(ang), np.sin(ang)], axis=-1).astype(f)

    rope_s = rope_tab(8192)
    rope_p = np.concatenate([rope_tab(4096), rope_tab(4096)], axis=0)
    gconst = np.zeros((128, GC_N), f)
    s_i = np.arange(64)[:, None]
    t_i = np.arange(64)[None, :]
    gconst[0:64, GC_MF:GC_MF + 64] = (s_i <= t_i)
    gconst[0:64, GC_MB:GC_MB + 64] = (s_i > t_i)
    for h in range(4):
        gconst[h * 32:(h + 1) * 32, GC_HM + h] = 32.0 ** -0.5
        gconst[h * 32:(h + 1) * 32, GC_BD + h * 64:GC_BD + (h + 1) * 64] = 1.0
    sm = np.ones(2048, f)
    sm[0::64] = 0.0
    gconst[:, GC_SM:GC_SM + 2048] = sm[None, :]

    a64 = np.arange(64, dtype=np.float64)
    th64 = 2 * np.pi * a64[:, None] * a64[None, :] / 64.0
    Cbd = np.kron(np.eye(4), np.cos(th64))
    Sbd = np.kron(np.eye(4), np.sin(th64))
    cs64 = np.concatenate([Cbd, Sbd], axis=1).reshape(2, 128, 512).transpose(1, 0, 2).astype(f)
    cs64 = np.ascontiguousarray(cs64)
    shared = {
        "ada_w": np.ascontiguousarray(np.asarray(inputs["ada_w"], f)),
        "w_in": np.ascontiguousarray(np.asarray(inputs["w_in"], f)),
        "w_out": np.ascontiguousarray(np.asarray(inputs["w_out"], f)),
        "fnet_w": np.ascontiguousarray(np.asarray(inputs["fnet_w"], f)),
        "sgu_wT": sgu_wT, "wg2": wg2, "prm": prm, "bc": bcm, "gconst": gconst,
        "cs64": cs64,
    }
    rope_p1 = rope_tab(4096)
    tabs = {}
    for smp in (False, True):
        n2 = np.arange(64, dtype=np.float64)[:, None, None]
        n1 = np.arange(128, dtype=np.float64)[None, :, None]
        pp = np.arange(128, dtype=np.float64)[None, None, :]
        if smp:
            th = 2 * np.pi * (n2 * pp / 8192.0 + n1 * pp / 128.0)
            mre, mim = np.cos(th), -np.sin(th)
            nseq = 8192.0
        else:
            th = 2 * np.pi * (n2 * (pp % 64) / 4096.0 + (n1 % 64) * (pp % 64) / 64.0)
            same = ((n1 // 64) == (pp // 64)).astype(np.float64)
            mre, mim = np.cos(th) * same, -np.sin(th) * same
            nseq = 4096.0
        t1 = np.concatenate([mre, mim], axis=2).astype(ml_dtypes.bfloat16)
        a_ = np.arange(64, dtype=np.float64)
        th2 = 2 * np.pi * a_[:, None] * a_[None, :] / 64.0
        sc = 1.0 / np.sqrt(64.0 * nseq)
        C2, S2 = np.cos(th2) * sc, np.sin(th2) * sc
        t2 = np.block([[C2, -S2], [S2, C2]]).astype(ml_dtypes.bfloat16)
        tabs[smp] = (t1, t2)
    maps = []
    for r in range(8):
        b, k = r % 4, r // 4
        smp = is_sample[b]
        m = dict(shared)
        m["x"] = np.ascontiguousarray(blocks_x[b][k * TO:(k + 1) * TO])
        c1 = blocks_c[b][k]
        c2 = np.stack([c1, c1])
        m["cT"] = np.ascontiguousarray(c2.reshape(2, 8, 128).transpose(2, 1, 0).reshape(128, 16))
        m["rope"] = np.ascontiguousarray(rope_s[k * TO:(k + 1) * TO]) if smp else rope_p1
        fl = np.zeros((128, 16), f)
        own = [1.0 if k == 0 else 0.0, 1.0 if k == 1 else 0.0]
        fS, fP = (1.0, 0.0) if smp else (0.0, 1.0)
        fl[:, 0], fl[:, 1], fl[:, 2], fl[:, 3] = fS * own[0], fS * own[1], fP * own[0], fP * own[1]
        fl[:, 4], fl[:, 5] = own[0], own[1]
        fl[:, 6] = 1.0 if smp else 0.0
        fl[:, 7] = 1.0 if (smp or k == 0) else 0.0
        fl[:, 8] = 1.0 if (smp or k == 1) else 0.0
        m["flags"] = fl
        m["tab1"], m["tab2"] = tabs[smp]
        maps.append(m)
    return maps


_CACHE = {}


def kernel(**inputs):
    maps = _host_layout(inputs)
    if "nc" not in _CACHE:
        _CACHE["nc"] = build()[0]
    nc = _CACHE["nc"]
    res = run_bass_kernel_spmd(nc, maps, core_ids=list(range(8)))
    ys = [np.asarray(res.results[i]["y"], np.float32) for i in range(8)]
    y_prompt = np.stack([ys[0], ys[4], ys[1], ys[5]], axis=0)
    y_sample = np.stack([np.concatenate([ys[2], ys[6]], 0), np.concatenate([ys[3], ys[7]], 0)], axis=0)
    return (y_prompt, y_sample)
```

```python
from contextlib import ExitStack
import numpy as np
import ml_dtypes
import concourse.bass as bass
import concourse.mybir as mybir
from concourse.bass_utils import run_bass_kernel_spmd

F32 = mybir.dt.float32
BF16 = mybir.dt.bfloat16
AF = mybir.ActivationFunctionType
ALU = mybir.AluOpType
AX = mybir.AxisListType

T = 8192
TO = 4096
D = 1024
DIN = 3360
DMIX = 1280
NTILE = T // 128
NSUP = TO // 512
NTO = TO // 128
EPS = 1e-6
NPRM = 40
SIDE_RATIO = 1.5
NBC = 3200
BC_ADABG, BC_POSTG, BC_QKG, BC_SGUG, BC_ONG = 0, 1024, 2048, 2688, 2944
P_ADAB, P_PREG, P_BGF, P_BGB, P_SGUB = 0, 24, 32, 33, 34
GC_MF, GC_MB, GC_HM, GC_BD, GC_SM, GC_N = 0, 64, 128, 132, 388, 388 + 2048


class Buf:
    __slots__ = ("name", "w", "r")

    def __init__(self, name):
        self.name = name
        self.w = {}
        self.r = {}


class Tl:
    def __init__(self, t, name):
        self.t = t
        self.b = Buf(name)

    def __getitem__(self, k):
        return self.t[k]


class Sched:
    EPOCH = 30000

    def __init__(self, nc):
        self.nc = nc
        self.E = {"pe": nc.tensor, "act": nc.scalar, "dve": nc.vector, "pool": nc.gpsimd, "sp": nc.sync}
        self.esem = {}
        for e in ("pe", "act", "dve", "pool"):
            self.esem[e] = [nc.alloc_semaphore("es_%s_0" % e), 0, 0]
        self.seen = {e: {} for e in self.E}
        self.dsem = {}
        self.free_ds = []
        self.psum_names = set()
        self.watch = None
        self.pending = False
        self.csem = {}
        self.nds = 0
        self.nins = 0

    def sb(self, name, shape, dt):
        return Tl(self.nc.alloc_sbuf_tensor(name, list(shape), dt), name)

    def ps(self, name, shape, dt):
        return Tl(self.nc.alloc_psum_tensor(name, list(shape), dt), name)

    def _wait(self, e, need):
        for k, (h, v) in need.items():
            if self.seen[e].get(k, 0) < v:
                self.E[e].wait_ge(h, v)
                self.seen[e][k] = v

    @staticmethod
    def _add(need, k, hv):
        if k not in need or need[k][1] < hv[1]:
            need[k] = hv

    def op(self, e, fn, r=(), w=(), pw=()):
        need = {}
        for b in r:
            for k, hv in b.w.items():
                self._add(need, k, hv)
            if b.name in self.psum_names:
                for k, hv in b.r.items():
                    if k[0] != e:
                        self._add(need, k, hv)
        for b in list(w) + list(pw):
            for k, hv in b.r.items():
                if k[0] != e:
                    self._add(need, k, hv)
            for k, hv in b.w.items():
                if k[0] != e:
                    self._add(need, k, hv)
        self._wait(e, need)
        ins = fn()
        if self.watch is not None:
            if e == "pe" and any(b is self.watch for b in list(w) + list(pw)):
                self.pending = True
            elif e != "pe" and any(b is self.watch for b in r):
                self.pending = False
        es = self.esem[e]
        if es[1] >= self.EPOCH:
            es[2] += 1
            es[0] = self.nc.alloc_semaphore("es_%s_%d" % (e, es[2]))
            es[1] = 0
        es[1] += 1
        ins.then_inc(es[0], 1)
        key = (e, es[2])
        hv = (es[0], es[1])
        for b in r:
            b.r[key] = hv
        for b in w:
            b.w = {key: hv}
            b.r = {}
        for b in pw:
            b.w[key] = hv
        self.nins += 1
        return ins

    def dma(self, out, in_, r=(), w=(), pw=(), sem="d", q="sp"):
        if sem not in self.dsem:
            if self.free_ds:
                self.dsem[sem] = self.free_ds.pop()
            else:
                self.nds += 1
                self.dsem[sem] = [self.nc.alloc_semaphore("ds_%d" % self.nds), 0, self.nds]
        ds = self.dsem[sem]
        key = ("dma", ds[2])
        need = {}
        if ds[1] > 0:
            need[key] = (ds[0], ds[1])
        for b in r:
            for k, hv in b.w.items():
                self._add(need, k, hv)
        for b in w:
            for k, hv in b.r.items():
                self._add(need, k, hv)
            for k, hv in b.w.items():
                self._add(need, k, hv)
        for b in pw:
            for k, hv in b.r.items():
                self._add(need, k, hv)
            for k, hv in b.w.items():
                if k[0] != "dma":
                    self._add(need, k, hv)
        self._wait(q, need)
        ins = self.E[q].dma_start(out=out, in_=in_)
        ds[1] += 16
        ins.then_inc(ds[0], 16)
        hv = (ds[0], ds[1])
        for b in r:
            b.r[key] = hv
        for b in w:
            b.w = {key: hv}
            b.r = {}
        for b in pw:
            b.w[key] = hv
        self.nins += 1
        return ins

    def allgather(self, src, dst, rg, name):
        if name not in self.csem:
            self.csem[name] = [self.nc.alloc_semaphore("cs_" + name), 0]
        ds = self.csem[name]
        key = ("cc", name)
        need = {}
        if ds[1] > 0:
            need[key] = (ds[0], ds[1])
        for k, hv in src.b.w.items():
            self._add(need, k, hv)
        for k, hv in list(dst.b.r.items()) + list(dst.b.w.items()):
            self._add(need, k, hv)
        self._wait("pool", need)
        ins = self.nc.gpsimd.collective_compute("AllGather", ALU.bypass, replica_groups=rg,
                                                ins=[src.h.ap().opt()], outs=[dst.h.ap().opt()])
        ds[1] += 1
        ins.then_inc(ds[0])
        hv = (ds[0], ds[1])
        src.b.r[key] = hv
        dst.b.w = {key: hv}
        dst.b.r = {}
        self.nins += 1

    def barrier(self):
        need = {}
        for e, es in self.esem.items():
            if es[1] > 0:
                need[(e, es[2])] = (es[0], es[1])
        for name, ds in self.dsem.items():
            if ds[1] > 0:
                need[("dma", ds[2])] = (ds[0], ds[1])
        for e in self.E:
            self._wait(e, need)
        self.free_ds.extend(self.dsem.values())
        self.dsem = {}

    def finish(self, bufs):
        need = {}
        for b in bufs:
            for k, hv in b.b.w.items():
                self._add(need, k, hv)
        self._wait("sp", need)


def build(dbg=()):
    nc = bass.Bass("TRN2", target_bir_lowering=False)
    S = Sched(nc)
    V, A, P, G = nc.vector, nc.scalar, nc.tensor, nc.gpsimd

    def din(name, shape, dt=F32):
        return Tl(nc.dram_tensor(name, list(shape), dt, kind="ExternalInput").ap(), name)

    def dscr(name, shape, dt):
        kind = "ExternalOutput" if name in dbg else "Internal"
        return Tl(nc.dram_tensor(name, list(shape), dt, kind=kind).ap(), name)

    x_in = din("x", [TO, D])
    cT_in = din("cT", [128, 16])
    adaw_in = din("ada_w", [2, D, 3 * D])
    win_in = din("w_in", [2, D, DIN])
    wout_in = din("w_out", [2, DMIX, D])
    fnet_in = din("fnet_w", [2, 256, 256])
    sguw_in = din("sgu_wT", [2, 128, 512])
    wg2_in = din("wg2", [2, 16, 256])
    prm_in = din("prm", [2, 128, NPRM])
    bc_in = din("bc", [2, 128, NBC])
    rope_in = din("rope", [TO, 64])
    flags_in = din("flags", [128, 16])
    gconst_in = din("gconst", [128, GC_N])
    tab1_in = din("tab1", [64, 128, 256], BF16)
    tab2_in = din("tab2", [128, 128], BF16)
    cs64_in = din("cs64", [128, 2, 512])
    y_out = Tl(nc.dram_tensor("y", [TO, D], F32, kind="ExternalOutput").ap(), "y")

    SROWS = [2048, 2048, 2048, 2048, 256]
    SD, RV = [], []
    for i, rws in enumerate(SROWS):
        h = nc.dram_tensor("SD%d" % i, [rws, 512], BF16)
        t_ = Tl(h.ap(), "SD%d" % i)
        t_.h = h
        SD.append(t_)
        h = nc.dram_tensor("RV%d" % i, [2 * rws, 512], BF16)
        t_ = Tl(h.ap(), "RV%d" % i)
        t_.h = h
        RV.append(t_)
    RG = [[0, 4], [1, 5], [2, 6], [3, 7]]

    def xviews(B, j):
        o = [j * r for r in SROWS]
        v = {}
        v["vv"] = B[2][o[2]:o[2] + 1024, :].rearrange("r (a c) -> (r a) c", a=4)
        v["kt"] = B[2][o[2] + 1024:o[2] + 2048, :].rearrange("(d g tt) c -> d g (tt c)", d=64, g=2, tt=8)
        v["gqt"] = B[3][o[3]:o[3] + 1024, :].rearrange("(f tt) c -> f (tt c)", tt=8)
        v["gkt"] = B[3][o[3] + 1024:o[3] + 2048, :].rearrange("(f tt) c -> f (tt c)", tt=8)
        v["lrf"] = B[4][o[4]:o[4] + 128, :].rearrange("(f tt) c -> f (tt c)", tt=8)
        v["lrb"] = B[4][o[4] + 128:o[4] + 256, :].rearrange("(f tt) c -> f (tt c)", tt=8)
        return v

    def avu_rows(B, j, t0, n):
        i = 0 if t0 < 2048 else 1
        r0 = j * 2048 + (t0 % 2048)
        return B[i], B[i][r0:r0 + n, :]

    SV = xviews(SD, 0)
    RVV = [xviews(RV, 0), xviews(RV, 1)]
    QT = dscr("QT", [64, 8, TO], BF16)
    ZS = dscr("ZS", [TO, DMIX], BF16)
    MX = dscr("MX", [TO, DMIX], BF16)
    GO = dscr("GO", [T, 256], BF16)
    MXF = dscr("MXF", [T, 256], BF16)
    FP = dscr("FP", [T, 256], BF16)
    OF = dscr("OF", [T, 256], F32)
    G1 = dscr("G1", [128, 2, 64, 256], BF16)
    Y1 = dscr("Y1", [TO, D], F32)

    ident = S.sb("ident", [128, 128], BF16)
    identf = S.sb("identf", [128, 128], F32)
    flags = S.sb("flags_sb", [128, 16], F32)
    prm = S.sb("prm_sb", [128, NPRM], F32)
    bc = S.sb("bc_sb", [128, NBC], F32)
    gmod = S.sb("gmod", [128, 32], F32)
    pgt = [S.sb("pgt%d" % h, [128, D], F32) for h in range(2)]
    stat = S.sb("stat", [128, 64], F32)

    PD = [nc.alloc_psum_tensor("PD%d" % i, [128, 1024], F32) for i in range(3)]
    PS_ = [nc.alloc_psum_tensor("PS%d" % i, [128, 512], F32) for i in range(2)]
    pb = [Tl(PD[0][:, 0:512], "pb0"), Tl(PD[0][:, 512:1024], "pb1"), Tl(PD[1][:, 0:512], "pb2"),
          Tl(PD[1][:, 512:1024], "pb3"), Tl(PD[2][:, 0:512], "pb4")]
    b16 = lambda t_: t_.bitcast(BF16).rearrange("p (c t) -> p c t", t=128)
    ptr = Tl(b16(PD[2][:, 512:1024]), "ptr")
    ptq = Tl(b16(PS_[0][:, :]), "ptq")
    ptk = Tl(b16(PS_[1][:, :]), "ptk")
    sideT = Tl(PS_[1][:, :], "sideT")
    sideT.b = ptk.b
    S.psum_names = {t_.b.name for t_ in pb + [ptr, ptq, ptk]}
    S.watch = ptk.b
    pbi = [0]

    def bank():
        pbi[0] += 1
        return pb[pbi[0] % 5]

    S.op("pool", lambda: G.memset(ident[:], 1.0), w=[ident.b])
    S.op("pool", lambda: G.affine_select(out=ident[:], in_=ident[:], pattern=[[-1, 128]], compare_op=ALU.is_equal,
                                        fill=0.0, base=0, channel_multiplier=1), r=[ident.b], w=[ident.b])
    S.op("pool", lambda: G.memset(identf[:], 1.0), w=[identf.b])
    S.op("pool", lambda: G.affine_select(out=identf[:], in_=identf[:], pattern=[[-1, 128]], compare_op=ALU.is_equal,
                                        fill=0.0, base=0, channel_multiplier=1), r=[identf.b], w=[identf.b])
    S.dma(flags[:], flags_in[:, :], r=[flags_in.b], w=[flags.b], sem="misc")

    mhalf = S.sb("mhalf", [128, 64], F32)
    S.op("pool", lambda: G.memset(mhalf[:], -0.5), w=[mhalf.b])

    def rstd_chain(src, dst, n, inv_n):
        S.op("dve", lambda: V.tensor_scalar(out=stat[:, dst:dst + n], in0=stat[:, src:src + n], scalar1=inv_n,
                                            scalar2=EPS, op0=ALU.mult, op1=ALU.add), r=[stat.b], w=[stat.b])
        S.op("pool", lambda: G.tensor_tensor(out=stat[:, dst:dst + n], in0=stat[:, dst:dst + n], in1=mhalf[:, 0:n], op=ALU.pow),
             r=[stat.b, mhalf.b], w=[stat.b])

    def phase0(l, kc_hook=None):
        with ExitStack() as es:
            al = lambda n, s, d: Tl(es.enter_context(nc.sbuf_tensor(n + "_L%d" % l, list(s), d)), n)
            cs = al("cs_sb", [128, 16], F32)
            csrep = al("csrep", [128, 8, 2, 128], F32)
            ad = [al("adst0", [128, 3 * D], F32), al("adst1", [128, 3 * D], F32)]
            S.dma(prm[:], prm_in[l], r=[prm_in.b], w=[prm.b], sem="misc")
            S.dma(bc[:], bc_in[l], r=[bc_in.b], w=[bc.b], sem="misc")
            S.dma(cs[:], cT_in[:, :], r=[cT_in.b], w=[cs.b], sem="misc")
            S.op("act", lambda: A.activation(out=cs[:], in_=cs[:], func=AF.Silu), r=[cs.b], w=[cs.b])
            csv = cs[:].rearrange("p (k h) -> p k h", h=2)
            S.op("dve", lambda: V.tensor_copy(out=csrep[:], in_=csv.unsqueeze(3).to_broadcast([128, 8, 2, 128])),
                 r=[cs.b], w=[csrep.b])
            pm = pb[0]
            pg = [pb[1], pb[2], pb[3], pb[4]]
            S.op("dve", lambda: V.memset(pm[:], 0.0), w=[pm.b])
            for kc in range(8):
                if kc_hook is not None:
                    kc_hook(kc)
                a = ad[kc % 2]
                S.dma(a[:], adaw_in[l, kc * 128:(kc + 1) * 128, :], r=[adaw_in.b], w=[a.b], sem="ad%d" % (kc % 2))
                for j in range(16):
                    S.op("pe", lambda: P.matmul(out=pm[:, j * 2:(j + 1) * 2], lhsT=a[:, j * 128:(j + 1) * 128],
                                               rhs=cs[:, kc * 2:(kc + 1) * 2], start=False, stop=(kc == 7),
                                               skip_group_check=True), r=[a.b, cs.b], pw=[pm.b])
                for h in range(2):
                    for nb in range(2):
                        S.op("pe", lambda: P.matmul(out=pg[h * 2 + nb][:, :], lhsT=csrep[:, kc, h, :],
                                                   rhs=a[:, 2048 + nb * 512:2048 + (nb + 1) * 512],
                                                   start=(kc == 0), stop=(kc == 7)),
                             r=[a.b, csrep.b], **({"w": [pg[h * 2 + nb].b]} if kc == 0 else {"pw": [pg[h * 2 + nb].b]}))
            pmv = pm[:, 0:32].rearrange("p (j h) -> p j h", h=2)
            for h in range(2):
                S.op("dve", lambda: V.tensor_tensor(out=gmod[:, h * 16 + 8:h * 16 + 16], in0=pmv[:, 0:8, h],
                                                    in1=prm[:, P_ADAB:P_ADAB + 8], op=ALU.add),
                     r=[pm.b, prm.b], pw=[gmod.b])
                S.op("dve", lambda: V.tensor_tensor(out=gmod[:, h * 16:h * 16 + 8], in0=pmv[:, 8:16, h],
                                                    in1=prm[:, P_ADAB + 8:P_ADAB + 16], op=ALU.add),
                     r=[pm.b, prm.b], pw=[gmod.b])
                S.op("dve", lambda: V.scalar_tensor_tensor(out=gmod[:, h * 16:h * 16 + 8], in0=gmod[:, h * 16:h * 16 + 8],
                                                           scalar=1.0, in1=prm[:, P_PREG:P_PREG + 8],
                                                           op0=ALU.add, op1=ALU.mult),
                     r=[gmod.b, prm.b], pw=[gmod.b])
                for nb in range(2):
                    sl = slice(nb * 512, (nb + 1) * 512)
                    S.op("dve", lambda: V.tensor_tensor(out=pgt[h][:, sl], in0=pg[h * 2 + nb][:, :],
                                                        in1=bc[:, BC_ADABG + nb * 512:BC_ADABG + (nb + 1) * 512],
                                                        op=ALU.add), r=[pg[h * 2 + nb].b, bc.b], pw=[pgt[h].b])
                    S.op("pool", lambda: G.tensor_tensor(out=pgt[h][:, sl], in0=pgt[h][:, sl],
                                                         in1=bc[:, BC_POSTG + nb * 512:BC_POSTG + (nb + 1) * 512],
                                                         op=ALU.mult), r=[pgt[h].b, bc.b], pw=[pgt[h].b])

    def phaseA(l, xsrc, win):
        with ExitStack() as es:
            al = lambda n, s, d: Tl(es.enter_context(nc.sbuf_tensor(n + "_L%d" % l, list(s), d)), n)
            xa = [al("xa0", [128, 4, D], F32), al("xa1", [128, 4, D], F32)]
            junk = al("junk", [128, D], F32)
            xb = [al("xb%d" % i, [128, D], BF16) for i in range(4)]
            hT = [al("hT0", [128, 8, 512], BF16), al("hT1", [128, 8, 512], BF16)]
            fm = [al("fm%d" % i, [128, 512], BF16) for i in range(4)]
            avu = [al("avu0", [128, 512], BF16), al("avu1", [128, 512], BF16)]
            qk = [al("qk0", [128, 640], F32), al("qk1", [128, 640], F32)]
            rt = [al("rt%d" % i, [128, 320], F32) for i in range(4)]
            qkr = [al("qkr0", [128, 640], BF16), al("qkr1", [128, 640], BF16)]
            qkT = [al("qkT0", [64, 10, 128], BF16), al("qkT1", [64, 10, 128], BF16)]
            vt = [al("vt0", [128, 128], BF16), al("vt1", [128, 128], BF16)]
            vn = [al("vn0", [128, 256], BF16), al("vn1", [128, 256], BF16)]
            sgt = al("sgt", [128, 256], F32)
            od = [al("od0", [128, 256], BF16), al("od1", [128, 256], BF16)]
            zs = [al("zs0", [128, DMIX], BF16), al("zs1", [128, DMIX], BF16)]
            wsT = al("wsT", [128, 4, 128], BF16)
            wsTf = al("wsTf", [128, 512], F32)
            S.dma(wsTf[:], sguw_in[l], r=[sguw_in.b], w=[wsTf.b], sem="misc")
            S.op("dve", lambda: V.tensor_copy(out=wsT[:].rearrange("p g t -> p (g t)"), in_=wsTf[:]), r=[wsTf.b], w=[wsT.b])

            stP = al("stP", [128, 8], F32)
            stQ = [al("stQ0", [128, 32], F32), al("stQ1", [128, 32], F32)]
            stS = [al("stS0", [128, 4], F32), al("stS1", [128, 4], F32)]
            junkQ = [al("junkQ0", [128, 640], F32), al("junkQ1", [128, 640], F32)]
            junkS = al("junkS", [128, 256], F32)
            cr = [al("cr0", [128, 512], F32), al("cr1", [128, 512], F32)]
            dvr = [al("dvr0", [128, 256], F32), al("dvr1", [128, 256], F32)]
            rp4 = [al("rp4%d" % i, [128, 4, 64], F32) for i in range(4)]
            gG = gmod[:, 0:8].unsqueeze(2).to_broadcast([128, 8, 128])
            gS = gmod[:, 8:16].unsqueeze(2).to_broadcast([128, 8, 128])

            def rstd2(t_, src_c, dst_c, n, inv_n):
                S.op("dve", lambda: V.tensor_scalar(out=t_[:, dst_c:dst_c + n], in0=t_[:, src_c:src_c + n], scalar1=inv_n,
                                                    scalar2=EPS, op0=ALU.mult, op1=ALU.add), r=[t_.b], w=[t_.b])
                S.op("pool", lambda: G.tensor_tensor(out=t_[:, dst_c:dst_c + n], in0=t_[:, dst_c:dst_c + n], in1=mhalf[:, 0:n],
                                                     op=ALU.pow), r=[t_.b, mhalf.b], w=[t_.b])

            def load_x(st):
                S.dma(xa[st % 2][:], xsrc[st * 512:(st + 1) * 512, :].rearrange("(a p) d -> p a d", p=128),
                      r=[xsrc.b], w=[xa[st % 2].b], sem="xa%d" % (st % 2))
                if "A_norope" not in dbg:
                    S.dma(rp4[st % 4][:], rope_in[st * 512:(st + 1) * 512, :].rearrange("(a p) c -> p a c", p=128),
                          r=[rope_in.b], w=[rp4[st % 4].b], sem="rp%d" % (st % 4))

            def prepA(st):
                xt = xa[st % 2]
                for a in range(4):
                    S.op("act", lambda: A.activation(out=junk[:], in_=xt[:, a, :], func=AF.Square,
                                                     accum_out=stP[:, a:a + 1]), r=[xt.b], w=[junk.b], pw=[stP.b])
                rstd2(stP, 0, 4, 4, 1.0 / D)
                for a in range(4):
                    xb_ = xb[a]
                    S.op("dve", lambda: V.tensor_scalar(out=xb_[:], in0=xt[:, a, :], scalar1=stP[:, 4 + a:5 + a],
                                                        scalar2=None, op0=ALU.mult), r=[xt.b, stP.b], w=[xb_.b])

            def prepB(st):
                h_ = hT[st % 2]
                for a in range(4):
                    xb_ = xb[a]
                    for c in range(8):
                        S.op("pe", lambda: P.transpose(out=ptr[:, c, :], in_=xb_[:, c * 128:(c + 1) * 128],
                                                      identity=ident[:]), r=[xb_.b, ident.b],
                             **({"w": [ptr.b]} if c == 0 else {"pw": [ptr.b]}))
                    S.op("dve", lambda: V.tensor_tensor(out=h_[:, :, a * 128:(a + 1) * 128], in0=ptr[:], in1=gG,
                                                        op=ALU.mult), r=[ptr.b, gmod.b], pw=[h_.b])
                    S.op("pool", lambda: G.tensor_tensor(out=h_[:, :, a * 128:(a + 1) * 128],
                                                         in0=h_[:, :, a * 128:(a + 1) * 128], in1=gS, op=ALU.add),
                         r=[h_.b, gmod.b], pw=[h_.b])

            def fm_groups(ti):
                st, a = divmod(ti, 4)
                h_ = hT[st % 2]
                tsl = slice(st * 512, (st + 1) * 512)
                for gi, (c0, m, dst, dstT) in enumerate(((0, 128, SV["gqt"], SD[3]), (128, 128, SV["gkt"], SD[3]),
                                                         (512, 16, SV["lrf"], SD[4]), (528, 16, SV["lrb"], SD[4]))):
                    bk = pb[1 + gi]
                    for kc in range(8):
                        S.op("pe", lambda: P.matmul(out=bk[0:m, :], lhsT=win[:, kc, c0:c0 + m], rhs=h_[:, kc, :],
                                                   start=(kc == 0), stop=(kc == 7)), r=[win.b, h_.b],
                             **({"w": [bk.b]} if kc == 0 else {"pw": [bk.b]}))
                    f_ = fm[gi]
                    S.op("act", lambda: A.copy(out=f_[0:m, :], in_=bk[0:m, :]), r=[bk.b], w=[f_.b])
                    S.dma(dst[0:m, tsl], f_[0:m, :], r=[f_.b], pw=[dstT.b], sem="fm%d" % gi)

            def mmgroup(ti, bk, col_lo, col_hi, out_lo, first):
                st, a = divmod(ti, 4)
                h_ = hT[st % 2]
                for kc in range(8):
                    S.op("pe", lambda: P.matmul(out=bk[:, out_lo:out_lo + (col_hi - col_lo)], lhsT=h_[:, kc, a * 128:(a + 1) * 128],
                                               rhs=win[:, kc, col_lo:col_hi], start=(kc == 0), stop=(kc == 7)),
                         r=[win.b, h_.b], **({"w": [bk.b]} if (kc == 0 and first) else {"pw": [bk.b]}))

            def stageXM(ti):
                mmgroup(ti, pb[0], 256, 512, 0, True)
                mmgroup(ti, pb[0], 544, 800, 256, False)
                mmgroup(ti, pb[1], 800, 1312, 0, True)
                mmgroup(ti, pb[2], 1312, 1824, 0, True)
                mmgroup(ti, pb[3], 1824, 2336, 0, True)
                mmgroup(ti, pb[4], 2336, 2848, 0, True)

            def stageXE(ti):
                st, a = divmod(ti, 4)
                rows = slice(ti * 128, (ti + 1) * 128)
                bA, bB, bC, bD, bE = pb
                av_ = avu[ti % 2]
                S.op("act", lambda: A.copy(out=av_[:], in_=bA[:, :]), r=[bA.b], w=[av_.b])
                avT, avv = avu_rows(SD, 0, ti * 128, 128)
                S.dma(avv, av_[:], r=[av_.b], pw=[avT.b], sem="avu%d" % (ti % 2))
                qk_ = qk[ti % 2]
                S.op("dve", lambda: V.tensor_copy(out=qk_[:, 0:512], in_=bB[:, :]), r=[bB.b], w=[qk_.b])
                S.op("dve", lambda: V.tensor_copy(out=qk_[:, 512:640], in_=bC[:, 0:128]), r=[bC.b], pw=[qk_.b])
                cr_ = cr[ti % 2]
                S.op("dve", lambda: V.tensor_copy(out=cr_[:, 128:512], in_=bC[:, 128:512]), r=[bC.b], w=[cr_.b])
                vt_ = vt[ti % 2]
                S.op("pool", lambda: G.tensor_copy(out=vt_[:], in_=cr_[:, 128:256]), r=[cr_.b], w=[vt_.b])
                S.dma(SV["vv"][rows, :], vt_[:], r=[vt_.b], pw=[SD[2].b], sem="vt%d" % (ti % 2))
                dv_ = dvr[ti % 2]
                S.op("act", lambda: A.copy(out=dv_[:], in_=bD[:, 0:256]), r=[bD.b], w=[dv_.b])
                zs_ = zs[ti % 2]
                S.op("act", lambda: A.activation(out=zs_[:, 0:256], in_=bD[:, 256:512], func=AF.Silu), r=[bD.b], w=[zs_.b])
                S.op("act", lambda: A.activation(out=zs_[:, 256:768], in_=bE[:, :], func=AF.Silu), r=[bE.b], pw=[zs_.b])
                mmgroup(ti, pb[0], 2848, 3360, 0, True)
                S.op("act", lambda: A.activation(out=zs_[:, 768:1280], in_=pb[0][:, :], func=AF.Silu), r=[pb[0].b], pw=[zs_.b])
                S.dma(ZS[rows, :], zs_[:], r=[zs_.b], pw=[ZS.b], sem="zs%d" % (ti % 2))
                if a == 0:
                    fm_groups(ti)

            def stageY(ti):
                st, a = divmod(ti, 4)
                tsl = slice(st * 512, (st + 1) * 512)
                rows = slice(ti * 128, (ti + 1) * 128)
                cr_ = cr[ti % 2]
                dv_ = dvr[ti % 2]
                ss_ = stS[ti % 2]
                vn_ = vn[ti % 2]
                S.op("act", lambda: A.activation(out=junkS[:], in_=dv_[:], func=AF.Square, accum_out=ss_[:, 0:1]),
                     r=[dv_.b], w=[junkS.b], pw=[ss_.b])
                rstd2(ss_, 0, 1, 1, 1.0 / 256)
                S.op("dve", lambda: V.scalar_tensor_tensor(out=vn_[:], in0=dv_[:], scalar=ss_[:, 1:2],
                                                           in1=bc[:, BC_SGUG:BC_SGUG + 256], op0=ALU.mult, op1=ALU.mult),
                     r=[dv_.b, ss_.b, bc.b], w=[vn_.b])
                qT_ = qkT[ti % 2]
                qk_ = qk[ti % 2]
                sq_ = stQ[ti % 2]
                jq_ = junkQ[ti % 2]
                S.op("act", lambda: A.activation(out=jq_[:], in_=qk_[:], func=AF.Square), r=[qk_.b], w=[jq_.b])
                S.op("dve", lambda: V.tensor_reduce(out=sq_[:, 0:10], in_=jq_[:].rearrange("p (h d) -> p h d", d=64),
                                                    op=ALU.add, axis=AX.X), r=[jq_.b], w=[sq_.b])
                rstd2(sq_, 0, 10, 10, 1.0 / 64)
                S.op("dve", lambda: V.tensor_tensor(out=qk_[:].rearrange("p (h d) -> p h d", d=64),
                                                    in0=qk_[:].rearrange("p (h d) -> p h d", d=64),
                                                    in1=sq_[:, 10:20].unsqueeze(2).to_broadcast([128, 10, 64]),
                                                    op=ALU.mult), r=[qk_.b, sq_.b], w=[qk_.b])
                S.op("dve", lambda: V.tensor_tensor(out=qk_[:], in0=qk_[:], in1=bc[:, BC_QKG:BC_QKG + 640],
                                                    op=ALU.mult), r=[qk_.b, bc.b], w=[qk_.b])
                rp_ = rp4[st % 4]
                qv = qk_[:].rearrange("p (h i two) -> p h i two", h=10, two=2)
                x0, x1 = qv[:, :, :, 0], qv[:, :, :, 1]
                cosb = rp_[:, a, 0:32].unsqueeze(1).to_broadcast([128, 10, 32])
                sinb = rp_[:, a, 32:64].unsqueeze(1).to_broadcast([128, 10, 32])
                qr_ = qkr[ti % 2]
                ov = qr_[:].rearrange("p (h i two) -> p h i two", h=10, two=2)
                r3 = lambda t_: t_[:].rearrange("p (h i) -> p h i", h=10)
                S.op("dve", lambda: V.tensor_tensor(out=r3(rt[0]), in0=x0, in1=cosb, op=ALU.mult), r=[qk_.b, rp_.b], w=[rt[0].b])
                S.op("pool", lambda: G.tensor_tensor(out=r3(rt[1]), in0=x1, in1=sinb, op=ALU.mult), r=[qk_.b, rp_.b], w=[rt[1].b])
                S.op("pool", lambda: G.tensor_tensor(out=r3(rt[2]), in0=x0, in1=sinb, op=ALU.mult), r=[qk_.b, rp_.b], w=[rt[2].b])
                S.op("dve", lambda: V.tensor_tensor(out=r3(rt[3]), in0=x1, in1=cosb, op=ALU.mult), r=[qk_.b, rp_.b], w=[rt[3].b])
                S.op("dve", lambda: V.tensor_tensor(out=ov[:, :, :, 0], in0=r3(rt[0]), in1=r3(rt[1]), op=ALU.subtract),
                     r=[rt[0].b, rt[1].b], w=[qr_.b])
                S.op("pool", lambda: G.tensor_tensor(out=ov[:, :, :, 1], in0=r3(rt[2]), in1=r3(rt[3]), op=ALU.add),
                     r=[rt[2].b, rt[3].b], pw=[qr_.b])
                for j in range(10):
                    pt_ = ptq if j < 8 else ptk
                    jj = j if j < 8 else j - 8
                    S.op("pe", lambda: P.transpose(out=pt_[0:64, jj, :], in_=qr_[:, j * 64:(j + 1) * 64], identity=ident[:]),
                         r=[qr_.b, ident.b], **({"w": [pt_.b]} if jj == 0 else {"pw": [pt_.b]}))
                S.op("act", lambda: A.copy(out=qT_[:, 0:8, :], in_=ptq[0:64, :, :]), r=[ptq.b], w=[qT_.b])
                S.op("act", lambda: A.copy(out=qT_[:, 8:10, :], in_=ptk[0:64, 0:2, :]), r=[ptk.b], pw=[qT_.b])
                S.dma(QT[:, :, rows], qT_[:, 0:8, :], r=[qT_.b], pw=[QT.b], sem="qT%d" % (ti % 2))
                S.dma(SV["kt"][:, :, rows], qT_[:, 8:10, :], r=[qT_.b], pw=[SD[2].b], sem="kT%d" % (ti % 2))
                bS = sideT
                for g in range(4):
                    S.op("pe", lambda: P.matmul(out=bS[:, g * 64:(g + 1) * 64], lhsT=wsT[:, g, :], rhs=vn_[:, g * 64:(g + 1) * 64],
                                               start=True, stop=True), r=[wsT.b, vn_.b],
                         **({"w": [bS.b]} if g == 0 else {"pw": [bS.b]}))
                S.op("dve", lambda: V.tensor_tensor(out=sgt[:].rearrange("p (g c) -> p g c", c=64),
                                                    in0=bS[:, 0:256].rearrange("p (g c) -> p g c", c=64),
                                                    in1=prm[:, P_SGUB:P_SGUB + 4].unsqueeze(2).to_broadcast([128, 4, 64]),
                                                    op=ALU.add), r=[bS.b, prm.b], w=[sgt.b])
                od_ = od[ti % 2]
                S.op("dve", lambda: V.tensor_tensor(out=od_[:], in0=sgt[:], in1=cr_[:, 256:512], op=ALU.mult),
                     r=[sgt.b, cr_.b], w=[od_.b])
                S.dma(MX[rows, 1024:1280], od_[:], r=[od_.b], pw=[MX.b], sem="od%d" % (ti % 2))

            load_x(0)
            load_x(1)
            prepA(0)
            prepB(0)
            if "A_preponly" in dbg:
                return
            for ti in range(NTO):
                st, a = divmod(ti, 4)
                stageXM(ti)
                if ti >= 1:
                    stageY(ti - 1)
                stageXE(ti)
                if ti == 15:
                    S.allgather(SD[0], RV[0], RG, "ag0")
                if a == 0 and st + 1 < NSUP:
                    prepA(st + 1)
                if a == 2 and st + 1 < NSUP:
                    prepB(st + 1)
                    if st + 2 < NSUP:
                        load_x(st + 2)
            stageY(NTO - 1)

    def phaseC(l, xsrc, ydst):
        NS = 4
        with ExitStack() as es:
            al = lambda n, s, d: Tl(es.enter_context(nc.sbuf_tensor(n + "_L%d" % l, list(s), d)), n)
            cm = [al("cm%d" % i, [128, DMIX], BF16) for i in range(NS)]
            cz = [al("cz%d" % i, [128, DMIX], BF16) for i in range(NS)]
            cf = [al("cf%d" % i, [128, 6, 256], BF16) for i in range(NS)]
            cx = [al("cx%d" % i, [128, D], F32) for i in range(NS)]
            cmz = [al("cmz%d" % i, [128, DMIX], BF16) for i in range(NS)]
            cmT = [al("cmT%d" % i, [128, 10, 128], BF16) for i in range(NS)]
            cjunk = al("cjunk", [128, 512], F32)
            cy = [al("cy%d" % i, [128, D], F32) for i in range(NS)]
            wout = al("wout", [128, 10, D], BF16)
            ws = [al("wsc0", [128, D], F32), al("wsc1", [128, D], F32)]
            for kc in range(10):
                w_ = ws[kc % 2]
                S.dma(w_[:, 0:D], wout_in[l, kc * 128:(kc + 1) * 128, :], r=[wout_in.b], w=[w_.b], sem="ws%d" % (kc % 2))
                eng = ("act", "dve", "dve")[kc % 3]
                if eng == "act":
                    S.op("act", lambda: A.copy(out=wout[:, kc, :], in_=w_[:, 0:D]), r=[w_.b], pw=[wout.b])
                elif eng == "pool":
                    S.op("pool", lambda: G.tensor_copy(out=wout[:, kc, :], in_=w_[:, 0:D]), r=[w_.b], pw=[wout.b])
                else:
                    S.op("dve", lambda: V.tensor_copy(out=wout[:, kc, :], in_=w_[:, 0:D]), r=[w_.b], pw=[wout.b])
            def loads(ti):
                s = ti % NS
                rows = slice(ti * 128, (ti + 1) * 128)
                S.dma(cm[s][:], MX[rows, :], r=[MX.b], w=[cm[s].b], sem="cm%d" % s)
                cands = [(MXF, 0), (MXF, 1), (FP, 0), (FP, 1), (GO, 0), (GO, 1)]
                for ci, (src_t, rg_) in enumerate(cands):
                    S.dma(cf[s][:, ci, :], src_t[rg_ * TO + ti * 128:rg_ * TO + (ti + 1) * 128, :], r=[src_t.b],
                          **({"w": [cf[s].b]} if ci == 0 else {"pw": [cf[s].b]}), sem="cf%d_%d" % (s, ci))
                S.dma(cz[s][:], ZS[rows, :], r=[ZS.b], w=[cz[s].b], sem="cz%d" % s)
                S.dma(cx[s][:], xsrc[rows, :], r=[xsrc.b], w=[cx[s].b], sem="cx%d" % s)

            cfts = [al("cft%d" % i, [128, 256], F32) for i in range(3)]
            cgts = [al("cgt%d" % i, [128, 256], F32) for i in range(3)]
            stC = [al("stC%d" % i, [128, 8], F32) for i in range(2)]
            byT = {}

            def front(ti):
                s = ti % NS
                cft_ = cfts[ti % 3]
                cgt_ = cgts[ti % 3]
                S.op("act", lambda: A.activation(out=cft_[:], in_=cf[s][:, 0, :], func=AF.Copy, scale=flags[:, 0:1]),
                     r=[cf[s].b, flags.b], w=[cft_.b])
                S.op("act", lambda: A.activation(out=cgt_[:], in_=cf[s][:, 4, :], func=AF.Copy, scale=flags[:, 4:5]),
                     r=[cf[s].b, flags.b], w=[cgt_.b])
                for ci in (1, 2):
                    S.op("dve", lambda: V.scalar_tensor_tensor(out=cft_[:], in0=cf[s][:, ci, :], scalar=flags[:, ci:ci + 1],
                                                               in1=cft_[:], op0=ALU.mult, op1=ALU.add),
                         r=[cf[s].b, cft_.b, flags.b], w=[cft_.b])
                S.op("dve", lambda: V.scalar_tensor_tensor(out=cm[s][:, 256:512], in0=cf[s][:, 3, :], scalar=flags[:, 3:4],
                                                           in1=cft_[:], op0=ALU.mult, op1=ALU.add),
                     r=[cf[s].b, cft_.b, flags.b], pw=[cm[s].b])
                S.op("dve", lambda: V.scalar_tensor_tensor(out=cm[s][:, 0:256], in0=cf[s][:, 5, :], scalar=flags[:, 5:6],
                                                           in1=cgt_[:], op0=ALU.mult, op1=ALU.add),
                     r=[cf[s].b, cgt_.b, flags.b], pw=[cm[s].b])
                S.op("dve", lambda: V.tensor_tensor(out=cmz[s][:], in0=cm[s][:], in1=cz[s][:], op=ALU.mult),
                     r=[cm[s].b, cz[s].b], w=[cmz[s].b])

            def mid1(ti):
                s = ti % NS
                for c in range(10):
                    pt_ = ptr if c < 8 else ptk
                    cc = c if c < 8 else c - 8
                    S.op("pe", lambda: P.transpose(out=pt_[:, cc, :], in_=cmz[s][:, c * 128:(c + 1) * 128], identity=ident[:]),
                         r=[cmz[s].b, ident.b], **({"w": [pt_.b]} if cc == 0 else {"pw": [pt_.b]}))
                S.op("act", lambda: A.copy(out=cmT[s][:, 0:8, :], in_=ptr[:]), r=[ptr.b], w=[cmT[s].b])
                S.op("act", lambda: A.copy(out=cmT[s][:, 8:10, :], in_=ptk[:, 0:2, :]), r=[ptk.b], pw=[cmT[s].b])

            def mid2(ti):
                s = ti % NS
                by = [pb[(2 * ti) % 4], pb[(2 * ti + 1) % 4]]
                byT[ti] = by
                for nb in range(2):
                    for kc in range(10):
                        S.op("pe", lambda: P.matmul(out=by[nb][:, :], lhsT=cmT[s][:, kc, :], rhs=wout[:, kc, nb * 512:(nb + 1) * 512],
                                                   start=(kc == 0), stop=(kc == 9)), r=[cmT[s].b, wout.b],
                             **({"w": [by[nb].b]} if kc == 0 else {"pw": [by[nb].b]}))

            def tail(ti):
                s = ti % NS
                rows = slice(ti * 128, (ti + 1) * 128)
                by = byT.pop(ti)
                st_ = stC[ti % 2]
                for nb in range(2):
                    S.op("act", lambda: A.activation(out=cjunk[:], in_=by[nb][:, :], func=AF.Square,
                                                     accum_out=st_[:, nb:nb + 1]), r=[by[nb].b], w=[cjunk.b], pw=[st_.b])
                S.op("dve", lambda: V.tensor_tensor(out=st_[:, 2:3], in0=st_[:, 0:1], in1=st_[:, 1:2], op=ALU.add),
                     r=[st_.b], w=[st_.b])
                S.op("dve", lambda: V.tensor_scalar(out=st_[:, 3:4], in0=st_[:, 2:3], scalar1=1.0 / D, scalar2=EPS,
                                                    op0=ALU.mult, op1=ALU.add), r=[st_.b], w=[st_.b])
                S.op("pool", lambda: G.tensor_tensor(out=st_[:, 3:4], in0=st_[:, 3:4], in1=mhalf[:, 0:1], op=ALU.pow),
                     r=[st_.b, mhalf.b], w=[st_.b])
                for nb in range(2):
                    sl = slice(nb * 512, (nb + 1) * 512)
                    S.op("dve", lambda: V.scalar_tensor_tensor(out=cy[s][:, sl], in0=by[nb][:, :], scalar=st_[:, 3:4],
                                                               in1=pgt[0][:, sl], op0=ALU.mult, op1=ALU.mult),
                         r=[by[nb].b, st_.b, pgt[0].b], **({"w": [cy[s].b]} if nb == 0 else {"pw": [cy[s].b]}))
                S.op("dve", lambda: V.tensor_tensor(out=cy[s][:], in0=cy[s][:], in1=cx[s][:], op=ALU.add),
                     r=[cy[s].b, cx[s].b], w=[cy[s].b])
                S.dma(ydst[rows, :], cy[s][:], r=[cy[s].b], pw=[ydst.b], sem="cy%d" % s)

            loads(0)
            loads(1)
            loads(2)
            front(0)
            front(1)
            mid1(0)
            for ti in range(NTO):
                if ti + 3 < NTO:
                    loads(ti + 3)
                if ti + 2 < NTO:
                    front(ti + 2)
                mid2(ti)
                if ti + 1 < NTO:
                    mid1(ti + 1)
                tail(ti)

    def phaseAttn(l, es, qtiles=None):
        if True:
            al = lambda n, s, d: Tl(es.enter_context(nc.sbuf_tensor(n + "_L%d" % l, list(s), d)), n)
            KTs = al("KTs", [128, T], BF16)
            Va = al("Va", [128, 64, 2, 66], BF16)
            aq = [al("aq0", [128, 8, 128], BF16), al("aq1", [128, 8, 128], BF16)]
            pT = [al("pT%d" % i, [128, 1024], BF16) for i in range(3)]
            oT = [al("oT0", [65, 512], F32), al("oT1", [65, 512], F32)]
            on = [al("on0", [128, 512], BF16), al("on1", [128, 512], BF16)]
            rden = al("rden", [128, 8], F32)
            for j in range(2):
                for g in range(2):
                    S.dma(KTs[g * 64:(g + 1) * 64, j * TO:(j + 1) * TO], RVV[j]["kt"][:, g, :], r=[RV[2].b], pw=[KTs.b],
                          sem="kts%d" % g)
            for q_ in aq:
                S.op("pool", lambda: G.memset(q_[:], 0.0), w=[q_.b])
            Vst = al("Vst", [128, 64, 128], BF16)
            for j in range(2):
                S.dma(Vst[:, j * 32:(j + 1) * 32, :], RVV[j]["vv"].rearrange("(k p) c -> p k c", p=128), r=[RV[2].b],
                      pw=[Vst.b], sem="va%d" % j)
            S.op("dve", lambda: V.memset(Va[:], 1.0), w=[Va.b])
            for g in range(2):
                S.op("dve", lambda: V.tensor_copy(out=Va[:, :, g, 0:64], in_=Vst[:, :, g * 64:(g + 1) * 64]), r=[Vst.b], pw=[Va.b])
            for j in range(2):
                S.op("dve", lambda: V.tensor_scalar(out=Va[:, j * 32:(j + 1) * 32, :, :].rearrange("p a b c -> p (a b c)"),
                                                    in0=Va[:, j * 32:(j + 1) * 32, :, :].rearrange("p a b c -> p (a b c)"),
                                                    scalar1=flags[:, 7 + j:8 + j], scalar2=None, op0=ALU.mult),
                     r=[Va.b, flags.b], pw=[Va.b])
            bsb = [[pb[0].b, pb[1].b], [pb[2].b, pb[3].b], [pb[4].b, ptr.b]]
            bo1 = Tl(PS_[0][:, :], "x")
            bo1.b = ptq.b
            bo = [bo1, bo1]
            bt = sideT
            qlist = list(range(NTO) if qtiles is None else qtiles)
            NP2 = NTILE // 2
            its = [(qn, qi, g, kp) for qn, qi in enumerate(qlist) for g in range(2) for kp in range(NP2)]

            def load_q(qn):
                qi = qlist[qn]
                q_ = aq[qn % 2]
                for g in range(2):
                    S.dma(q_[g * 64:(g + 1) * 64, g * 4:(g + 1) * 4, :], QT[:, g * 4:(g + 1) * 4, qi * 128:(qi + 1) * 128],
                          r=[QT.b], pw=[q_.b], sem="aq%d_%d" % (qn % 2, g))

            def mm1(i):
                qn, qi, g, kp = its[i]
                q_ = aq[qn % 2]
                for j in range(2):
                    kt = kp * 2 + j
                    S.op("pe", lambda: P.matmul(out=PD[i % 3][:, j * 512:(j + 1) * 512], lhsT=KTs[:, kt * 128:(kt + 1) * 128],
                                               rhs=q_[:, g * 4:(g + 1) * 4, :], start=True, stop=True),
                         r=[KTs.b, q_.b], w=[bsb[i % 3][j]])

            deferred = []
            load_q(0)
            if len(qlist) > 1:
                load_q(1)
            mm1(0)
            mm1(1)
            for i, (qn, qi, g, kp) in enumerate(its):
                rows = slice(qi * 128, (qi + 1) * 128)
                qhalf = qi // 32
                on_ = on[qn % 2]
                p_ = pT[i % 3]
                S.op("act", lambda: A.activation(out=p_[:], in_=PD[i % 3][:, :], func=AF.Exp, scale=0.125), r=bsb[i % 3], w=[p_.b])
                if i + 2 < len(its):
                    mm1(i + 2)
                for j in range(2):
                    kt = kp * 2 + j
                    vs = Va
                    S.op("pe", lambda: P.matmul(out=bo[g][0:65, :], lhsT=vs[:, kt, g, 0:65], rhs=p_[:, j * 512:(j + 1) * 512],
                                               start=(kt == 0), stop=(kt == NTILE - 1)),
                         r=[vs.b, p_.b], **({"w": [bo[g].b]} if kt == 0 else {"pw": [bo[g].b]}))
                if kp == NP2 - 1:
                    o_ = oT[g]
                    S.op("dve", lambda: V.tensor_copy(out=o_[:], in_=bo[g][0:65, :]), r=[bo[g].b], w=[o_.b])

                    def epi(g=g, o_=o_, on_=on_, rows=rows, qn=qn):
                        for h in range(4):
                            S.op("pe", lambda: P.transpose(out=bt[:, h * 65:(h + 1) * 65], in_=o_[0:65, h * 128:(h + 1) * 128],
                                                          identity=identf[0:65, 0:65]), r=[o_.b, identf.b],
                                 **({"w": [bt.b]} if h == 0 else {"pw": [bt.b]}))
                        btv = bt[:, 0:260].rearrange("p (h e) -> p h e", e=65)
                        S.op("dve", lambda: V.reciprocal(out=rden[:, g * 4:(g + 1) * 4], in_=btv[:, :, 64]), r=[bt.b], pw=[rden.b])
                        S.op("dve", lambda: V.tensor_tensor(out=on_[:, g * 256:(g + 1) * 256].rearrange("p (h d) -> p h d", d=64),
                                                            in0=btv[:, :, 0:64],
                                                            in1=rden[:, g * 4:(g + 1) * 4].unsqueeze(2).to_broadcast([128, 4, 64]),
                                                            op=ALU.mult), r=[bt.b, rden.b], **({"w": [on_.b]} if g == 0 else {"pw": [on_.b]}))
                        if g == 1:
                            S.dma(MX[rows, 512:1024], on_[:], r=[on_.b], pw=[MX.b], sem="on%d" % (qn % 2))
                            if qn + 2 < len(qlist):
                                load_q(qn + 2)

                    deferred.append((i + 3, epi))
                while deferred and deferred[0][0] <= i and not S.pending:
                    deferred.pop(0)[1]()
                yield
            while deferred:
                if S.pending:
                    yield
                    continue
                deferred.pop(0)[1]()
            yield

    def phaseGLA(l, side=False):
        BW = 512
        NCB = BW // 64
        gbank = (lambda: sideT) if side else bank
        ptrT = ptk if side else ptr
        with ExitStack() as es:
            al = lambda n, s, d: Tl(es.enter_context(nc.sbuf_tensor(n + "_L%d" % l, list(s), d)), n)
            gc = al("gc", [128, GC_N], F32)
            wgf = al("wgf", [16, 256], F32)
            wgb = al("wgb", [16, 256], BF16)
            nbg = al("nbg", [128, 4], F32)
            Sst = al("Sst", [128, 256], F32)
            Sbf = al("Sbf", [128, 256], BF16)
            tmpS = al("tmpS", [128, 256], F32)
            lrs = [al("glr%d" % i, [16, BW], BF16) for i in range(2)]
            qTbs = [al("gqTb%d" % i, [128, BW], BF16) for i in range(2)]
            kTbs = [al("gkTb%d" % i, [128, BW], BF16) for i in range(2)]
            sp = al("gsp", [128, BW], F32)
            cs = al("gcs", [128, BW], F32)
            Eb = al("gE", [128, BW], F32)
            Ei = al("gEi", [128, BW], F32)
            totc = al("gtot", [128, NCB], F32)
            Qbd = al("gQbd", [128, NCB, 4, 64], BF16)
            qtl = al("gqtl", [128, BW], BF16)
            ktl = al("gktl", [128, BW], BF16)
            vchs = [al("gvch%d" % i, [64, NCB, 256], BF16) for i in range(2)]
            ktok = [al("gktok0", [64, 128], BF16), al("gktok1", [64, 128], BF16)]
            sT = [al("gsT0", [64, 256], BF16), al("gsT1", [64, 256], BF16)]
            osb = al("gosb", [64, NCB, 256], F32)
            ofls = [al("gofl%d" % i, [64, NCB, 256], F32) for i in range(2)]
            obf = al("gobf", [64, NCB, 256], BF16)
            gst = al("ggst", [64, 2 * NCB * 4], F32)
            S.dma(gc[:], gconst_in[:, :], r=[gconst_in.b], w=[gc.b], sem="misc")
            S.dma(wgf[:], wg2_in[l], r=[wg2_in.b], w=[wgf.b], sem="misc")
            S.op("dve", lambda: V.tensor_copy(out=wgb[:], in_=wgf[:]), r=[wgf.b], w=[wgb.b])
            S.op("dve", lambda: V.tensor_scalar(out=nbg[:, 0:2], in0=prm[:, P_BGF:P_BGF + 2], scalar1=-1.0, scalar2=None,
                                                op0=ALU.mult), r=[prm.b], w=[nbg.b])
            S.op("dve", lambda: V.memset(nbg[:, 2:3], 1.0), pw=[nbg.b])
            nblk = T // BW
            NBR = TO // BW
            gblocks = [(d_, b_) for d_ in range(2) for b_ in (range(nblk) if d_ == 0 else range(nblk - 1, -1, -1))]
            kblk = [0]

            def gl_load(kb, late=False):
                d_, b_ = gblocks[kb]
                s_ = kb % 2
                rj, lb = b_ // NBR, b_ % NBR
                lsl = slice(lb * BW, (lb + 1) * BW)
                S.dma(lrs[s_][:], RVV[rj]["lrf" if d_ == 0 else "lrb"][:, lsl], r=[RV[4].b], w=[lrs[s_].b], sem="g_lr%d" % s_)
                S.dma(qTbs[s_][:], RVV[rj]["gqt"][:, lsl], r=[RV[3].b], w=[qTbs[s_].b], sem="g_q%d" % s_)
                S.dma(kTbs[s_][:], RVV[rj]["gkt"][:, lsl], r=[RV[3].b], w=[kTbs[s_].b], sem="g_k%d" % s_)
                avT, avv = avu_rows(RV, rj, lb * BW, BW)
                S.dma(vchs[s_][:], avv[:, 0:256].rearrange("(c s) f -> s c f", s=64), r=[avT.b], w=[vchs[s_].b], sem="g_v%d" % s_)
                if d_ == 1 and (kb != nblk or late):
                    S.dma(ofls[s_][:], OF[b_ * BW:(b_ + 1) * BW, :].rearrange("(c s) f -> s c f", s=64), r=[OF.b], w=[ofls[s_].b],
                          sem="g_of%d" % s_)

            gl_load(0)
            for d in range(2):
                S.op("dve", lambda: V.memset(Sst[:], 0.0), w=[Sst.b])
                S.op("dve", lambda: V.memset(Sbf[:], 0.0), w=[Sbf.b])
                lrk = "lrf" if d == 0 else "lrb"
                mcol = GC_MF if d == 0 else GC_MB
                for bi in (range(nblk) if d == 0 else range(nblk - 1, -1, -1)):
                    t0 = bi * BW
                    tsl = slice(t0, t0 + BW)
                    kb = kblk[0]
                    kblk[0] += 1
                    if kb + 1 < len(gblocks):
                        gl_load(kb + 1)
                    if kb == nblk:
                        d_, b_ = gblocks[kb]
                        S.dma(ofls[kb % 2][:], OF[b_ * BW:(b_ + 1) * BW, :].rearrange("(c s) f -> s c f", s=64), r=[OF.b],
                              w=[ofls[kb % 2].b], sem="g_of%d" % (kb % 2))
                    lr, qTb, kTb, vch, ofl = lrs[kb % 2], qTbs[kb % 2], kTbs[kb % 2], vchs[kb % 2], ofls[kb % 2]
                    for j in range(BW // 512):
                        bk = gbank()
                        S.op("pe", lambda: P.matmul(out=bk[:, :], lhsT=wgb[:, d * 128:(d + 1) * 128], rhs=lr[:, j * 512:(j + 1) * 512],
                                                   start=True, stop=True), r=[wgb.b, lr.b], w=[bk.b])
                        S.op("act", lambda: A.activation(out=sp[:, j * 512:(j + 1) * 512], in_=bk[:, :], func=AF.Exp, scale=-1.0,
                                                         bias=nbg[:, d:d + 1]), r=[bk.b, nbg.b], **({"w": [sp.b]} if j == 0 else {"pw": [sp.b]}))
                    yield
                    S.op("act", lambda: A.activation(out=sp[:], in_=sp[:], func=AF.Ln, bias=nbg[:, 2:3]), r=[sp.b, nbg.b], w=[sp.b])
                    yield
                    S.op("dve", lambda: V.tensor_tensor_scan(out=cs[:], data0=gc[:, GC_SM:GC_SM + BW], data1=sp[:], initial=0.0,
                                                             op0=ALU.mult, op1=ALU.add), r=[gc.b, sp.b], w=[cs.b])
                    c3 = lambda t_: t_[:].rearrange("p (c s) -> p c s", s=64)
                    if d == 1:
                        S.op("dve", lambda: V.tensor_copy(out=totc[:], in_=c3(cs)[:, :, 63]), r=[cs.b], w=[totc.b])
                        S.op("dve", lambda: V.tensor_tensor(out=Ei[:], in0=sp[:], in1=cs[:], op=ALU.subtract), r=[sp.b, cs.b], w=[Ei.b])
                        S.op("dve", lambda: V.tensor_tensor(out=c3(cs), in0=c3(Ei), in1=totc[:].unsqueeze(2).to_broadcast([128, NCB, 64]),
                                                            op=ALU.add), r=[Ei.b, totc.b], w=[cs.b])
                    yield
                    S.op("act", lambda: A.activation(out=Eb[:], in_=cs[:], func=AF.Exp, scale=-1.0 / 16), r=[cs.b], w=[Eb.b])
                    S.op("act", lambda: A.activation(out=Ei[:], in_=cs[:], func=AF.Exp, scale=1.0 / 16), r=[cs.b], w=[Ei.b])
                    yield
                    for h in range(4):
                        S.op("dve", lambda: V.scalar_tensor_tensor(out=Qbd[:, :, h, :], in0=c3(qTb), scalar=gc[:, GC_HM + h:GC_HM + h + 1],
                                                                   in1=c3(Eb), op0=ALU.mult, op1=ALU.mult),
                             r=[qTb.b, gc.b, Eb.b], **({"w": [Qbd.b]} if h == 0 else {"pw": [Qbd.b]}))
                    S.op("dve", lambda: V.scalar_tensor_tensor(out=qtl[:], in0=qTb[:], scalar=32.0 ** -0.5, in1=Eb[:],
                                                               op0=ALU.mult, op1=ALU.mult), r=[qTb.b, Eb.b], w=[qtl.b])
                    S.op("pool", lambda: G.tensor_tensor(out=ktl[:], in0=kTb[:], in1=Ei[:], op=ALU.mult), r=[kTb.b, Ei.b], w=[ktl.b])
                    yield
                    for c in (range(NCB) if d == 0 else range(NCB - 1, -1, -1)):
                        gcx = bi * NCB + c
                        if (d == 0 and gcx == 64) or (d == 1 and gcx == 63):
                            S.op("dve", lambda: V.tensor_scalar(out=Sst[:], in0=Sst[:], scalar1=flags[:, 6:7], scalar2=None, op0=ALU.mult),
                                 r=[Sst.b, flags.b], w=[Sst.b])
                            S.op("dve", lambda: V.tensor_copy(out=Sbf[:], in_=Sst[:]), r=[Sst.b], w=[Sbf.b])
                        csl = slice(c * 64, (c + 1) * 64)
                        kt_ = ktok[c % 2]
                        sT_ = sT[c % 2]
                        S.op("pe", lambda: P.transpose(out=ptrT[0:64, 0, :], in_=ktl[:, csl], identity=ident[:]), r=[ktl.b, ident.b], w=[ptrT.b])
                        yield
                        S.op("dve", lambda: V.tensor_copy(out=kt_[:], in_=ptrT[0:64, 0, :]), r=[ptrT.b], w=[kt_.b])
                        bsc = gbank()
                        S.op("pe", lambda: P.matmul(out=bsc[0:64, 0:256], lhsT=ktl[:, csl], rhs=Qbd[:, c, :, :], start=True, stop=True),
                             r=[ktl.b, Qbd.b], w=[bsc.b])
                        yield
                        S.op("dve", lambda: V.tensor_tensor(out=sT_[:].rearrange("p (h t) -> p h t", t=64),
                                                            in0=bsc[0:64, 0:256].rearrange("p (h t) -> p h t", t=64),
                                                            in1=gc[0:64, mcol:mcol + 64].unsqueeze(1).to_broadcast([64, 4, 64]),
                                                            op=ALU.mult), r=[bsc.b, gc.b], w=[sT_.b])
                        yield
                        bo = gbank()
                        S.op("pe", lambda: P.matmul(out=bo[0:64, 0:256], lhsT=qtl[:, csl], rhs=Sbf[:, :], start=True, stop=False),
                             r=[qtl.b, Sbf.b], w=[bo.b])
                        for h in range(4):
                            S.op("pe", lambda: P.matmul(out=bo[0:64, h * 64:(h + 1) * 64], lhsT=sT_[:, h * 64:(h + 1) * 64],
                                                       rhs=vch[:, c, h * 64:(h + 1) * 64], start=False, stop=(h == 3)),
                                 r=[sT_.b, vch.b], pw=[bo.b])
                        yield
                        if d == 0:
                            S.op("dve", lambda: V.tensor_copy(out=osb[:, c, :], in_=bo[0:64, 0:256]), r=[bo.b], pw=[osb.b])
                        else:
                            S.op("dve", lambda: V.tensor_tensor(out=osb[:, c, :], in0=bo[0:64, 0:256], in1=ofl[:, c, :], op=ALU.add),
                                 r=[bo.b, ofl.b], pw=[osb.b])
                        bst = gbank()
                        S.op("pe", lambda: P.matmul(out=bst[:, 0:256], lhsT=kt_[:], rhs=vch[:, c, :], start=True, stop=True),
                             r=[kt_.b, vch.b], w=[bst.b])
                        yield
                        didx = c * 64 + (63 if d == 0 else 0)
                        S.op("dve", lambda: V.scalar_tensor_tensor(out=tmpS[:], in0=bst[:, 0:256], scalar=Eb[:, didx:didx + 1],
                                                                   in1=gc[:, GC_BD:GC_BD + 256], op0=ALU.mult, op1=ALU.mult),
                             r=[bst.b, Eb.b, gc.b], w=[tmpS.b])
                        yield
                        S.op("dve", lambda: V.scalar_tensor_tensor(out=Sst[:], in0=Sst[:], scalar=Eb[:, didx:didx + 1], in1=tmpS[:],
                                                                   op0=ALU.mult, op1=ALU.add), r=[Sst.b, Eb.b, tmpS.b], w=[Sst.b])
                        yield
                        S.op("dve", lambda: V.tensor_copy(out=Sbf[:], in_=Sst[:]), r=[Sst.b], w=[Sbf.b])
                        yield
                    if "gla_no_end" in dbg or ("gla_no_end1" in dbg and d == 1):
                        continue
                    if d == 0:
                        S.dma(OF[tsl, :].rearrange("(c s) f -> s c f", s=64), osb[:], r=[osb.b], pw=[OF.b], sem="g_os")
                    else:
                        o4 = lambda t_: t_[:].rearrange("p c (h d) -> p (c h) d", d=64)
                        S.op("dve", lambda: V.tensor_tensor(out=ofl[:], in0=osb[:], in1=osb[:], op=ALU.mult), r=[osb.b], w=[ofl.b])
                        S.op("dve", lambda: V.tensor_reduce(out=gst[:, 0:NCB * 4], in_=o4(ofl), op=ALU.add, axis=AX.X), r=[ofl.b], w=[gst.b])
                        S.op("dve", lambda: V.tensor_scalar(out=gst[:, 0:NCB * 4], in0=gst[:, 0:NCB * 4], scalar1=1.0 / 64, scalar2=EPS,
                                                            op0=ALU.mult, op1=ALU.add), r=[gst.b], w=[gst.b])
                        S.op("act", lambda: A.activation(out=gst[:, 0:NCB * 4], in_=gst[:, 0:NCB * 4], func=AF.Ln), r=[gst.b], w=[gst.b])
                        S.op("act", lambda: A.activation(out=gst[:, 0:NCB * 4], in_=gst[:, 0:NCB * 4], func=AF.Exp, scale=-0.5),
                             r=[gst.b], w=[gst.b])
                        S.op("dve", lambda: V.tensor_tensor(out=o4(osb), in0=o4(osb),
                                                            in1=gst[:, 0:NCB * 4].unsqueeze(2).to_broadcast([64, NCB * 4, 64]), op=ALU.mult),
                             r=[osb.b, gst.b], w=[osb.b])
                        S.op("dve", lambda: V.tensor_tensor(out=obf[:], in0=osb[:],
                                                            in1=bc[0:64, BC_ONG:BC_ONG + 256].unsqueeze(1).to_broadcast([64, NCB, 256]),
                                                            op=ALU.mult), r=[osb.b, bc.b], w=[obf.b])
                        S.dma(GO[tsl, :].rearrange("(c s) f -> s c f", s=64), obf[:], r=[obf.b], pw=[GO.b], sem="g_ob")

    def phaseFFT(l, side=False):
        gbank = (lambda: sideT) if side else bank
        with ExitStack() as es:
            al = lambda n, s, d: Tl(es.enter_context(nc.sbuf_tensor(n + "_L%d" % l, list(s), d)), n)
            t1 = al("ft1", [128, 64, 256], BF16)
            t2 = al("ft2", [128, 128], BF16)
            up = al("fup", [128, 64, 256], BF16)
            csf = al("fcsf", [128, 2, 512], F32)
            csb = al("fcsb", [128, 2, 512], BF16)
            fwf = al("ffwf", [128, 2, 256], F32)
            fwb = al("ffwb", [128, 2, 256], BF16)
            w12 = al("fw12", [128, 2, 2, 256], BF16)
            g1s = [al("fg1s0", [128, 512], BF16), al("fg1s1", [128, 512], BF16)]
            ab = [al("fab0", [128, 8, 256], BF16), al("fab1", [128, 8, 256], BF16)]
            yT = [al("fyT0", [128, 2, 128], BF16), al("fyT1", [128, 2, 128], BF16)]
            ob = [al("fob0", [64, 8, 256], BF16), al("fob1", [64, 8, 256], BF16)]
            for q4 in range(4):
                S.dma(t1[:, q4 * 16:(q4 + 1) * 16, :], tab1_in[q4 * 16:(q4 + 1) * 16, :, :].rearrange("n p c -> p n c"),
                      r=[tab1_in.b], pw=[t1.b], sem="f_t1")
                avT, avv = avu_rows(RV, q4 // 2, (q4 % 2) * 2048, 2048)
                S.dma(up[q4 * 32:(q4 + 1) * 32, :, :], avv[:, 256:512].rearrange("(p n) c -> p n c", n=64),
                      r=[avT.b], pw=[up.b], sem="f_up")
            S.dma(t2[:], tab2_in[:, :], r=[tab2_in.b], w=[t2.b], sem="misc")
            S.dma(csf[:], cs64_in[:, :, :], r=[cs64_in.b], w=[csf.b], sem="misc")
            S.dma(fwf[:], fnet_in[l].rearrange("(k p) c -> p k c", p=128), r=[fnet_in.b], w=[fwf.b], sem="misc")
            if side:
                for _ in range(int(90 * SIDE_RATIO)):
                    yield
            S.op("dve", lambda: V.tensor_copy(out=csb[:], in_=csf[:]), r=[csf.b], w=[csb.b])
            S.op("dve", lambda: V.tensor_copy(out=fwb[:], in_=fwf[:]), r=[fwf.b], w=[fwb.b])
            for m in range(2):
                for wi in range(2):
                    bk = gbank()
                    for kc in range(2):
                        S.op("pe", lambda: P.matmul(out=bk[:, 0:256], lhsT=csb[:, kc, wi * 256 + m * 128:wi * 256 + (m + 1) * 128],
                                                   rhs=fwb[:, kc, :], start=(kc == 0), stop=(kc == 1)), r=[csb.b, fwb.b],
                             **({"w": [bk.b]} if kc == 0 else {"pw": [bk.b]}))
                    S.op("act", lambda: A.copy(out=w12[:, m, wi, :], in_=bk[:, 0:256]), r=[bk.b], pw=[w12.b])
            for n2 in range(64):
                bk = gbank()
                S.op("pe", lambda: P.matmul(out=bk[:, 0:256], lhsT=t1[:, n2, 0:128], rhs=up[:, n2, :], start=True, stop=True),
                     r=[t1.b, up.b], w=[bk.b])
                S.op("pe", lambda: P.matmul(out=bk[:, 256:512], lhsT=t1[:, n2, 128:256], rhs=up[:, n2, :], start=True, stop=True),
                     r=[t1.b, up.b], pw=[bk.b])
                yield
                g_ = g1s[n2 % 2]
                S.op("dve", lambda: V.tensor_copy(out=g_[:], in_=bk[:, :]), r=[bk.b], w=[g_.b])
                S.dma(G1[:, :, n2, :], g_[:].rearrange("p (r c) -> p r c", r=2), r=[g_.b], pw=[G1.b], sem="f_g%d" % (n2 % 2))
                yield
            mxv = MXF[:, :].rearrange("(k p) c -> k p c", p=128)
            for pgp in range(16):
                p0 = pgp * 8
                ab_ = ab[pgp % 2]
                ob_ = ob[pgp % 2]
                if pgp == 0:
                    S.dma(ab_[:], G1[p0:p0 + 8, :, :, :].rearrange("p r n c -> (r n) p c"), r=[G1.b], w=[ab_.b], sem="f_ab%d" % (pgp % 2))
                if pgp + 1 < 16:
                    S.dma(ab[(pgp + 1) % 2][:], G1[p0 + 8:p0 + 16, :, :, :].rearrange("p r n c -> (r n) p c"), r=[G1.b],
                          w=[ab[(pgp + 1) % 2].b], sem="f_ab%d" % ((pgp + 1) % 2))
                for j in range(8):
                    y_ = yT[j % 2]
                    bk = gbank()
                    for m in range(2):
                        S.op("pe", lambda: P.matmul(out=bk[:, m * 128:(m + 1) * 128], lhsT=ab_[:, j, m * 128:(m + 1) * 128], rhs=t2[:, :],
                                                   start=True, stop=True), r=[ab_.b, t2.b], **({"w": [bk.b]} if m == 0 else {"pw": [bk.b]}))
                    yield
                    S.op("dve", lambda: V.tensor_copy(out=y_[:].rearrange("p m k -> p (m k)"), in_=bk[:, 0:256]), r=[bk.b], w=[y_.b])
                    yield
                    b2 = gbank()
                    i4 = 0
                    for m in range(2):
                        for part in range(2):
                            S.op("pe", lambda: P.matmul(out=b2[0:64, 0:256], lhsT=y_[:, m, part * 64:(part + 1) * 64], rhs=w12[:, m, part, :],
                                                       start=(i4 == 0), stop=(i4 == 3)), r=[y_.b, w12.b],
                                 **({"w": [b2.b]} if i4 == 0 else {"pw": [b2.b]}))
                            i4 += 1
                    yield
                    S.op("dve", lambda: V.tensor_copy(out=ob_[:, j, :], in_=b2[0:64, 0:256]), r=[b2.b], pw=[ob_.b])
                    yield
                S.dma(mxv[:, p0:p0 + 8, :], ob_[:], r=[ob_.b], pw=[MXF.b], sem="f_o%d" % (pgp % 2))
                seq = p0 // 64
                k10 = p0 % 64
                fpv = FP[seq * 4096:(seq + 1) * 4096, :].rearrange("(k q) c -> k q c", q=64)
                S.dma(fpv[:, k10:k10 + 8, :], ob_[:], r=[ob_.b], pw=[FP.b], sem="f_p%d" % (pgp % 2))

    def side_chain(l):
        yield from phaseFFT(l, True)
        S.barrier()
        yield from phaseGLA(l, True)

    for l in range(2):
        xsrc = x_in if l == 0 else Y1
        ydst = Y1 if l == 0 else y_out
        S.barrier()
        with ExitStack() as esW:
            win_t = Tl(esW.enter_context(nc.sbuf_tensor("win_L%d" % l, [128, 8, DIN], BF16)), "win")
            with ExitStack() as esS:
                ws = [Tl(esS.enter_context(nc.sbuf_tensor("wst%d_L%d" % (i, l), [128, DIN], F32)), "wst%d" % i) for i in range(2)]

                def w_chunk(kc):
                    w_ = ws[kc % 2]
                    S.dma(w_[:, :], win_in[l, kc * 128:(kc + 1) * 128, :], r=[win_in.b], w=[w_.b], sem="ws%d" % (kc % 2), q="act")
                    S.op("dve", lambda: V.tensor_copy(out=win_t[:, kc, :], in_=w_[:, :]), r=[w_.b], pw=[win_t.b])

                phase0(l, w_chunk)
                S.barrier()
            phaseA(l, xsrc, win_t)
        for i in (2, 1, 3, 4):
            S.allgather(SD[i], RV[i], RG, "ag%d" % i)
        S.barrier()
        if "only_A" in dbg:
            break
        with ExitStack() as esA:
            main = phaseAttn(l, esA)
            side = side_chain(l)
            acc, side_done = 0.0, False
            n_main = n_side = n_tail = 0
            for _ in main:
                n_main += 1
                acc += SIDE_RATIO
                while acc >= 1.0 and not side_done:
                    acc -= 1.0
                    try:
                        next(side)
                        n_side += 1
                    except StopIteration:
                        side_done = True
                        S.side_done_at = n_main
            if not side_done:
                for _ in side:
                    n_tail += 1
            S.counts = (n_main, n_side, n_tail)
            S.barrier()
        phaseC(l, xsrc, ydst)
    outs = [y_out, Y1, MX, ZS, QT, FP, OF, GO, MXF] + RV
    S.finish(outs)
    return nc, S


def _host_layout(inputs):
    f = np.float32
    xp = np.asarray(inputs["x_prompt"], f)
    xs = np.asarray(inputs["x_sample"], f)
    cp = np.asarray(inputs["c_prompt"], f)
    csm = np.asarray(inputs["c_sample"], f)
    blocks_x = [xp[0:2].reshape(T, D), xp[2:4].reshape(T, D), xs[0], xs[1]]
    blocks_c = [cp[0:2], cp[2:4], np.stack([csm[0], csm[0]]), np.stack([csm[1], csm[1]])]
    is_sample = [False, False, True, True]

    def fm(v, n):
        return np.ascontiguousarray(np.asarray(v, f).reshape(n, 128).T)

    def rep(v, p=128):
        return np.ascontiguousarray(np.broadcast_to(np.asarray(v, f)[None, :], (p, v.shape[0])))

    prm = np.zeros((2, 128, NPRM), f)
    bcm = np.zeros((2, 128, NBC), f)
    wg2 = np.zeros((2, 16, 256), f)
    sgu_wT = np.zeros((2, 128, 512), f)
    for l in range(2):
        prm[l, :, P_ADAB:P_ADAB + 24] = fm(inputs["ada_b"][l], 24)
        prm[l, :, P_PREG:P_PREG + 8] = fm(inputs["norm_pre_g"][l], 8)
        prm[l, :, P_BGF] = np.asarray(inputs["gla_bg_f"][l], f)
        prm[l, :, P_BGB] = np.asarray(inputs["gla_bg_b"][l], f)
        prm[l, :, P_SGUB:P_SGUB + 4] = np.asarray(inputs["sgu_b"][l], f).T
        bcm[l, :, BC_ADABG:BC_ADABG + 1024] = rep(np.asarray(inputs["ada_b"][l], f)[2048:3072])
        bcm[l, :, BC_POSTG:BC_POSTG + 1024] = rep(np.asarray(inputs["norm_post_g"][l], f))
        qkg = np.concatenate([np.tile(np.asarray(inputs["q_norm_g"][l], f), 8), np.tile(np.asarray(inputs["k_norm_g"][l], f), 2)])
        bcm[l, :, BC_QKG:BC_QKG + 640] = rep(qkg)
        bcm[l, :, BC_SGUG:BC_SGUG + 256] = rep(np.asarray(inputs["sgu_norm_g"][l], f))
        bcm[l, :, BC_ONG:BC_ONG + 256] = rep(np.tile(np.asarray(inputs["gla_onorm_g"][l], f), 4))
        wg2[l, :, 0:128] = np.asarray(inputs["gla_wg2_f"][l], f)
        wg2[l, :, 128:256] = np.asarray(inputs["gla_wg2_b"][l], f)
        sgu_wT[l] = np.asarray(inputs["sgu_w"][l], f).transpose(2, 0, 1).reshape(128, 512)

    def rope_tab(n):
        t = np.arange(n)
        row = (t // 64).astype(np.float64)
        col = (t % 64).astype(np.float64)
        freqs = 10000.0 ** (-np.arange(0, 32, 2, dtype=np.float64) / 32)
        ang = np.concatenate([row[:, None] * freqs, col[:, None] * freqs], axis=-1)
        return np.concatenate([np.cos(ang), np.sin(ang)], axis=-1).astype(f)

    rope_s = rope_tab(8192)
    rope_p = np.concatenate([rope_tab(4096), rope_tab(4096)], axis=0)
    gconst = np.zeros((128, GC_N), f)
    s_i = np.arange(64)[:, None]
    t_i = np.arange(64)[None, :]
    gconst[0:64, GC_MF:GC_MF + 64] = (s_i <= t_i)
    gconst[0:64, GC_MB:GC_MB + 64] = (s_i > t_i)
    for h in range(4):
        gconst[h * 32:(h + 1) * 32, GC_HM + h] = 32.0 ** -0.5
        gconst[h * 32:(h + 1) * 32, GC_BD + h * 64:GC_BD + (h + 1) * 64] = 1.0
    sm = np.ones(2048, f)
    sm[0::64] = 0.0
    gconst[:, GC_SM:GC_SM + 2048] = sm[None, :]

    a64 = np.arange(64, dtype=np.float64)
    th64 = 2 * np.pi * a64[:, None] * a64[None, :] / 64.0
    Cbd = np.kron(np.eye(4), np.cos(th64))
    Sbd = np.kron(np.eye(4), np.sin(th64))
    cs64 = np.concatenate([Cbd, Sbd], axis=1).reshape(2, 128, 512).transpose(1, 0, 2).astype(f)
    cs64 = np.ascontiguousarray(cs64)
    shared = {
        "ada_w": np.ascontiguousarray(np.asarray(inputs["ada_w"], f)),
        "w_in": np.ascontiguousarray(np.asarray(inputs["w_in"], f)),
        "w_out": np.ascontiguousarray(np.asarray(inputs["w_out"], f)),
        "fnet_w": np.ascontiguousarray(np.asarray(inputs["fnet_w"], f)),
        "sgu_wT": sgu_wT, "wg2": wg2, "prm": prm, "bc": bcm, "gconst": gconst,
        "cs64": cs64,
    }
    rope_p1 = rope_tab(4096)
    tabs = {}
    for smp in (False, True):
        n2 = np.arange(64, dtype=np.float64)[:, None, None]
        n1 = np.arange(128, dtype=np.float64)[None, :, None]
        pp = np.arange(128, dtype=np.float64)[None, None, :]
        if smp:
            th = 2 * np.pi * (n2 * pp / 8192.0 + n1 * pp / 128.0)
            mre, mim = np.cos(th), -np.sin(th)
            nseq = 8192.0
        else:
            th = 2 * np.pi * (n2 * (pp % 64) / 4096.0 + (n1 % 64) * (pp % 64) / 64.0)
            same = ((n1 // 64) == (pp // 64)).astype(np.float64)
            mre, mim = np.cos(th) * same, -np.sin(th) * same
            nseq = 4096.0
        t1 = np.concatenate([mre, mim], axis=2).astype(ml_dtypes.bfloat16)
        a_ = np.arange(64, dtype=np.float64)
        th2 = 2 * np.pi * a_[:, None] * a_[None, :] / 64.0
        sc = 1.0 / np.sqrt(64.0 * nseq)
        C2, S2 = np.cos(th2) * sc, np.sin(th2) * sc
        t2 = np.block([[C2, -S2], [S2, C2]]).astype(ml_dtypes.bfloat16)
        tabs[smp] = (t1, t2)
    maps = []
    for r in range(8):
        b, k = r % 4, r // 4
        smp = is_sample[b]
        m = dict(shared)
        m["x"] = np.ascontiguousarray(blocks_x[b][k * TO:(k + 1) * TO])
        c1 = blocks_c[b][k]
        c2 = np.stack([c1, c1])
        m["cT"] = np.ascontiguousarray(c2.reshape(2, 8, 128).transpose(2, 1, 0).reshape(128, 16))
        m["rope"] = np.ascontiguousarray(rope_s[k * TO:(k + 1) * TO]) if smp else rope_p1
        fl = np.zeros((128, 16), f)
        own = [1.0 if k == 0 else 0.0, 1.0 if k == 1 else 0.0]
        fS, fP = (1.0, 0.0) if smp else (0.0, 1.0)
        fl[:, 0], fl[:, 1], fl[:, 2], fl[:, 3] = fS * own[0], fS * own[1], fP * own[0], fP * own[1]
        fl[:, 4], fl[:, 5] = own[0], own[1]
        fl[:, 6] = 1.0 if smp else 0.0
        fl[:, 7] = 1.0 if (smp or k == 0) else 0.0
        fl[:, 8] = 1.0 if (smp or k == 1) else 0.0
        m["flags"] = fl
        m["tab1"], m["tab2"] = tabs[smp]
        maps.append(m)
    return maps


_CACHE = {}


def kernel(**inputs):
    maps = _host_layout(inputs)
    if "nc" not in _CACHE:
        _CACHE["nc"] = build()[0]
    nc = _CACHE["nc"]
    res = run_bass_kernel_spmd(nc, maps, core_ids=list(range(8)))
    ys = [np.asarray(res.results[i]["y"], np.float32) for i in range(8)]
    y_prompt = np.stack([ys[0], ys[4], ys[1], ys[5]], axis=0)
    y_sample = np.stack([np.concatenate([ys[2], ys[6]], 0), np.concatenate([ys[3], ys[7]], 0)], axis=0)
    return (y_prompt, y_sample)
```

```python
from contextlib import ExitStack
import numpy as np
import ml_dtypes
import concourse.bass as bass
import concourse.mybir as mybir
from concourse.bass_utils import run_bass_kernel_spmd

F32 = mybir.dt.float32
BF16 = mybir.dt.bfloat16
AF = mybir.ActivationFunctionType
ALU = mybir.AluOpType
AX = mybir.AxisListType

T = 8192
TO = 4096
D = 1024
DIN = 3360
DMIX = 1280
NTILE = T // 128
NSUP = TO // 512
NTO = TO // 128
EPS = 1e-6
NPRM = 40
SIDE_RATIO = 1.5
NBC = 3200
BC_ADABG, BC_POSTG, BC_QKG, BC_SGUG, BC_ONG = 0, 1024, 2048, 2688, 2944
P_ADAB, P_PREG, P_BGF, P_BGB, P_SGUB = 0, 24, 32, 33, 34
GC_MF, GC_MB, GC_HM, GC_BD, GC_SM, GC_N = 0, 64, 128, 132, 388, 388 + 2048


class Buf:
    __slots__ = ("name", "w", "r")

    def __init__(self, name):
        self.name = name
        self.w = {}
        self.r = {}


class Tl:
    def __init__(self, t, name):
        self.t = t
        self.b = Buf(name)

    def __getitem__(self, k):
        return self.t[k]


class Sched:
    EPOCH = 30000

    def __init__(self, nc):
        self.nc = nc
        self.E = {"pe": nc.tensor, "act": nc.scalar, "dve": nc.vector, "pool": nc.gpsimd, "sp": nc.sync}
        self.esem = {}
        for e in ("pe", "act", "dve", "pool"):
            self.esem[e] = [nc.alloc_semaphore("es_%s_0" % e), 0, 0]
        self.seen = {e: {} for e in self.E}
        self.dsem = {}
        self.free_ds = []
        self.psum_names = set()
        self.watch = None
        self.pending = False
        self.csem = {}
        self.nds = 0
        self.nins = 0

    def sb(self, name, shape, dt):
        return Tl(self.nc.alloc_sbuf_tensor(name, list(shape), dt), name)

    def ps(self, name, shape, dt):
        return Tl(self.nc.alloc_psum_tensor(name, list(shape), dt), name)

    def _wait(self, e, need):
        for k, (h, v) in need.items():
            if self.seen[e].get(k, 0) < v:
                self.E[e].wait_ge(h, v)
                self.seen[e][k] = v

    @staticmethod
    def _add(need, k, hv):
        if k not in need or need[k][1] < hv[1]:
            need[k] = hv

    def op(self, e, fn, r=(), w=(), pw=()):
        need = {}
        for b in r:
            for k, hv in b.w.items():
                self._add(need, k, hv)
            if b.name in self.psum_names:
                for k, hv in b.r.items():
                    if k[0] != e:
                        self._add(need, k, hv)
        for b in list(w) + list(pw):
            for k, hv in b.r.items():
                if k[0] != e:
                    self._add(need, k, hv)
            for k, hv in b.w.items():
                if k[0] != e:
                    self._add(need, k, hv)
        self._wait(e, need)
        ins = fn()
        if self.watch is not None:
            if e == "pe" and any(b is self.watch for b in list(w) + list(pw)):
                self.pending = True
            elif e != "pe" and any(b is self.watch for b in r):
                self.pending = False
        es = self.esem[e]
        if es[1] >= self.EPOCH:
            es[2] += 1
            es[0] = self.nc.alloc_semaphore("es_%s_%d" % (e, es[2]))
            es[1] = 0
        es[1] += 1
        ins.then_inc(es[0], 1)
        key = (e, es[2])
        hv = (es[0], es[1])
        for b in r:
            b.r[key] = hv
        for b in w:
            b.w = {key: hv}
            b.r = {}
        for b in pw:
            b.w[key] = hv
        self.nins += 1
        return ins

    def dma(self, out, in_, r=(), w=(), pw=(), sem="d", q="sp"):
        if sem not in self.dsem:
            if self.free_ds:
                self.dsem[sem] = self.free_ds.pop()
            else:
                self.nds += 1
                self.dsem[sem] = [self.nc.alloc_semaphore("ds_%d" % self.nds), 0, self.nds]
        ds = self.dsem[sem]
        key = ("dma", ds[2])
        need = {}
        if ds[1] > 0:
            need[key] = (ds[0], ds[1])
        for b in r:
            for k, hv in b.w.items():
                self._add(need, k, hv)
        for b in w:
            for k, hv in b.r.items():
                self._add(need, k, hv)
            for k, hv in b.w.items():
                self._add(need, k, hv)
        for b in pw:
            for k, hv in b.r.items():
                self._add(need, k, hv)
            for k, hv in b.w.items():
                if k[0] != "dma":
                    self._add(need, k, hv)
        self._wait(q, need)
        ins = self.E[q].dma_start(out=out, in_=in_)
        ds[1] += 16
        ins.then_inc(ds[0], 16)
        hv = (ds[0], ds[1])
        for b in r:
            b.r[key] = hv
        for b in w:
            b.w = {key: hv}
            b.r = {}
        for b in pw:
            b.w[key] = hv
        self.nins += 1
        return ins

    def allgather(self, src, dst, rg, name):
        if name not in self.csem:
            self.csem[name] = [self.nc.alloc_semaphore("cs_" + name), 0]
        ds = self.csem[name]
        key = ("cc", name)
        need = {}
        if ds[1] > 0:
            need[key] = (ds[0], ds[1])
        for k, hv in src.b.w.items():
            self._add(need, k, hv)
        for k, hv in list(dst.b.r.items()) + list(dst.b.w.items()):
            self._add(need, k, hv)
        self._wait("pool", need)
        ins = self.nc.gpsimd.collective_compute("AllGather", ALU.bypass, replica_groups=rg,
                                                ins=[src.h.ap().opt()], outs=[dst.h.ap().opt()])
        ds[1] += 1
        ins.then_inc(ds[0])
        hv = (ds[0], ds[1])
        src.b.r[key] = hv
        dst.b.w = {key: hv}
        dst.b.r = {}
        self.nins += 1

    def barrier(self):
        need = {}
        for e, es in self.esem.items():
            if es[1] > 0:
                need[(e, es[2])] = (es[0], es[1])
        for name, ds in self.dsem.items():
            if ds[1] > 0:
                need[("dma", ds[2])] = (ds[0], ds[1])
        for e in self.E:
            self._wait(e, need)
        self.free_ds.extend(self.dsem.values())
        self.dsem = {}

    def finish(self, bufs):
        need = {}
        for b in bufs:
            for k, hv in b.b.w.items():
                self._add(need, k, hv)
        self._wait("sp", need)


def build(dbg=()):
    nc = bass.Bass("TRN2", target_bir_lowering=False)
    S = Sched(nc)
    V, A, P, G = nc.vector, nc.scalar, nc.tensor, nc.gpsimd

    def din(name, shape, dt=F32):
        return Tl(nc.dram_tensor(name, list(shape), dt, kind="ExternalInput").ap(), name)

    def dscr(name, shape, dt):
        kind = "ExternalOutput" if name in dbg else "Internal"
        return Tl(nc.dram_tensor(name, list(shape), dt, kind=kind).ap(), name)

    x_in = din("x", [TO, D])
    cT_in = din("cT", [128, 16])
    adaw_in = din("ada_w", [2, D, 3 * D])
    win_in = din("w_in", [2, D, DIN])
    wout_in = din("w_out", [2, DMIX, D])
    fnet_in = din("fnet_w", [2, 256, 256])
    sguw_in = din("sgu_wT", [2, 128, 512])
    wg2_in = din("wg2", [2, 16, 256])
    prm_in = din("prm", [2, 128, NPRM])
    bc_in = din("bc", [2, 128, NBC])
    rope_in = din("rope", [TO, 64])
    flags_in = din("flags", [128, 16])
    gconst_in = din("gconst", [128, GC_N])
    tab1_in = din("tab1", [64, 128, 256], BF16)
    tab2_in = din("tab2", [128, 128], BF16)
    cs64_in = din("cs64", [128, 2, 512])
    y_out = Tl(nc.dram_tensor("y", [TO, D], F32, kind="ExternalOutput").ap(), "y")

    SROWS = [2048, 2048, 2048, 2048, 256]
    SD, RV = [], []
    for i, rws in enumerate(SROWS):
        h = nc.dram_tensor("SD%d" % i, [rws, 512], BF16)
        t_ = Tl(h.ap(), "SD%d" % i)
        t_.h = h
        SD.append(t_)
        h = nc.dram_tensor("RV%d" % i, [2 * rws, 512], BF16)
        t_ = Tl(h.ap(), "RV%d" % i)
        t_.h = h
        RV.append(t_)
    RG = [[0, 4], [1, 5], [2, 6], [3, 7]]

    def xviews(B, j):
        o = [j * r for r in SROWS]
        v = {}
        v["vv"] = B[2][o[2]:o[2] + 1024, :].rearrange("r (a c) -> (r a) c", a=4)
        v["kt"] = B[2][o[2] + 1024:o[2] + 2048, :].rearrange("(d g tt) c -> d g (tt c)", d=64, g=2, tt=8)
        v["gqt"] = B[3][o[3]:o[3] + 1024, :].rearrange("(f tt) c -> f (tt c)", tt=8)
        v["gkt"] = B[3][o[3] + 1024:o[3] + 2048, :].rearrange("(f tt) c -> f (tt c)", tt=8)
        v["lrf"] = B[4][o[4]:o[4] + 128, :].rearrange("(f tt) c -> f (tt c)", tt=8)
        v["lrb"] = B[4][o[4] + 128:o[4] + 256, :].rearrange("(f tt) c -> f (tt c)", tt=8)
        return v

    def avu_rows(B, j, t0, n):
        i = 0 if t0 < 2048 else 1
        r0 = j * 2048 + (t0 % 2048)
        return B[i], B[i][r0:r0 + n, :]

    SV = xviews(SD, 0)
    RVV = [xviews(RV, 0), xviews(RV, 1)]
    QT = dscr("QT", [64, 8, TO], BF16)
    ZS = dscr("ZS", [TO, DMIX], BF16)
    MX = dscr("MX", [TO, DMIX], BF16)
    GO = dscr("GO", [T, 256], BF16)
    MXF = dscr("MXF", [T, 256], BF16)
    FP = dscr("FP", [T, 256], BF16)
    OF = dscr("OF", [T, 256], F32)
    G1 = dscr("G1", [128, 2, 64, 256], BF16)
    Y1 = dscr("Y1", [TO, D], F32)

    ident = S.sb("ident", [128, 128], BF16)
    identf = S.sb("identf", [128, 128], F32)
    flags = S.sb("flags_sb", [128, 16], F32)
    prm = S.sb("prm_sb", [128, NPRM], F32)
    bc = S.sb("bc_sb", [128, NBC], F32)
    gmod = S.sb("gmod", [128, 32], F32)
    pgt = [S.sb("pgt%d" % h, [128, D], F32) for h in range(2)]
    stat = S.sb("stat", [128, 64], F32)

    PD = [nc.alloc_psum_tensor("PD%d" % i, [128, 1024], F32) for i in range(3)]
    PS_ = [nc.alloc_psum_tensor("PS%d" % i, [128, 512], F32) for i in range(2)]
    pb = [Tl(PD[0][:, 0:512], "pb0"), Tl(PD[0][:, 512:1024], "pb1"), Tl(PD[1][:, 0:512], "pb2"),
          Tl(PD[1][:, 512:1024], "pb3"), Tl(PD[2][:, 0:512], "pb4")]
    b16 = lambda t_: t_.bitcast(BF16).rearrange("p (c t) -> p c t", t=128)
    ptr = Tl(b16(PD[2][:, 512:1024]), "ptr")
    ptq = Tl(b16(PS_[0][:, :]), "ptq")
    ptk = Tl(b16(PS_[1][:, :]), "ptk")
    sideT = Tl(PS_[1][:, :], "sideT")
    sideT.b = ptk.b
    S.psum_names = {t_.b.name for t_ in pb + [ptr, ptq, ptk]}
    S.watch = ptk.b
    pbi = [0]

    def bank():
        pbi[0] += 1
        return pb[pbi[0] % 5]

    S.op("pool", lambda: G.memset(ident[:], 1.0), w=[ident.b])
    S.op("pool", lambda: G.affine_select(out=ident[:], in_=ident[:], pattern=[[-1, 128]], compare_op=ALU.is_equal,
                                        fill=0.0, base=0, channel_multiplier=1), r=[ident.b], w=[ident.b])
    S.op("pool", lambda: G.memset(identf[:], 1.0), w=[identf.b])
    S.op("pool", lambda: G.affine_select(out=identf[:], in_=identf[:], pattern=[[-1, 128]], compare_op=ALU.is_equal,
                                        fill=0.0, base=0, channel_multiplier=1), r=[identf.b], w=[identf.b])
    S.dma(flags[:], flags_in[:, :], r=[flags_in.b], w=[flags.b], sem="misc")

    mhalf = S.sb("mhalf", [128, 64], F32)
    S.op("pool", lambda: G.memset(mhalf[:], -0.5), w=[mhalf.b])

    def rstd_chain(src, dst, n, inv_n):
        S.op("dve", lambda: V.tensor_scalar(out=stat[:, dst:dst + n], in0=stat[:, src:src + n], scalar1=inv_n,
                                            scalar2=EPS, op0=ALU.mult, op1=ALU.add), r=[stat.b], w=[stat.b])
        S.op("pool", lambda: G.tensor_tensor(out=stat[:, dst:dst + n], in0=stat[:, dst:dst + n], in1=mhalf[:, 0:n], op=ALU.pow),
             r=[stat.b, mhalf.b], w=[stat.b])

    def phase0(l, kc_hook=None):
        with ExitStack() as es:
            al = lambda n, s, d: Tl(es.enter_context(nc.sbuf_tensor(n + "_L%d" % l, list(s), d)), n)
            cs = al("cs_sb", [128, 16], F32)
            csrep = al("csrep", [128, 8, 2, 128], F32)
            ad = [al("adst0", [128, 3 * D], F32), al("adst1", [128, 3 * D], F32)]
            S.dma(prm[:], prm_in[l], r=[prm_in.b], w=[prm.b], sem="misc")
            S.dma(bc[:], bc_in[l], r=[bc_in.b], w=[bc.b], sem="misc")
            S.dma(cs[:], cT_in[:, :], r=[cT_in.b], w=[cs.b], sem="misc")
            S.op("act", lambda: A.activation(out=cs[:], in_=cs[:], func=AF.Silu), r=[cs.b], w=[cs.b])
            csv = cs[:].rearrange("p (k h) -> p k h", h=2)
            S.op("dve", lambda: V.tensor_copy(out=csrep[:], in_=csv.unsqueeze(3).to_broadcast([128, 8, 2, 128])),
                 r=[cs.b], w=[csrep.b])
            pm = pb[0]
            pg = [pb[1], pb[2], pb[3], pb[4]]
            S.op("dve", lambda: V.memset(pm[:], 0.0), w=[pm.b])
            for kc in range(8):
                if kc_hook is not None:
                    kc_hook(kc)
                a = ad[kc % 2]
                S.dma(a[:], adaw_in[l, kc * 128:(kc + 1) * 128, :], r=[adaw_in.b], w=[a.b], sem="ad%d" % (kc % 2))
                for j in range(16):
                    S.op("pe", lambda: P.matmul(out=pm[:, j * 2:(j + 1) * 2], lhsT=a[:, j * 128:(j + 1) * 128],
                                               rhs=cs[:, kc * 2:(kc + 1) * 2], start=False, stop=(kc == 7),
                                               skip_group_check=True), r=[a.b, cs.b], pw=[pm.b])
                for h in range(2):
                    for nb in range(2):
                        S.op("pe", lambda: P.matmul(out=pg[h * 2 + nb][:, :], lhsT=csrep[:, kc, h, :],
                                                   rhs=a[:, 2048 + nb * 512:2048 + (nb + 1) * 512],
                                                   start=(kc == 0), stop=(kc == 7)),
                             r=[a.b, csrep.b], **({"w": [pg[h * 2 + nb].b]} if kc == 0 else {"pw": [pg[h * 2 + nb].b]}))
            pmv = pm[:, 0:32].rearrange("p (j h) -> p j h", h=2)
            for h in range(2):
                S.op("dve", lambda: V.tensor_tensor(out=gmod[:, h * 16 + 8:h * 16 + 16], in0=pmv[:, 0:8, h],
                                                    in1=prm[:, P_ADAB:P_ADAB + 8], op=ALU.add),
                     r=[pm.b, prm.b], pw=[gmod.b])
                S.op("dve", lambda: V.tensor_tensor(out=gmod[:, h * 16:h * 16 + 8], in0=pmv[:, 8:16, h],
                                                    in1=prm[:, P_ADAB + 8:P_ADAB + 16], op=ALU.add),
                     r=[pm.b, prm.b], pw=[gmod.b])
                S.op("dve", lambda: V.scalar_tensor_tensor(out=gmod[:, h * 16:h * 16 + 8], in0=gmod[:, h * 16:h * 16 + 8],
                                                           scalar=1.0, in1=prm[:, P_PREG:P_PREG + 8],
                                                           op0=ALU.add, op1=ALU.mult),
                     r=[gmod.b, prm.b], pw=[gmod.b])
                for nb in range(2):
                    sl = slice(nb * 512, (nb + 1) * 512)
                    S.op("dve", lambda: V.tensor_tensor(out=pgt[h][:, sl], in0=pg[h * 2 + nb][:, :],
                                                        in1=bc[:, BC_ADABG + nb * 512:BC_ADABG + (nb + 1) * 512],
                                                        op=ALU.add), r=[pg[h * 2 + nb].b, bc.b], pw=[pgt[h].b])
                    S.op("pool", lambda: G.tensor_tensor(out=pgt[h][:, sl], in0=pgt[h][:, sl],
                                                         in1=bc[:, BC_POSTG + nb * 512:BC_POSTG + (nb + 1) * 512],
                                                         op=ALU.mult), r=[pgt[h].b, bc.b], pw=[pgt[h].b])

    def phaseA(l, xsrc, win):
        with ExitStack() as es:
            al = lambda n, s, d: Tl(es.enter_context(nc.sbuf_tensor(n + "_L%d" % l, list(s), d)), n)
            xa = [al("xa0", [128, 4, D], F32), al("xa1", [128, 4, D], F32)]
            junk = al("junk", [128, D], F32)
            xb = [al("xb%d" % i, [128, D], BF16) for i in range(4)]
            hT = [al("hT0", [128, 8, 512], BF16), al("hT1", [128, 8, 512], BF16)]
            fm = [al("fm%d" % i, [128, 512], BF16) for i in range(4)]
            avu = [al("avu0", [128, 512], BF16), al("avu1", [128, 512], BF16)]
            qk = [al("qk0", [128, 640], F32), al("qk1", [128, 640], F32)]
            rt = [al("rt%d" % i, [128, 320], F32) for i in range(4)]
            qkr = [al("qkr0", [128, 640], BF16), al("qkr1", [128, 640], BF16)]
            qkT = [al("qkT0", [64, 10, 128], BF16), al("qkT1", [64, 10, 128], BF16)]
            vt = [al("vt0", [128, 128], BF16), al("vt1", [128, 128], BF16)]
            vn = [al("vn0", [128, 256], BF16), al("vn1", [128, 256], BF16)]
            sgt = al("sgt", [128, 256], F32)
            od = [al("od0", [128, 256], BF16), al("od1", [128, 256], BF16)]
            zs = [al("zs0", [128, DMIX], BF16), al("zs1", [128, DMIX], BF16)]
            wsT = al("wsT", [128, 4, 128], BF16)
            wsTf = al("wsTf", [128, 512], F32)
            S.dma(wsTf[:], sguw_in[l], r=[sguw_in.b], w=[wsTf.b], sem="misc")
            S.op("dve", lambda: V.tensor_copy(out=wsT[:].rearrange("p g t -> p (g t)"), in_=wsTf[:]), r=[wsTf.b], w=[wsT.b])

            stP = al("stP", [128, 8], F32)
            stQ = [al("stQ0", [128, 32], F32), al("stQ1", [128, 32], F32)]
            stS = [al("stS0", [128, 4], F32), al("stS1", [128, 4], F32)]
            junkQ = [al("junkQ0", [128, 640], F32), al("junkQ1", [128, 640], F32)]
            junkS = al("junkS", [128, 256], F32)
            cr = [al("cr0", [128, 512], F32), al("cr1", [128, 512], F32)]
            dvr = [al("dvr0", [128, 256], F32), al("dvr1", [128, 256], F32)]
            rp4 = [al("rp4%d" % i, [128, 4, 64], F32) for i in range(4)]
            gG = gmod[:, 0:8].unsqueeze(2).to_broadcast([128, 8, 128])
            gS = gmod[:, 8:16].unsqueeze(2).to_broadcast([128, 8, 128])

            def rstd2(t_, src_c, dst_c, n, inv_n):
                S.op("dve", lambda: V.tensor_scalar(out=t_[:, dst_c:dst_c + n], in0=t_[:, src_c:src_c + n], scalar1=inv_n,
                                                    scalar2=EPS, op0=ALU.mult, op1=ALU.add), r=[t_.b], w=[t_.b])
                S.op("pool", lambda: G.tensor_tensor(out=t_[:, dst_c:dst_c + n], in0=t_[:, dst_c:dst_c + n], in1=mhalf[:, 0:n],
                                                     op=ALU.pow), r=[t_.b, mhalf.b], w=[t_.b])

            def load_x(st):
                S.dma(xa[st % 2][:], xsrc[st * 512:(st + 1) * 512, :].rearrange("(a p) d -> p a d", p=128),
                      r=[xsrc.b], w=[xa[st % 2].b], sem="xa%d" % (st % 2))
                if "A_norope" not in dbg:
                    S.dma(rp4[st % 4][:], rope_in[st * 512:(st + 1) * 512, :].rearrange("(a p) c -> p a c", p=128),
                          r=[rope_in.b], w=[rp4[st % 4].b], sem="rp%d" % (st % 4))

            def prepA(st):
                xt = xa[st % 2]
                for a in range(4):
                    S.op("act", lambda: A.activation(out=junk[:], in_=xt[:, a, :], func=AF.Square,
                                                     accum_out=stP[:, a:a + 1]), r=[xt.b], w=[junk.b], pw=[stP.b])
                rstd2(stP, 0, 4, 4, 1.0 / D)
                for a in range(4):
                    xb_ = xb[a]
                    S.op("dve", lambda: V.tensor_scalar(out=xb_[:], in0=xt[:, a, :], scalar1=stP[:, 4 + a:5 + a],
                                                        scalar2=None, op0=ALU.mult), r=[xt.b, stP.b], w=[xb_.b])

            def prepB(st):
                h_ = hT[st % 2]
                for a in range(4):
                    xb_ = xb[a]
                    for c in range(8):
                        S.op("pe", lambda: P.transpose(out=ptr[:, c, :], in_=xb_[:, c * 128:(c + 1) * 128],
                                                      identity=ident[:]), r=[xb_.b, ident.b],
                             **({"w": [ptr.b]} if c == 0 else {"pw": [ptr.b]}))
                    S.op("dve", lambda: V.tensor_tensor(out=h_[:, :, a * 128:(a + 1) * 128], in0=ptr[:], in1=gG,
                                                        op=ALU.mult), r=[ptr.b, gmod.b], pw=[h_.b])
                    S.op("pool", lambda: G.tensor_tensor(out=h_[:, :, a * 128:(a + 1) * 128],
                                                         in0=h_[:, :, a * 128:(a + 1) * 128], in1=gS, op=ALU.add),
                         r=[h_.b, gmod.b], pw=[h_.b])

            def fm_groups(ti):
                st, a = divmod(ti, 4)
                h_ = hT[st % 2]
                tsl = slice(st * 512, (st + 1) * 512)
                for gi, (c0, m, dst, dstT) in enumerate(((0, 128, SV["gqt"], SD[3]), (128, 128, SV["gkt"], SD[3]),
                                                         (512, 16, SV["lrf"], SD[4]), (528, 16, SV["lrb"], SD[4]))):
                    bk = pb[1 + gi]
                    for kc in range(8):
                        S.op("pe", lambda: P.matmul(out=bk[0:m, :], lhsT=win[:, kc, c0:c0 + m], rhs=h_[:, kc, :],
                                                   start=(kc == 0), stop=(kc == 7)), r=[win.b, h_.b],
                             **({"w": [bk.b]} if kc == 0 else {"pw": [bk.b]}))
                    f_ = fm[gi]
                    S.op("act", lambda: A.copy(out=f_[0:m, :], in_=bk[0:m, :]), r=[bk.b], w=[f_.b])
                    S.dma(dst[0:m, tsl], f_[0:m, :], r=[f_.b], pw=[dstT.b], sem="fm%d" % gi)

            def mmgroup(ti, bk, col_lo, col_hi, out_lo, first):
                st, a = divmod(ti, 4)
                h_ = hT[st % 2]
                for kc in range(8):
                    S.op("pe", lambda: P.matmul(out=bk[:, out_lo:out_lo + (col_hi - col_lo)], lhsT=h_[:, kc, a * 128:(a + 1) * 128],
                                               rhs=win[:, kc, col_lo:col_hi], start=(kc == 0), stop=(kc == 7)),
                         r=[win.b, h_.b], **({"w": [bk.b]} if (kc == 0 and first) else {"pw": [bk.b]}))

            def stageXM(ti):
                mmgroup(ti, pb[0], 256, 512, 0, True)
                mmgroup(ti, pb[0], 544, 800, 256, False)
                mmgroup(ti, pb[1], 800, 1312, 0, True)
                mmgroup(ti, pb[2], 1312, 1824, 0, True)
                mmgroup(ti, pb[3], 1824, 2336, 0, True)
                mmgroup(ti, pb[4], 2336, 2848, 0, True)

            def stageXE(ti):
                st, a = divmod(ti, 4)
                rows = slice(ti * 128, (ti + 1) * 128)
                bA, bB, bC, bD, bE = pb
                av_ = avu[ti % 2]
                S.op("act", lambda: A.copy(out=av_[:], in_=bA[:, :]), r=[bA.b], w=[av_.b])
                avT, avv = avu_rows(SD, 0, ti * 128, 128)
                S.dma(avv, av_[:], r=[av_.b], pw=[avT.b], sem="avu%d" % (ti % 2))
                qk_ = qk[ti % 2]
                S.op("act", lambda: A.copy(out=qk_[:, 0:512], in_=bB[:, :]), r=[bB.b], w=[qk_.b])
                S.op("dve", lambda: V.tensor_copy(out=qk_[:, 512:640], in_=bC[:, 0:128]), r=[bC.b], pw=[qk_.b])
                cr_ = cr[ti % 2]
                S.op("dve", lambda: V.tensor_copy(out=cr_[:, 128:512], in_=bC[:, 128:512]), r=[bC.b], w=[cr_.b])
                vt_ = vt[ti % 2]
                S.op("pool", lambda: G.tensor_copy(out=vt_[:], in_=cr_[:, 128:256]), r=[cr_.b], w=[vt_.b])
                S.dma(SV["vv"][rows, :], vt_[:], r=[vt_.b], pw=[SD[2].b], sem="vt%d" % (ti % 2))
                dv_ = dvr[ti % 2]
                S.op("act", lambda: A.copy(out=dv_[:], in_=bD[:, 0:256]), r=[bD.b], w=[dv_.b])
                zs_ = zs[ti % 2]
                S.op("act", lambda: A.activation(out=zs_[:, 0:256], in_=bD[:, 256:512], func=AF.Silu), r=[bD.b], w=[zs_.b])
                S.op("act", lambda: A.activation(out=zs_[:, 256:768], in_=bE[:, :], func=AF.Silu), r=[bE.b], pw=[zs_.b])
                mmgroup(ti, pb[0], 2848, 3360, 0, True)
                S.op("act", lambda: A.activation(out=zs_[:, 768:1280], in_=pb[0][:, :], func=AF.Silu), r=[pb[0].b], pw=[zs_.b])
                S.dma(ZS[rows, :], zs_[:], r=[zs_.b], pw=[ZS.b], sem="zs%d" % (ti % 2))
                if a == 0:
                    fm_groups(ti)

            def stageY(ti):
                st, a = divmod(ti, 4)
                tsl = slice(st * 512, (st + 1) * 512)
                rows = slice(ti * 128, (ti + 1) * 128)
                cr_ = cr[ti % 2]
                dv_ = dvr[ti % 2]
                ss_ = stS[ti % 2]
                vn_ = vn[ti % 2]
                S.op("act", lambda: A.activation(out=junkS[:], in_=dv_[:], func=AF.Square, accum_out=ss_[:, 0:1]),
                     r=[dv_.b], w=[junkS.b], pw=[ss_.b])
                rstd2(ss_, 0, 1, 1, 1.0 / 256)
                S.op("dve", lambda: V.scalar_tensor_tensor(out=vn_[:], in0=dv_[:], scalar=ss_[:, 1:2],
                                                           in1=bc[:, BC_SGUG:BC_SGUG + 256], op0=ALU.mult, op1=ALU.mult),
                     r=[dv_.b, ss_.b, bc.b], w=[vn_.b])
                qT_ = qkT[ti % 2]
                qk_ = qk[ti % 2]
                sq_ = stQ[ti % 2]
                jq_ = junkQ[ti % 2]
                S.op("act", lambda: A.activation(out=jq_[:], in_=qk_[:], func=AF.Square), r=[qk_.b], w=[jq_.b])
                S.op("dve", lambda: V.tensor_reduce(out=sq_[:, 0:10], in_=jq_[:].rearrange("p (h d) -> p h d", d=64),
                                                    op=ALU.add, axis=AX.X), r=[jq_.b], w=[sq_.b])
                rstd2(sq_, 0, 10, 10, 1.0 / 64)
                S.op("dve", lambda: V.tensor_tensor(out=qk_[:].rearrange("p (h d) -> p h d", d=64),
                                                    in0=qk_[:].rearrange("p (h d) -> p h d", d=64),
                                                    in1=sq_[:, 10:20].unsqueeze(2).to_broadcast([128, 10, 64]),
                                                    op=ALU.mult), r=[qk_.b, sq_.b], w=[qk_.b])
                S.op("dve", lambda: V.tensor_tensor(out=qk_[:], in0=qk_[:], in1=bc[:, BC_QKG:BC_QKG + 640],
                                                    op=ALU.mult), r=[qk_.b, bc.b], w=[qk_.b])
                rp_ = rp4[st % 4]
                qv = qk_[:].rearrange("p (h i two) -> p h i two", h=10, two=2)
                x0, x1 = qv[:, :, :, 0], qv[:, :, :, 1]
                cosb = rp_[:, a, 0:32].unsqueeze(1).to_broadcast([128, 10, 32])
                sinb = rp_[:, a, 32:64].unsqueeze(1).to_broadcast([128, 10, 32])
                qr_ = qkr[ti % 2]
                ov = qr_[:].rearrange("p (h i two) -> p h i two", h=10, two=2)
                r3 = lambda t_: t_[:].rearrange("p (h i) -> p h i", h=10)
                S.op("dve", lambda: V.tensor_tensor(out=r3(rt[0]), in0=x0, in1=cosb, op=ALU.mult), r=[qk_.b, rp_.b], w=[rt[0].b])
                S.op("pool", lambda: G.tensor_tensor(out=r3(rt[1]), in0=x1, in1=sinb, op=ALU.mult), r=[qk_.b, rp_.b], w=[rt[1].b])
                S.op("pool", lambda: G.tensor_tensor(out=r3(rt[2]), in0=x0, in1=sinb, op=ALU.mult), r=[qk_.b, rp_.b], w=[rt[2].b])
                S.op("dve", lambda: V.tensor_tensor(out=r3(rt[3]), in0=x1, in1=cosb, op=ALU.mult), r=[qk_.b, rp_.b], w=[rt[3].b])
                S.op("dve", lambda: V.tensor_tensor(out=ov[:, :, :, 0], in0=r3(rt[0]), in1=r3(rt[1]), op=ALU.subtract),
                     r=[rt[0].b, rt[1].b], w=[qr_.b])
                S.op("pool", lambda: G.tensor_tensor(out=ov[:, :, :, 1], in0=r3(rt[2]), in1=r3(rt[3]), op=ALU.add),
                     r=[rt[2].b, rt[3].b], pw=[qr_.b])
                for j in range(10):
                    pt_ = ptq if j < 8 else ptk
                    jj = j if j < 8 else j - 8
                    S.op("pe", lambda: P.transpose(out=pt_[0:64, jj, :], in_=qr_[:, j * 64:(j + 1) * 64], identity=ident[:]),
                         r=[qr_.b, ident.b], **({"w": [pt_.b]} if jj == 0 else {"pw": [pt_.b]}))
                S.op("act", lambda: A.copy(out=qT_[:, 0:8, :], in_=ptq[0:64, :, :]), r=[ptq.b], w=[qT_.b])
                S.op("act", lambda: A.copy(out=qT_[:, 8:10, :], in_=ptk[0:64, 0:2, :]), r=[ptk.b], pw=[qT_.b])
                S.dma(QT[:, :, rows], qT_[:, 0:8, :], r=[qT_.b], pw=[QT.b], sem="qT%d" % (ti % 2))
                S.dma(SV["kt"][:, :, rows], qT_[:, 8:10, :], r=[qT_.b], pw=[SD[2].b], sem="kT%d" % (ti % 2))
                bS = sideT
                for g in range(4):
                    S.op("pe", lambda: P.matmul(out=bS[:, g * 64:(g + 1) * 64], lhsT=wsT[:, g, :], rhs=vn_[:, g * 64:(g + 1) * 64],
                                               start=True, stop=True), r=[wsT.b, vn_.b],
                         **({"w": [bS.b]} if g == 0 else {"pw": [bS.b]}))
                S.op("dve", lambda: V.tensor_tensor(out=sgt[:].rearrange("p (g c) -> p g c", c=64),
                                                    in0=bS[:, 0:256].rearrange("p (g c) -> p g c", c=64),
                                                    in1=prm[:, P_SGUB:P_SGUB + 4].unsqueeze(2).to_broadcast([128, 4, 64]),
                                                    op=ALU.add), r=[bS.b, prm.b], w=[sgt.b])
                od_ = od[ti % 2]
                S.op("dve", lambda: V.tensor_tensor(out=od_[:], in0=sgt[:], in1=cr_[:, 256:512], op=ALU.mult),
                     r=[sgt.b, cr_.b], w=[od_.b])
                S.dma(MX[rows, 1024:1280], od_[:], r=[od_.b], pw=[MX.b], sem="od%d" % (ti % 2))

            load_x(0)
            load_x(1)
            prepA(0)
            prepB(0)
            if "A_preponly" in dbg:
                return
            for ti in range(NTO):
                st, a = divmod(ti, 4)
                stageXM(ti)
                if ti >= 1:
                    stageY(ti - 1)
                stageXE(ti)
                if ti == 15:
                    S.allgather(SD[0], RV[0], RG, "ag0")
                if a == 0 and st + 1 < NSUP:
                    prepA(st + 1)
                if a == 2 and st + 1 < NSUP:
                    prepB(st + 1)
                    if st + 2 < NSUP:
                        load_x(st + 2)
            stageY(NTO - 1)

    def phaseC(l, xsrc, ydst):
        NS = 4
        with ExitStack() as es:
            al = lambda n, s, d: Tl(es.enter_context(nc.sbuf_tensor(n + "_L%d" % l, list(s), d)), n)
            cm = [al("cm%d" % i, [128, DMIX], BF16) for i in range(NS)]
            cz = [al("cz%d" % i, [128, DMIX], BF16) for i in range(NS)]
            cf = [al("cf%d" % i, [128, 6, 256], BF16) for i in range(NS)]
            cx = [al("cx%d" % i, [128, D], F32) for i in range(NS)]
            cmz = [al("cmz%d" % i, [128, DMIX], BF16) for i in range(NS)]
            cmT = [al("cmT%d" % i, [128, 10, 128], BF16) for i in range(NS)]
            cjunk = al("cjunk", [128, 512], F32)
            cy = [al("cy%d" % i, [128, D], F32) for i in range(NS)]
            wout = al("wout", [128, 10, D], BF16)
            ws = [al("wsc0", [128, D], F32), al("wsc1", [128, D], F32)]
            for kc in range(10):
                w_ = ws[kc % 2]
                S.dma(w_[:, 0:D], wout_in[l, kc * 128:(kc + 1) * 128, :], r=[wout_in.b], w=[w_.b], sem="ws%d" % (kc % 2))
                eng = ("act", "dve", "dve")[kc % 3]
                if eng == "act":
                    S.op("act", lambda: A.copy(out=wout[:, kc, :], in_=w_[:, 0:D]), r=[w_.b], pw=[wout.b])
                elif eng == "pool":
                    S.op("pool", lambda: G.tensor_copy(out=wout[:, kc, :], in_=w_[:, 0:D]), r=[w_.b], pw=[wout.b])
                else:
                    S.op("dve", lambda: V.tensor_copy(out=wout[:, kc, :], in_=w_[:, 0:D]), r=[w_.b], pw=[wout.b])
            def loads(ti):
                s = ti % NS
                rows = slice(ti * 128, (ti + 1) * 128)
                S.dma(cm[s][:], MX[rows, :], r=[MX.b], w=[cm[s].b], sem="cm%d" % s)
                cands = [(MXF, 0), (MXF, 1), (FP, 0), (FP, 1), (GO, 0), (GO, 1)]
                for ci, (src_t, rg_) in enumerate(cands):
                    S.dma(cf[s][:, ci, :], src_t[rg_ * TO + ti * 128:rg_ * TO + (ti + 1) * 128, :], r=[src_t.b],
                          **({"w": [cf[s].b]} if ci == 0 else {"pw": [cf[s].b]}), sem="cf%d_%d" % (s, ci))
                S.dma(cz[s][:], ZS[rows, :], r=[ZS.b], w=[cz[s].b], sem="cz%d" % s)
                S.dma(cx[s][:], xsrc[rows, :], r=[xsrc.b], w=[cx[s].b], sem="cx%d" % s)

            cfts = [al("cft%d" % i, [128, 256], F32) for i in range(3)]
            cgts = [al("cgt%d" % i, [128, 256], F32) for i in range(3)]
            stC = [al("stC%d" % i, [128, 8], F32) for i in range(2)]
            byT = {}

            def front(ti):
                s = ti % NS
                cft_ = cfts[ti % 3]
                cgt_ = cgts[ti % 3]
                S.op("act", lambda: A.activation(out=cft_[:], in_=cf[s][:, 0, :], func=AF.Copy, scale=flags[:, 0:1]),
                     r=[cf[s].b, flags.b], w=[cft_.b])
                S.op("act", lambda: A.activation(out=cgt_[:], in_=cf[s][:, 4, :], func=AF.Copy, scale=flags[:, 4:5]),
                     r=[cf[s].b, flags.b], w=[cgt_.b])
                for ci in (1, 2):
                    S.op("dve", lambda: V.scalar_tensor_tensor(out=cft_[:], in0=cf[s][:, ci, :], scalar=flags[:, ci:ci + 1],
                                                               in1=cft_[:], op0=ALU.mult, op1=ALU.add),
                         r=[cf[s].b, cft_.b, flags.b], w=[cft_.b])
                S.op("dve", lambda: V.scalar_tensor_tensor(out=cm[s][:, 256:512], in0=cf[s][:, 3, :], scalar=flags[:, 3:4],
                                                           in1=cft_[:], op0=ALU.mult, op1=ALU.add),
                     r=[cf[s].b, cft_.b, flags.b], pw=[cm[s].b])
                S.op("dve", lambda: V.scalar_tensor_tensor(out=cm[s][:, 0:256], in0=cf[s][:, 5, :], scalar=flags[:, 5:6],
                                                           in1=cgt_[:], op0=ALU.mult, op1=ALU.add),
                     r=[cf[s].b, cgt_.b, flags.b], pw=[cm[s].b])
                S.op("dve", lambda: V.tensor_tensor(out=cmz[s][:], in0=cm[s][:], in1=cz[s][:], op=ALU.mult),
                     r=[cm[s].b, cz[s].b], w=[cmz[s].b])

            def mid1(ti):
                s = ti % NS
                for c in range(10):
                    pt_ = ptr if c < 8 else ptk
                    cc = c if c < 8 else c - 8
                    S.op("pe", lambda: P.transpose(out=pt_[:, cc, :], in_=cmz[s][:, c * 128:(c + 1) * 128], identity=ident[:]),
                         r=[cmz[s].b, ident.b], **({"w": [pt_.b]} if cc == 0 else {"pw": [pt_.b]}))
                S.op("act", lambda: A.copy(out=cmT[s][:, 0:8, :], in_=ptr[:]), r=[ptr.b], w=[cmT[s].b])
                S.op("act", lambda: A.copy(out=cmT[s][:, 8:10, :], in_=ptk[:, 0:2, :]), r=[ptk.b], pw=[cmT[s].b])

            def mid2(ti):
                s = ti % NS
                by = [pb[(2 * ti) % 4], pb[(2 * ti + 1) % 4]]
                byT[ti] = by
                for nb in range(2):
                    for kc in range(10):
                        S.op("pe", lambda: P.matmul(out=by[nb][:, :], lhsT=cmT[s][:, kc, :], rhs=wout[:, kc, nb * 512:(nb + 1) * 512],
                                                   start=(kc == 0), stop=(kc == 9)), r=[cmT[s].b, wout.b],
                             **({"w": [by[nb].b]} if kc == 0 else {"pw": [by[nb].b]}))

            def tail(ti):
                s = ti % NS
                rows = slice(ti * 128, (ti + 1) * 128)
                by = byT.pop(ti)
                st_ = stC[ti % 2]
                for nb in range(2):
                    S.op("act", lambda: A.activation(out=cjunk[:], in_=by[nb][:, :], func=AF.Square,
                                                     accum_out=st_[:, nb:nb + 1]), r=[by[nb].b], w=[cjunk.b], pw=[st_.b])
                S.op("dve", lambda: V.tensor_tensor(out=st_[:, 2:3], in0=st_[:, 0:1], in1=st_[:, 1:2], op=ALU.add),
                     r=[st_.b], w=[st_.b])
                S.op("dve", lambda: V.tensor_scalar(out=st_[:, 3:4], in0=st_[:, 2:3], scalar1=1.0 / D, scalar2=EPS,
                                                    op0=ALU.mult, op1=ALU.add), r=[st_.b], w=[st_.b])
                S.op("pool", lambda: G.tensor_tensor(out=st_[:, 3:4], in0=st_[:, 3:4], in1=mhalf[:, 0:1], op=ALU.pow),
                     r=[st_.b, mhalf.b], w=[st_.b])
                for nb in range(2):
                    sl = slice(nb * 512, (nb + 1) * 512)
                    S.op("dve", lambda: V.scalar_tensor_tensor(out=cy[s][:, sl], in0=by[nb][:, :], scalar=st_[:, 3:4],
                                                               in1=pgt[0][:, sl], op0=ALU.mult, op1=ALU.mult),
                         r=[by[nb].b, st_.b, pgt[0].b], **({"w": [cy[s].b]} if nb == 0 else {"pw": [cy[s].b]}))
                S.op("dve", lambda: V.tensor_tensor(out=cy[s][:], in0=cy[s][:], in1=cx[s][:], op=ALU.add),
                     r=[cy[s].b, cx[s].b], w=[cy[s].b])
                S.dma(ydst[rows, :], cy[s][:], r=[cy[s].b], pw=[ydst.b], sem="cy%d" % s)

            loads(0)
            loads(1)
            loads(2)
            front(0)
            front(1)
            mid1(0)
            for ti in range(NTO):
                if ti + 3 < NTO:
                    loads(ti + 3)
                if ti + 2 < NTO:
                    front(ti + 2)
                mid2(ti)
                if ti + 1 < NTO:
                    mid1(ti + 1)
                tail(ti)

    def phaseAttn(l, es, qtiles=None):
        if True:
            al = lambda n, s, d: Tl(es.enter_context(nc.sbuf_tensor(n + "_L%d" % l, list(s), d)), n)
            KTs = al("KTs", [128, T], BF16)
            Va = al("Va", [128, 64, 2, 66], BF16)
            aq = [al("aq0", [128, 8, 128], BF16), al("aq1", [128, 8, 128], BF16)]
            pT = [al("pT%d" % i, [128, 1024], BF16) for i in range(3)]
            oT = [al("oT0", [65, 512], F32), al("oT1", [65, 512], F32)]
            on = [al("on0", [128, 512], BF16), al("on1", [128, 512], BF16)]
            rden = al("rden", [128, 8], F32)
            for j in range(2):
                for g in range(2):
                    S.dma(KTs[g * 64:(g + 1) * 64, j * TO:(j + 1) * TO], RVV[j]["kt"][:, g, :], r=[RV[2].b], pw=[KTs.b],
                          sem="kts%d" % g)
            for q_ in aq:
                S.op("pool", lambda: G.memset(q_[:], 0.0), w=[q_.b])
            Vst = al("Vst", [128, 64, 128], BF16)
            for j in range(2):
                S.dma(Vst[:, j * 32:(j + 1) * 32, :], RVV[j]["vv"].rearrange("(k p) c -> p k c", p=128), r=[RV[2].b],
                      pw=[Vst.b], sem="va%d" % j)
            S.op("dve", lambda: V.memset(Va[:], 1.0), w=[Va.b])
            for g in range(2):
                S.op("dve", lambda: V.tensor_copy(out=Va[:, :, g, 0:64], in_=Vst[:, :, g * 64:(g + 1) * 64]), r=[Vst.b], pw=[Va.b])
            for j in range(2):
                S.op("dve", lambda: V.tensor_scalar(out=Va[:, j * 32:(j + 1) * 32, :, :].rearrange("p a b c -> p (a b c)"),
                                                    in0=Va[:, j * 32:(j + 1) * 32, :, :].rearrange("p a b c -> p (a b c)"),
                                                    scalar1=flags[:, 7 + j:8 + j], scalar2=None, op0=ALU.mult),
                     r=[Va.b, flags.b], pw=[Va.b])
            bsb = [[pb[0].b, pb[1].b], [pb[2].b, pb[3].b], [pb[4].b, ptr.b]]
            bo1 = Tl(PS_[0][:, :], "x")
            bo1.b = ptq.b
            bo = [bo1, bo1]
            bt = sideT
            qlist = list(range(NTO) if qtiles is None else qtiles)
            NP2 = NTILE // 2
            its = [(qn, qi, g, kp) for qn, qi in enumerate(qlist) for g in range(2) for kp in range(NP2)]

            def load_q(qn):
                qi = qlist[qn]
                q_ = aq[qn % 2]
                for g in range(2):
                    S.dma(q_[g * 64:(g + 1) * 64, g * 4:(g + 1) * 4, :], QT[:, g * 4:(g + 1) * 4, qi * 128:(qi + 1) * 128],
                          r=[QT.b], pw=[q_.b], sem="aq%d_%d" % (qn % 2, g))

            def mm1(i):
                qn, qi, g, kp = its[i]
                q_ = aq[qn % 2]
                for j in range(2):
                    kt = kp * 2 + j
                    S.op("pe", lambda: P.matmul(out=PD[i % 3][:, j * 512:(j + 1) * 512], lhsT=KTs[:, kt * 128:(kt + 1) * 128],
                                               rhs=q_[:, g * 4:(g + 1) * 4, :], start=True, stop=True),
                         r=[KTs.b, q_.b], w=[bsb[i % 3][j]])

            deferred = []
            load_q(0)
            if len(qlist) > 1:
                load_q(1)
            mm1(0)
            mm1(1)
            for i, (qn, qi, g, kp) in enumerate(its):
                rows = slice(qi * 128, (qi + 1) * 128)
                qhalf = qi // 32
                on_ = on[qn % 2]
                p_ = pT[i % 3]
                S.op("act", lambda: A.activation(out=p_[:], in_=PD[i % 3][:, :], func=AF.Exp, scale=0.125), r=bsb[i % 3], w=[p_.b])
                if i + 2 < len(its):
                    mm1(i + 2)
                for j in range(2):
                    kt = kp * 2 + j
                    vs = Va
                    S.op("pe", lambda: P.matmul(out=bo[g][0:65, :], lhsT=vs[:, kt, g, 0:65], rhs=p_[:, j * 512:(j + 1) * 512],
                                               start=(kt == 0), stop=(kt == NTILE - 1)),
                         r=[vs.b, p_.b], **({"w": [bo[g].b]} if kt == 0 else {"pw": [bo[g].b]}))
                if kp == NP2 - 1:
                    o_ = oT[g]
                    S.op("dve", lambda: V.tensor_copy(out=o_[:], in_=bo[g][0:65, :]), r=[bo[g].b], w=[o_.b])

                    def epi(g=g, o_=o_, on_=on_, rows=rows, qn=qn):
                        for h in range(4):
                            S.op("pe", lambda: P.transpose(out=bt[:, h * 65:(h + 1) * 65], in_=o_[0:65, h * 128:(h + 1) * 128],
                                                          identity=identf[0:65, 0:65]), r=[o_.b, identf.b],
                                 **({"w": [bt.b]} if h == 0 else {"pw": [bt.b]}))
                        btv = bt[:, 0:260].rearrange("p (h e) -> p h e", e=65)
                        S.op("dve", lambda: V.reciprocal(out=rden[:, g * 4:(g + 1) * 4], in_=btv[:, :, 64]), r=[bt.b], pw=[rden.b])
                        S.op("dve", lambda: V.tensor_tensor(out=on_[:, g * 256:(g + 1) * 256].rearrange("p (h d) -> p h d", d=64),
                                                            in0=btv[:, :, 0:64],
                                                            in1=rden[:, g * 4:(g + 1) * 4].unsqueeze(2).to_broadcast([128, 4, 64]),
                                                            op=ALU.mult), r=[bt.b, rden.b], **({"w": [on_.b]} if g == 0 else {"pw": [on_.b]}))
                        if g == 1:
                            S.dma(MX[rows, 512:1024], on_[:], r=[on_.b], pw=[MX.b], sem="on%d" % (qn % 2))
                            if qn + 2 < len(qlist):
                                load_q(qn + 2)

                    deferred.append((i + 3, epi))
                while deferred and deferred[0][0] <= i and not S.pending:
                    deferred.pop(0)[1]()
                yield
            while deferred:
                if S.pending:
                    yield
                    continue
                deferred.pop(0)[1]()
            yield

    def phaseGLA(l, side=False):
        BW = 512
        NCB = BW // 64
        gbank = (lambda: sideT) if side else bank
        ptrT = ptk if side else ptr
        with ExitStack() as es:
            al = lambda n, s, d: Tl(es.enter_context(nc.sbuf_tensor(n + "_L%d" % l, list(s), d)), n)
            gc = al("gc", [128, GC_N], F32)
            wgf = al("wgf", [16, 256], F32)
            wgb = al("wgb", [16, 256], BF16)
            nbg = al("nbg", [128, 4], F32)
            Sst = al("Sst", [128, 256], F32)
            Sbf = al("Sbf", [128, 256], BF16)
            tmpS = al("tmpS", [128, 256], F32)
            lrs = [al("glr%d" % i, [16, BW], BF16) for i in range(2)]
            qTbs = [al("gqTb%d" % i, [128, BW], BF16) for i in range(2)]
            kTbs = [al("gkTb%d" % i, [128, BW], BF16) for i in range(2)]
            sp = al("gsp", [128, BW], F32)
            cs = al("gcs", [128, BW], F32)
            Eb = al("gE", [128, BW], F32)
            Ei = al("gEi", [128, BW], F32)
            totc = al("gtot", [128, NCB], F32)
            Qbd = al("gQbd", [128, NCB, 4, 64], BF16)
            qtl = al("gqtl", [128, BW], BF16)
            ktl = al("gktl", [128, BW], BF16)
            vchs = [al("gvch%d" % i, [64, NCB, 256], BF16) for i in range(2)]
            ktok = [al("gktok0", [64, 128], BF16), al("gktok1", [64, 128], BF16)]
            sT = [al("gsT0", [64, 256], BF16), al("gsT1", [64, 256], BF16)]
            osb = al("gosb", [64, NCB, 256], F32)
            ofls = [al("gofl%d" % i, [64, NCB, 256], F32) for i in range(2)]
            obf = al("gobf", [64, NCB, 256], BF16)
            gst = al("ggst", [64, 2 * NCB * 4], F32)
            S.dma(gc[:], gconst_in[:, :], r=[gconst_in.b], w=[gc.b], sem="misc")
            S.dma(wgf[:], wg2_in[l], r=[wg2_in.b], w=[wgf.b], sem="misc")
            S.op("dve", lambda: V.tensor_copy(out=wgb[:], in_=wgf[:]), r=[wgf.b], w=[wgb.b])
            S.op("dve", lambda: V.tensor_scalar(out=nbg[:, 0:2], in0=prm[:, P_BGF:P_BGF + 2], scalar1=-1.0, scalar2=None,
                                                op0=ALU.mult), r=[prm.b], w=[nbg.b])
            S.op("dve", lambda: V.memset(nbg[:, 2:3], 1.0), pw=[nbg.b])
            nblk = T // BW
            NBR = TO // BW
            gblocks = [(d_, b_) for d_ in range(2) for b_ in (range(nblk) if d_ == 0 else range(nblk - 1, -1, -1))]
            kblk = [0]

            def gl_load(kb, late=False):
                d_, b_ = gblocks[kb]
                s_ = kb % 2
                rj, lb = b_ // NBR, b_ % NBR
                lsl = slice(lb * BW, (lb + 1) * BW)
                S.dma(lrs[s_][:], RVV[rj]["lrf" if d_ == 0 else "lrb"][:, lsl], r=[RV[4].b], w=[lrs[s_].b], sem="g_lr%d" % s_)
                S.dma(qTbs[s_][:], RVV[rj]["gqt"][:, lsl], r=[RV[3].b], w=[qTbs[s_].b], sem="g_q%d" % s_)
                S.dma(kTbs[s_][:], RVV[rj]["gkt"][:, lsl], r=[RV[3].b], w=[kTbs[s_].b], sem="g_k%d" % s_)
                avT, avv = avu_rows(RV, rj, lb * BW, BW)
                S.dma(vchs[s_][:], avv[:, 0:256].rearrange("(c s) f -> s c f", s=64), r=[avT.b], w=[vchs[s_].b], sem="g_v%d" % s_)
                if d_ == 1 and (kb != nblk or late):
                    S.dma(ofls[s_][:], OF[b_ * BW:(b_ + 1) * BW, :].rearrange("(c s) f -> s c f", s=64), r=[OF.b], w=[ofls[s_].b],
                          sem="g_of%d" % s_)

            gl_load(0)
            for d in range(2):
                S.op("dve", lambda: V.memset(Sst[:], 0.0), w=[Sst.b])
                S.op("dve", lambda: V.memset(Sbf[:], 0.0), w=[Sbf.b])
                lrk = "lrf" if d == 0 else "lrb"
                mcol = GC_MF if d == 0 else GC_MB
                for bi in (range(nblk) if d == 0 else range(nblk - 1, -1, -1)):
                    t0 = bi * BW
                    tsl = slice(t0, t0 + BW)
                    kb = kblk[0]
                    kblk[0] += 1
                    if kb + 1 < len(gblocks):
                        gl_load(kb + 1)
                    if kb == nblk:
                        d_, b_ = gblocks[kb]
                        S.dma(ofls[kb % 2][:], OF[b_ * BW:(b_ + 1) * BW, :].rearrange("(c s) f -> s c f", s=64), r=[OF.b],
                              w=[ofls[kb % 2].b], sem="g_of%d" % (kb % 2))
                    lr, qTb, kTb, vch, ofl = lrs[kb % 2], qTbs[kb % 2], kTbs[kb % 2], vchs[kb % 2], ofls[kb % 2]
                    for j in range(BW // 512):
                        bk = gbank()
                        S.op("pe", lambda: P.matmul(out=bk[:, :], lhsT=wgb[:, d * 128:(d + 1) * 128], rhs=lr[:, j * 512:(j + 1) * 512],
                                                   start=True, stop=True), r=[wgb.b, lr.b], w=[bk.b])
                        S.op("act", lambda: A.activation(out=sp[:, j * 512:(j + 1) * 512], in_=bk[:, :], func=AF.Exp, scale=-1.0,
                                                         bias=nbg[:, d:d + 1]), r=[bk.b, nbg.b], **({"w": [sp.b]} if j == 0 else {"pw": [sp.b]}))
                    yield
                    S.op("act", lambda: A.activation(out=sp[:], in_=sp[:], func=AF.Ln, bias=nbg[:, 2:3]), r=[sp.b, nbg.b], w=[sp.b])
                    yield
                    S.op("dve", lambda: V.tensor_tensor_scan(out=cs[:], data0=gc[:, GC_SM:GC_SM + BW], data1=sp[:], initial=0.0,
                                                             op0=ALU.mult, op1=ALU.add), r=[gc.b, sp.b], w=[cs.b])
                    c3 = lambda t_: t_[:].rearrange("p (c s) -> p c s", s=64)
                    if d == 1:
                        S.op("dve", lambda: V.tensor_copy(out=totc[:], in_=c3(cs)[:, :, 63]), r=[cs.b], w=[totc.b])
                        S.op("dve", lambda: V.tensor_tensor(out=Ei[:], in0=sp[:], in1=cs[:], op=ALU.subtract), r=[sp.b, cs.b], w=[Ei.b])
                        S.op("dve", lambda: V.tensor_tensor(out=c3(cs), in0=c3(Ei), in1=totc[:].unsqueeze(2).to_broadcast([128, NCB, 64]),
                                                            op=ALU.add), r=[Ei.b, totc.b], w=[cs.b])
                    yield
                    S.op("act", lambda: A.activation(out=Eb[:], in_=cs[:], func=AF.Exp, scale=-1.0 / 16), r=[cs.b], w=[Eb.b])
                    S.op("act", lambda: A.activation(out=Ei[:], in_=cs[:], func=AF.Exp, scale=1.0 / 16), r=[cs.b], w=[Ei.b])
                    yield
                    for h in range(4):
                        S.op("dve", lambda: V.scalar_tensor_tensor(out=Qbd[:, :, h, :], in0=c3(qTb), scalar=gc[:, GC_HM + h:GC_HM + h + 1],
                                                                   in1=c3(Eb), op0=ALU.mult, op1=ALU.mult),
                             r=[qTb.b, gc.b, Eb.b], **({"w": [Qbd.b]} if h == 0 else {"pw": [Qbd.b]}))
                    S.op("dve", lambda: V.scalar_tensor_tensor(out=qtl[:], in0=qTb[:], scalar=32.0 ** -0.5, in1=Eb[:],
                                                               op0=ALU.mult, op1=ALU.mult), r=[qTb.b, Eb.b], w=[qtl.b])
                    S.op("pool", lambda: G.tensor_tensor(out=ktl[:], in0=kTb[:], in1=Ei[:], op=ALU.mult), r=[kTb.b, Ei.b], w=[ktl.b])
                    yield
                    for c in (range(NCB) if d == 0 else range(NCB - 1, -1, -1)):
                        gcx = bi * NCB + c
                        if (d == 0 and gcx == 64) or (d == 1 and gcx == 63):
                            S.op("dve", lambda: V.tensor_scalar(out=Sst[:], in0=Sst[:], scalar1=flags[:, 6:7], scalar2=None, op0=ALU.mult),
                                 r=[Sst.b, flags.b], w=[Sst.b])
                            S.op("dve", lambda: V.tensor_copy(out=Sbf[:], in_=Sst[:]), r=[Sst.b], w=[Sbf.b])
                        csl = slice(c * 64, (c + 1) * 64)
                        kt_ = ktok[c % 2]
                        sT_ = sT[c % 2]
                        S.op("pe", lambda: P.transpose(out=ptrT[0:64, 0, :], in_=ktl[:, csl], identity=ident[:]), r=[ktl.b, ident.b], w=[ptrT.b])
                        yield
                        S.op("dve", lambda: V.tensor_copy(out=kt_[:], in_=ptrT[0:64, 0, :]), r=[ptrT.b], w=[kt_.b])
                        bsc = gbank()
                        S.op("pe", lambda: P.matmul(out=bsc[0:64, 0:256], lhsT=ktl[:, csl], rhs=Qbd[:, c, :, :], start=True, stop=True),
                             r=[ktl.b, Qbd.b], w=[bsc.b])
                        yield
                        S.op("dve", lambda: V.tensor_tensor(out=sT_[:].rearrange("p (h t) -> p h t", t=64),
                                                            in0=bsc[0:64, 0:256].rearrange("p (h t) -> p h t", t=64),
                                                            in1=gc[0:64, mcol:mcol + 64].unsqueeze(1).to_broadcast([64, 4, 64]),
                                                            op=ALU.mult), r=[bsc.b, gc.b], w=[sT_.b])
                        yield
                        bo = gbank()
                        S.op("pe", lambda: P.matmul(out=bo[0:64, 0:256], lhsT=qtl[:, csl], rhs=Sbf[:, :], start=True, stop=False),
                             r=[qtl.b, Sbf.b], w=[bo.b])
                        for h in range(4):
                            S.op("pe", lambda: P.matmul(out=bo[0:64, h * 64:(h + 1) * 64], lhsT=sT_[:, h * 64:(h + 1) * 64],
                                                       rhs=vch[:, c, h * 64:(h + 1) * 64], start=False, stop=(h == 3)),
                                 r=[sT_.b, vch.b], pw=[bo.b])
                        yield
                        if d == 0:
                            S.op("dve", lambda: V.tensor_copy(out=osb[:, c, :], in_=bo[0:64, 0:256]), r=[bo.b], pw=[osb.b])
                        else:
                            S.op("dve", lambda: V.tensor_tensor(out=osb[:, c, :], in0=bo[0:64, 0:256], in1=ofl[:, c, :], op=ALU.add),
                                 r=[bo.b, ofl.b], pw=[osb.b])
                        bst = gbank()
                        S.op("pe", lambda: P.matmul(out=bst[:, 0:256], lhsT=kt_[:], rhs=vch[:, c, :], start=True, stop=True),
                             r=[kt_.b, vch.b], w=[bst.b])
                        yield
                        didx = c * 64 + (63 if d == 0 else 0)
                        S.op("dve", lambda: V.scalar_tensor_tensor(out=tmpS[:], in0=bst[:, 0:256], scalar=Eb[:, didx:didx + 1],
                                                                   in1=gc[:, GC_BD:GC_BD + 256], op0=ALU.mult, op1=ALU.mult),
                             r=[bst.b, Eb.b, gc.b], w=[tmpS.b])
                        yield
                        S.op("dve", lambda: V.scalar_tensor_tensor(out=Sst[:], in0=Sst[:], scalar=Eb[:, didx:didx + 1], in1=tmpS[:],
                                                                   op0=ALU.mult, op1=ALU.add), r=[Sst.b, Eb.b, tmpS.b], w=[Sst.b])
                        yield
                        S.op("dve", lambda: V.tensor_copy(out=Sbf[:], in_=Sst[:]), r=[Sst.b], w=[Sbf.b])
                        yield
                    if "gla_no_end" in dbg or ("gla_no_end1" in dbg and d == 1):
                        continue
                    if d == 0:
                        S.dma(OF[tsl, :].rearrange("(c s) f -> s c f", s=64), osb[:], r=[osb.b], pw=[OF.b], sem="g_os")
                    else:
                        o4 = lambda t_: t_[:].rearrange("p c (h d) -> p (c h) d", d=64)
                        S.op("dve", lambda: V.tensor_tensor(out=ofl[:], in0=osb[:], in1=osb[:], op=ALU.mult), r=[osb.b], w=[ofl.b])
                        S.op("dve", lambda: V.tensor_reduce(out=gst[:, 0:NCB * 4], in_=o4(ofl), op=ALU.add, axis=AX.X), r=[ofl.b], w=[gst.b])
                        S.op("dve", lambda: V.tensor_scalar(out=gst[:, 0:NCB * 4], in0=gst[:, 0:NCB * 4], scalar1=1.0 / 64, scalar2=EPS,
                                                            op0=ALU.mult, op1=ALU.add), r=[gst.b], w=[gst.b])
                        S.op("act", lambda: A.activation(out=gst[:, 0:NCB * 4], in_=gst[:, 0:NCB * 4], func=AF.Ln), r=[gst.b], w=[gst.b])
                        S.op("act", lambda: A.activation(out=gst[:, 0:NCB * 4], in_=gst[:, 0:NCB * 4], func=AF.Exp, scale=-0.5),
                             r=[gst.b], w=[gst.b])
                        S.op("dve", lambda: V.tensor_tensor(out=o4(osb), in0=o4(osb),
                                                            in1=gst[:, 0:NCB * 4].unsqueeze(2).to_broadcast([64, NCB * 4, 64]), op=ALU.mult),
                             r=[osb.b, gst.b], w=[osb.b])
                        S.op("dve", lambda: V.tensor_tensor(out=obf[:], in0=osb[:],
                                                            in1=bc[0:64, BC_ONG:BC_ONG + 256].unsqueeze(1).to_broadcast([64, NCB, 256]),
                                                            op=ALU.mult), r=[osb.b, bc.b], w=[obf.b])
                        S.dma(GO[tsl, :].rearrange("(c s) f -> s c f", s=64), obf[:], r=[obf.b], pw=[GO.b], sem="g_ob")

    def phaseFFT(l, side=False):
        gbank = (lambda: sideT) if side else bank
        with ExitStack() as es:
            al = lambda n, s, d: Tl(es.enter_context(nc.sbuf_tensor(n + "_L%d" % l, list(s), d)), n)
            t1 = al("ft1", [128, 64, 256], BF16)
            t2 = al("ft2", [128, 128], BF16)
            up = al("fup", [128, 64, 256], BF16)
            csf = al("fcsf", [128, 2, 512], F32)
            csb = al("fcsb", [128, 2, 512], BF16)
            fwf = al("ffwf", [128, 2, 256], F32)
            fwb = al("ffwb", [128, 2, 256], BF16)
            w12 = al("fw12", [128, 2, 2, 256], BF16)
            g1s = [al("fg1s0", [128, 512], BF16), al("fg1s1", [128, 512], BF16)]
            ab = [al("fab0", [128, 8, 256], BF16), al("fab1", [128, 8, 256], BF16)]
            yT = [al("fyT0", [128, 2, 128], BF16), al("fyT1", [128, 2, 128], BF16)]
            ob = [al("fob0", [64, 8, 256], BF16), al("fob1", [64, 8, 256], BF16)]
            for q4 in range(4):
                S.dma(t1[:, q4 * 16:(q4 + 1) * 16, :], tab1_in[q4 * 16:(q4 + 1) * 16, :, :].rearrange("n p c -> p n c"),
                      r=[tab1_in.b], pw=[t1.b], sem="f_t1")
                avT, avv = avu_rows(RV, q4 // 2, (q4 % 2) * 2048, 2048)
                S.dma(up[q4 * 32:(q4 + 1) * 32, :, :], avv[:, 256:512].rearrange("(p n) c -> p n c", n=64),
                      r=[avT.b], pw=[up.b], sem="f_up")
            S.dma(t2[:], tab2_in[:, :], r=[tab2_in.b], w=[t2.b], sem="misc")
            S.dma(csf[:], cs64_in[:, :, :], r=[cs64_in.b], w=[csf.b], sem="misc")
            S.dma(fwf[:], fnet_in[l].rearrange("(k p) c -> p k c", p=128), r=[fnet_in.b], w=[fwf.b], sem="misc")
            if side:
                for _ in range(int(90 * SIDE_RATIO)):
                    yield
            S.op("dve", lambda: V.tensor_copy(out=csb[:], in_=csf[:]), r=[csf.b], w=[csb.b])
            S.op("dve", lambda: V.tensor_copy(out=fwb[:], in_=fwf[:]), r=[fwf.b], w=[fwb.b])
            for m in range(2):
                for wi in range(2):
                    bk = gbank()
                    for kc in range(2):
                        S.op("pe", lambda: P.matmul(out=bk[:, 0:256], lhsT=csb[:, kc, wi * 256 + m * 128:wi * 256 + (m + 1) * 128],
                                                   rhs=fwb[:, kc, :], start=(kc == 0), stop=(kc == 1)), r=[csb.b, fwb.b],
                             **({"w": [bk.b]} if kc == 0 else {"pw": [bk.b]}))
                    S.op("act", lambda: A.copy(out=w12[:, m, wi, :], in_=bk[:, 0:256]), r=[bk.b], pw=[w12.b])
            for n2 in range(64):
                bk = gbank()
                S.op("pe", lambda: P.matmul(out=bk[:, 0:256], lhsT=t1[:, n2, 0:128], rhs=up[:, n2, :], start=True, stop=True),
                     r=[t1.b, up.b], w=[bk.b])
                S.op("pe", lambda: P.matmul(out=bk[:, 256:512], lhsT=t1[:, n2, 128:256], rhs=up[:, n2, :], start=True, stop=True),
                     r=[t1.b, up.b], pw=[bk.b])
                yield
                g_ = g1s[n2 % 2]
                S.op("dve", lambda: V.tensor_copy(out=g_[:], in_=bk[:, :]), r=[bk.b], w=[g_.b])
                S.dma(G1[:, :, n2, :], g_[:].rearrange("p (r c) -> p r c", r=2), r=[g_.b], pw=[G1.b], sem="f_g%d" % (n2 % 2))
                yield
            mxv = MXF[:, :].rearrange("(k p) c -> k p c", p=128)
            for pgp in range(16):
                p0 = pgp * 8
                ab_ = ab[pgp % 2]
                ob_ = ob[pgp % 2]
                if pgp == 0:
                    S.dma(ab_[:], G1[p0:p0 + 8, :, :, :].rearrange("p r n c -> (r n) p c"), r=[G1.b], w=[ab_.b], sem="f_ab%d" % (pgp % 2))
                if pgp + 1 < 16:
                    S.dma(ab[(pgp + 1) % 2][:], G1[p0 + 8:p0 + 16, :, :, :].rearrange("p r n c -> (r n) p c"), r=[G1.b],
                          w=[ab[(pgp + 1) % 2].b], sem="f_ab%d" % ((pgp + 1) % 2))
                for j in range(8):
                    y_ = yT[j % 2]
                    bk = gbank()
                    for m in range(2):
                        S.op("pe", lambda: P.matmul(out=bk[:, m * 128:(m + 1) * 128], lhsT=ab_[:, j, m * 128:(m + 1) * 128], rhs=t2[:, :],
                                                   start=True, stop=True), r=[ab_.b, t2.b], **({"w": [bk.b]} if m == 0 else {"pw": [bk.b]}))
                    yield
                    S.op("dve", lambda: V.tensor_copy(out=y_[:].rearrange("p m k -> p (m k)"), in_=bk[:, 0:256]), r=[bk.b], w=[y_.b])
                    yield
                    b2 = gbank()
                    i4 = 0
                    for m in range(2):
                        for part in range(2):
                            S.op("pe", lambda: P.matmul(out=b2[0:64, 0:256], lhsT=y_[:, m, part * 64:(part + 1) * 64], rhs=w12[:, m, part, :],
                                                       start=(i4 == 0), stop=(i4 == 3)), r=[y_.b, w12.b],
                                 **({"w": [b2.b]} if i4 == 0 else {"pw": [b2.b]}))
                            i4 += 1
                    yield
                    S.op("dve", lambda: V.tensor_copy(out=ob_[:, j, :], in_=b2[0:64, 0:256]), r=[b2.b], pw=[ob_.b])
                    yield
                S.dma(mxv[:, p0:p0 + 8, :], ob_[:], r=[ob_.b], pw=[MXF.b], sem="f_o%d" % (pgp % 2))
                seq = p0 // 64
                k10 = p0 % 64
                fpv = FP[seq * 4096:(seq + 1) * 4096, :].rearrange("(k q) c -> k q c", q=64)
                S.dma(fpv[:, k10:k10 + 8, :], ob_[:], r=[ob_.b], pw=[FP.b], sem="f_p%d" % (pgp % 2))

    def side_chain(l):
        yield from phaseFFT(l, True)
        S.barrier()
        yield from phaseGLA(l, True)

    for l in range(2):
        xsrc = x_in if l == 0 else Y1
        ydst = Y1 if l == 0 else y_out
        S.barrier()
        with ExitStack() as esW:
            win_t = Tl(esW.enter_context(nc.sbuf_tensor("win_L%d" % l, [128, 8, DIN], BF16)), "win")
            with ExitStack() as esS:
                ws = [Tl(esS.enter_context(nc.sbuf_tensor("wst%d_L%d" % (i, l), [128, DIN], F32)), "wst%d" % i) for i in range(2)]

                def w_chunk(kc):
                    w_ = ws[kc % 2]
                    S.dma(w_[:, :], win_in[l, kc * 128:(kc + 1) * 128, :], r=[win_in.b], w=[w_.b], sem="ws%d" % (kc % 2), q="act")
                    S.op("dve", lambda: V.tensor_copy(out=win_t[:, kc, :], in_=w_[:, :]), r=[w_.b], pw=[win_t.b])

                phase0(l, w_chunk)
                S.barrier()
            phaseA(l, xsrc, win_t)
        for i in (2, 1, 3, 4):
            S.allgather(SD[i], RV[i], RG, "ag%d" % i)
        S.barrier()
        if "only_A" in dbg:
            break
        with ExitStack() as esA:
            main = phaseAttn(l, esA)
            side = side_chain(l)
            acc, side_done = 0.0, False
            n_main = n_side = n_tail = 0
            for _ in main:
                n_main += 1
                acc += SIDE_RATIO
                while acc >= 1.0 and not side_done:
                    acc -= 1.0
                    try:
                        next(side)
                        n_side += 1
                    except StopIteration:
                        side_done = True
                        S.side_done_at = n_main
            if not side_done:
                for _ in side:
                    n_tail += 1
            S.counts = (n_main, n_side, n_tail)
            S.barrier()
        phaseC(l, xsrc, ydst)
    outs = [y_out, Y1, MX, ZS, QT, FP, OF, GO, MXF] + RV
    S.finish(outs)
    return nc, S


def _host_layout(inputs):
    f = np.float32
    xp = np.asarray(inputs["x_prompt"], f)
    xs = np.asarray(inputs["x_sample"], f)
    cp = np.asarray(inputs["c_prompt"], f)
    csm = np.asarray(inputs["c_sample"], f)
    blocks_x = [xp[0:2].reshape(T, D), xp[2:4].reshape(T, D), xs[0], xs[1]]
    blocks_c = [cp[0:2], cp[2:4], np.stack([csm[0], csm[0]]), np.stack([csm[1], csm[1]])]
    is_sample = [False, False, True, True]

    def fm(v, n):
        return np.ascontiguousarray(np.asarray(v, f).reshape(n, 128).T)

    def rep(v, p=128):
        return np.ascontiguousarray(np.broadcast_to(np.asarray(v, f)[None, :], (p, v.shape[0])))

    prm = np.zeros((2, 128, NPRM), f)
    bcm = np.zeros((2, 128, NBC), f)
    wg2 = np.zeros((2, 16, 256), f)
    sgu_wT = np.zeros((2, 128, 512), f)
    for l in range(2):
        prm[l, :, P_ADAB:P_ADAB + 24] = fm(inputs["ada_b"][l], 24)
        prm[l, :, P_PREG:P_PREG + 8] = fm(inputs["norm_pre_g"][l], 8)
        prm[l, :, P_BGF] = np.asarray(inputs["gla_bg_f"][l], f)
        prm[l, :, P_BGB] = np.asarray(inputs["gla_bg_b"][l], f)
        prm[l, :, P_SGUB:P_SGUB + 4] = np.asarray(inputs["sgu_b"][l], f).T
        bcm[l, :, BC_ADABG:BC_ADABG + 1024] = rep(np.asarray(inputs["ada_b"][l], f)[2048:3072])
        bcm[l, :, BC_POSTG:BC_POSTG + 1024] = rep(np.asarray(inputs["norm_post_g"][l], f))
        qkg = np.concatenate([np.tile(np.asarray(inputs["q_norm_g"][l], f), 8), np.tile(np.asarray(inputs["k_norm_g"][l], f), 2)])
        bcm[l, :, BC_QKG:BC_QKG + 640] = rep(qkg)
        bcm[l, :, BC_SGUG:BC_SGUG + 256] = rep(np.asarray(inputs["sgu_norm_g"][l], f))
        bcm[l, :, BC_ONG:BC_ONG + 256] = rep(np.tile(np.asarray(inputs["gla_onorm_g"][l], f), 4))
        wg2[l, :, 0:128] = np.asarray(inputs["gla_wg2_f"][l], f)
        wg2[l, :, 128:256] = np.asarray(inputs["gla_wg2_b"][l], f)
        sgu_wT[l] = np.asarray(inputs["sgu_w"][l], f).transpose(2, 0, 1).reshape(128, 512)

    def rope_tab(n):
        t = np.arange(n)
        row = (t // 64).astype(np.float64)
        col = (t % 64).astype(np.float64)
        freqs = 10000.0 ** (-np.arange(0, 32, 2, dtype=np.float64) / 32)
        ang = np.concatenate([row[:, None] * freqs, col[:, None] * freqs], axis=-1)
        return np.concatenate([np.cos(ang), np.sin(ang)], axis=-1).astype(f)

    rope_s = rope_tab(8192)
    rope_p = np.concatenate([rope_tab(4096), rope_tab(4096)], axis=0)
    gconst = np.zeros((128, GC_N), f)
    s_i = np.arange(64)[:, None]
    t_i = np.arange(64)[None, :]
    gconst[0:64, GC_MF:GC_MF + 64] = (s_i <= t_i)
    gconst[0:64, GC_MB:GC_MB + 64] = (s_i > t_i)
    for h in range(4):
        gconst[h * 32:(h + 1) * 32, GC_HM + h] = 32.0 ** -0.5
        gconst[h * 32:(h + 1) * 32, GC_BD + h * 64:GC_BD + (h + 1) * 64] = 1.0
    sm = np.ones(2048, f)
    sm[0::64] = 0.0
    gconst[:, GC_SM:GC_SM + 2048] = sm[None, :]

    a64 = np.arange(64, dtype=np.float64)
    th64 = 2 * np.pi * a64[:, None] * a64[None, :] / 64.0
    Cbd = np.kron(np.eye(4), np.cos(th64))
    Sbd = np.kron(np.eye(4), np.sin(th64))
    cs64 = np.concatenate([Cbd, Sbd], axis=1).reshape(2, 128, 512).transpose(1, 0, 2).astype(f)
    cs64 = np.ascontiguousarray(cs64)
    shared = {
        "ada_w": np.ascontiguousarray(np.asarray(inputs["ada_w"], f)),
        "w_in": np.ascontiguousarray(np.asarray(inputs["w_in"], f)),
        "w_out": np.ascontiguousarray(np.asarray(inputs["w_out"], f)),
        "fnet_w": np.ascontiguousarray(np.asarray(inputs["fnet_w"], f)),
        "sgu_wT": sgu_wT, "wg2": wg2, "prm": prm, "bc": bcm, "gconst": gconst,
        "cs64": cs64,
    }
    rope_p1 = rope_tab(4096)
    tabs = {}
    for smp in (False, True):
        n2 = np.arange(64, dtype=np.float64)[:, None, None]
        n1 = np.arange(128, dtype=np.float64)[None, :, None]
        pp = np.arange(128, dtype=np.float64)[None, None, :]
        if smp:
            th = 2 * np.pi * (n2 * pp / 8192.0 + n1 * pp / 128.0)
            mre, mim = np.cos(th), -np.sin(th)
            nseq = 8192.0
        else:
            th = 2 * np.pi * (n2 * (pp % 64) / 4096.0 + (n1 % 64) * (pp % 64) / 64.0)
            same = ((n1 // 64) == (pp // 64)).astype(np.float64)
            mre, mim = np.cos(th) * same, -np.sin(th) * same
            nseq = 4096.0
        t1 = np.concatenate([mre, mim], axis=2).astype(ml_dtypes.bfloat16)
        a_ = np.arange(64, dtype=np.float64)
        th2 = 2 * np.pi * a_[:, None] * a_[None, :] / 64.0
        sc = 1.0 / np.sqrt(64.0 * nseq)
        C2, S2 = np.cos(th2) * sc, np.sin(th2) * sc
        t2 = np.block([[C2, -S2], [S2, C2]]).astype(ml_dtypes.bfloat16)
        tabs[smp] = (t1, t2)
    maps = []
    for r in range(8):
        b, k = r % 4, r // 4
        smp = is_sample[b]
        m = dict(shared)
        m["x"] = np.ascontiguousarray(blocks_x[b][k * TO:(k + 1) * TO])
        c1 = blocks_c[b][k]
        c2 = np.stack([c1, c1])
        m["cT"] = np.ascontiguousarray(c2.reshape(2, 8, 128).transpose(2, 1, 0).reshape(128, 16))
        m["rope"] = np.ascontiguousarray(rope_s[k * TO:(k + 1) * TO]) if smp else rope_p1
        fl = np.zeros((128, 16), f)
        own = [1.0 if k == 0 else 0.0, 1.0 if k == 1 else 0.0]
        fS, fP = (1.0, 0.0) if smp else (0.0, 1.0)
        fl[:, 0], fl[:, 1], fl[:, 2], fl[:, 3] = fS * own[0], fS * own[1], fP * own[0], fP * own[1]
        fl[:, 4], fl[:, 5] = own[0], own[1]
        fl[:, 6] = 1.0 if smp else 0.0
        fl[:, 7] = 1.0 if (smp or k == 0) else 0.0
        fl[:, 8] = 1.0 if (smp or k == 1) else 0.0
        m["flags"] = fl
        m["tab1"], m["tab2"] = tabs[smp]
        maps.append(m)
    return maps


_CACHE = {}


def kernel(**inputs):
    maps = _host_layout(inputs)
    if "nc" not in _CACHE:
        _CACHE["nc"] = build()[0]
    nc = _CACHE["nc"]
    res = run_bass_kernel_spmd(nc, maps, core_ids=list(range(8)))
    ys = [np.asarray(res.results[i]["y"], np.float32) for i in range(8)]
    y_prompt = np.stack([ys[0], ys[4], ys[1], ys[5]], axis=0)
    y_sample = np.stack([np.concatenate([ys[2], ys[6]], 0), np.concatenate([ys[3], ys[7]], 0)], axis=0)
    return (y_prompt, y_sample)
```
